# Optimizing a Trainium2 kernel written in Bass

```python
import jax
import jax.numpy as jnp
from jax import lax
import numpy as np

D_MODEL = 2048
BATCH = 4
SEQ = 2048
DEPTH = 4
DEC_BATCH = 128
DEC_SEQ = 1
PAST_LEN = 16384
PAGE_SIZE = 128

N_EVEN = (DEPTH + 1) // 2
N_ODD = DEPTH // 2
D_FF = 5632
N_MOD = 9
EPS = 1e-6
CHUNK = 64
NEG_BIG = -1e30

W_A = D_MODEL // 2
NB_A = 8
BS_A = W_A // NB_A
CONV_W = 4
C_RGLRU = 8.0
H_B = 8
DK_B = 128
DV_B = (D_MODEL // 2) // H_B
H_C = 4
DK_C = (D_MODEL // 4) // H_C
DV_C = (D_MODEL // 2) // H_C
N_D = 64
H_D = (D_MODEL // 2) // N_D
W_D = H_D * N_D
R_W = 64
R_A = 64
R_G = 128
GN_EPS_D = 64e-5

SIZES_EVEN = (W_A, W_A, H_B * DK_B, H_B * DK_B, H_B * DV_B, H_B * DV_B)
P_EVEN = sum(SIZES_EVEN)
SIZES_D = (W_D, W_D, W_D, R_W, R_A, R_G)
P_D = sum(SIZES_D)
SIZES_ODD = (H_C * DK_C, H_C * DK_C, H_C * DV_C, H_C * DV_C, H_C, H_C, P_D)
P_ODD = sum(SIZES_ODD)
MIX_EVEN = W_A + H_B * DV_B
MIX_ODD = H_C * DV_C + W_D

kernel_name = 'hybrid_rglru_hgrn2_mlstm_rwkv7_decode_step'


def _split(z, sizes):
    idx = np.cumsum(sizes)[:-1].tolist()
    return jnp.split(z, idx, axis=-1)


def _rms(x):
    xf = x.astype(jnp.float32)
    return xf * lax.rsqrt(jnp.mean(xf * xf, axis=-1, keepdims=True) + EPS)


def _adaln(t, shift, scale):
    return (_rms(t) * (1.0 + scale) + shift).astype(t.dtype)


def _swiglu(h, wg, wu, wd):
    return (jax.nn.silu(h @ wg) * (h @ wu)) @ wd


def _chunk_len(L):
    return CHUNK if L % CHUNK == 0 else L


def _to_chunks(x, cl):
    B, H, L = x.shape[:3]
    return jnp.moveaxis(x.reshape((B, H, L // cl, cl) + x.shape[3:]), 2, 0)


def _from_chunks(x):
    NC, B, H, cl = x.shape[:4]
    return jnp.moveaxis(x, 0, 2).reshape((B, H, NC * cl) + x.shape[4:])


def _causal_conv(u, buf, w, b):
    L = u.shape[1]
    full = jnp.concatenate([buf, u], axis=1)
    out = b + sum(full[:, k:k + L] * w[k] for k in range(CONV_W))
    return out, full[:, L:]


def _rglru(u, h0, wr, br, wi, bi, lam):
    B, L, _ = u.shape
    ub = u.reshape(B, L, NB_A, BS_A)
    r = jax.nn.sigmoid(jnp.einsum('blnc,ncd->blnd', ub, wr).reshape(B, L, W_A) + br)
    i = jax.nn.sigmoid(jnp.einsum('blnc,ncd->blnd', ub, wi).reshape(B, L, W_A) + bi)
    log_a = -C_RGLRU * r * jax.nn.softplus(-lam)
    a = jnp.exp(log_a)
    b = jnp.sqrt(-jnp.expm1(2.0 * log_a)) * (i * u)
    b = b.at[:, 0].add(a[:, 0] * h0)

    def combine(left, right):
        a1, b1 = left
        a2, b2 = right
        return a1 * a2, a2 * b1 + b2

    _, h = lax.associative_scan(combine, (a, b), axis=1)
    return h, h[:, -1]


def _gla_chunked(q, k, v, log_f, s0):
    cl = _chunk_len(q.shape[2])
    tril = jnp.tril(jnp.ones((cl, cl), dtype=bool))
    tril3 = tril[:, :, None]

    def body(S, inp):
        qc, kc, vc, fc = inp
        b = jnp.cumsum(fc, axis=2)
        g = b[:, :, -1]
        o = jnp.einsum('bhtk,bhkv->bhtv', qc * jnp.exp(b), S)
        diff = b[:, :, :, None, :] - b[:, :, None, :, :]
        dec = jnp.where(tril3, jnp.exp(jnp.where(tril3, diff, 0.0)), 0.0)
        att = jnp.einsum('bhtk,bhsk,bhtsk->bhts', qc, kc, dec)
        o = o + jnp.einsum('bhts,bhsv->bhtv', att, vc)
        S = jnp.exp(g)[..., None] * S + jnp.einsum('bhsk,bhsv->bhkv', kc * jnp.exp(g[:, :, None] - b), vc)
        return S, o

    S, o = lax.scan(body, s0, tuple(_to_chunks(t, cl) for t in (q, k, v, log_f)))
    return _from_chunks(o), S


def _hgrn2(q, fpre, iv, g, s0, lb, norm_w):
    B, L, _ = q.shape
    k = (1.0 - lb) * jax.nn.sigmoid(-fpre)
    log_f = jnp.log1p(-k)

    def heads(t, d):
        return t.reshape(B, L, H_B, d).transpose(0, 2, 1, 3)

    o, s1 = _gla_chunked(heads(q, DK_B), heads(k, DK_B), heads(iv, DV_B), heads(log_f, DK_B), s0)
    o = (_rms(o) * norm_w).transpose(0, 2, 1, 3).reshape(B, L, H_B * DV_B)
    return o * jax.nn.silu(g), s1


def _mlstm(q, k, v, o_pre, i_pre, f_pre, c0, n0, m0, norm_w):
    B, L, _ = q.shape

    def heads(t, d):
        return t.reshape(B, L, H_C, d).transpose(0, 2, 1, 3)

    qh = heads(q, DK_C)
    kh = heads(k, DK_C) * (DK_C ** -0.5)
    vh = heads(v, DV_C)
    ig = i_pre.transpose(0, 2, 1)
    lf = jax.nn.log_sigmoid(f_pre).transpose(0, 2, 1)
    cl = _chunk_len(L)
    tril = jnp.tril(jnp.ones((cl, cl), dtype=bool))

    def body(carry, inp):
        C, n, m = carry
        qc, kc, vc, ic, fc = inp
        b = jnp.cumsum(fc, axis=-1)
        dm = b[..., :, None] - b[..., None, :] + ic[..., None, :]
        dm = jnp.where(tril, dm, NEG_BIG)
        inter = b + m[..., None]
        m_t = jnp.maximum(inter, jnp.max(dm, axis=-1))
        w_inter = jnp.exp(inter - m_t)
        s = jnp.einsum('bhtk,bhsk->bhts', qc, kc) * jnp.exp(dm - m_t[..., None])
        num = w_inter[..., None] * jnp.einsum('bhtk,bhkv->bhtv', qc, C) + jnp.einsum('bhts,bhsv->bhtv', s, vc)
        den = w_inter * jnp.einsum('bhtk,bhk->bht', qc, n) + jnp.sum(s, axis=-1)
        h = num / jnp.maximum(jnp.abs(den), jnp.exp(-m_t))[..., None]
        g = b[..., -1]
        m_new = m_t[..., -1]
        wk = jnp.exp(g[..., None] - b + ic - m_new[..., None])
        f_state = jnp.exp(g + m - m_new)
        C = f_state[..., None, None] * C + jnp.einsum('bhs,bhsk,bhsv->bhkv', wk, kc, vc)
        n = f_state[..., None] * n + jnp.einsum('bhs,bhsk->bhk', wk, kc)
        return (C, n, m_new), h

    (c1, n1, m1), h = lax.scan(body, (c0, n0, m0), tuple(_to_chunks(t, cl) for t in (qh, kh, vh, ig, lf)))
    h = (_rms(_from_chunks(h)) * norm_w).transpose(0, 2, 1, 3).reshape(B, L, H_C * DV_C)
    return h * jax.nn.sigmoid(o_pre), c1, n1, m1


def _rwkv7(zd, prev, s0, mu, w0, w2, a0, a2, g2, k_k, k_a, r_k, ln_w, ln_b):
    B, L, _ = zd.shape
    shifted = jnp.concatenate([prev[:, None], zd[:, :-1]], axis=1)
    zs = zd + (shifted - zd) * mu
    r, k, v, wl, al, gl = _split(zs, SIZES_D)
    w = -jax.nn.softplus(-(w0 + jnp.tanh(wl) @ w2)) - 0.5
    decay = jnp.exp(-jnp.exp(w))
    a = jax.nn.sigmoid(a0 + al @ a2)
    gate = jax.nn.sigmoid(gl) @ g2

    def hd(t):
        return t.reshape(B, L, H_D, N_D)

    kk = hd(k * k_k)
    kk = kk / jnp.maximum(jnp.sqrt(jnp.sum(kk * kk, axis=-1, keepdims=True)), 1e-12)
    k = hd(k * (1.0 + (a - 1.0) * k_a))
    r, v, a, decay = hd(r), hd(v), hd(a), hd(decay)

    def step(S, inp):
        r_t, d_t, k_t, v_t, kk_t, a_t = inp
        sa = jnp.einsum('bhvk,bhk->bhv', S, -kk_t)
        S = S * d_t[:, :, None, :] + sa[..., None] * (kk_t * a_t)[:, :, None, :] + v_t[..., None] * k_t[:, :, None, :]
        return S, jnp.einsum('bhvk,bhk->bhv', S, r_t)

    s1, y = lax.scan(step, s0, tuple(jnp.moveaxis(t, 1, 0) for t in (r, decay, k, v, kk, a)))
    y = jnp.moveaxis(y, 0, 1)
    mean = jnp.mean(y, axis=-1, keepdims=True)
    var = jnp.mean(jnp.square(y - mean), axis=-1, keepdims=True)
    y = ((y - mean) * lax.rsqrt(var + GN_EPS_D)).reshape(B, L, W_D) * ln_w + ln_b
    bonus = jnp.sum(r * k * r_k, axis=-1, keepdims=True) * v
    y = (y + bonus.reshape(B, L, W_D)) * gate
    return y, zd[:, -1], s1


def _even_mixer(h, conv0, h0, s0, lb, p, j):
    z = (h @ p['w_in_even'][j]).astype(jnp.float32)
    ax, ag, bq, bf, bi, bg = _split(z, SIZES_EVEN)
    u, conv1 = _causal_conv(ax, conv0, p['a_conv_w'][j], p['a_conv_b'][j])
    ha, ha_last = _rglru(u, h0, p['a_gate_r_w'][j], p['a_gate_r_b'][j], p['a_gate_i_w'][j], p['a_gate_i_b'][j], p['a_lambda'][j])
    ya = ha * jax.nn.gelu(ag)
    yb, s1 = _hgrn2(bq, bf, bi, bg, s0, lb, p['b_norm_w'][j])
    y = jnp.concatenate([ya, yb], axis=-1).astype(h.dtype) @ p['w_out_even'][j]
    return y, conv1, ha_last, s1


def _odd_mixer(h, c0, n0, m0, prev0, sd0, p, j):
    z = (h @ p['w_in_odd'][j]).astype(jnp.float32)
    cq, ck, cv, co, ci, cf, zd = _split(z, SIZES_ODD)
    yc, c1, n1, m1 = _mlstm(cq, ck, cv, co, ci + p['c_igate_b'][j], cf + p['c_fgate_b'][j], c0, n0, m0, p['c_norm_w'][j])
    yd, prev1, sd1 = _rwkv7(zd, prev0, sd0, p['d_mu'][j], p['d_w0'][j], p['d_w2'][j], p['d_a0'][j], p['d_a2'][j], p['d_g2'][j], p['d_k_k'][j], p['d_k_a'][j], p['d_r_k'][j], p['d_ln_w'][j], p['d_ln_b'][j])
    y = jnp.concatenate([yc, yd], axis=-1).astype(h.dtype) @ p['w_out_odd'][j]
    return y, c1, n1, m1, prev1, sd1


def _trunk(x, c, a_conv, a_h, b_s, c_c, c_n, c_m, d_shift, d_s, p):
    f32 = jnp.float32
    sm = jax.nn.softmax(p['b_lb_gamma'].astype(f32), axis=0)
    lower_bounds = jnp.cumsum(sm, axis=0) - sm[0]
    cs = jax.nn.silu(c)
    n_a_conv, n_a_h, n_b_s = [], [], []
    n_c_c, n_c_n, n_c_m, n_d_shift, n_d_s = [], [], [], [], []
    for l in range(DEPTH):
        j = l // 2
        mod = (cs @ p['w_mod'][l] + p['b_mod'][l]).reshape(c.shape[0], N_MOD, 1, D_MODEL)
        hf = _adaln(x, mod[:, 0], mod[:, 1])
        x = x + 0.5 * (1.0 + mod[:, 2]) * _swiglu(hf, p['ffn_w_gate'][l, 0], p['ffn_w_up'][l, 0], p['ffn_w_down'][l, 0])
        h = _adaln(x, mod[:, 3], mod[:, 4])
        if l % 2 == 0:
            y, cv1, ha1, bs1 = _even_mixer(h, a_conv[j].astype(f32), a_h[j].astype(f32), b_s[j].astype(f32), lower_bounds[j], p, j)
            n_a_conv.append(cv1)
            n_a_h.append(ha1)
            n_b_s.append(bs1)
        else:
            y, cc1, cn1, cm1, ds1, dss1 = _odd_mixer(h, c_c[j].astype(f32), c_n[j].astype(f32), c_m[j].astype(f32), d_shift[j].astype(f32), d_s[j].astype(f32), p, j)
            n_c_c.append(cc1)
            n_c_n.append(cn1)
            n_c_m.append(cm1)
            n_d_shift.append(ds1)
            n_d_s.append(dss1)
        x = x + (1.0 + mod[:, 5]) * y
        hf = _adaln(x, mod[:, 6], mod[:, 7])
        x = x + 0.5 * (1.0 + mod[:, 8]) * _swiglu(hf, p['ffn_w_gate'][l, 1], p['ffn_w_up'][l, 1], p['ffn_w_down'][l, 1])
    y = (_rms(x) * p['final_norm_w']).astype(x.dtype)
    return (y, jnp.stack(n_a_conv), jnp.stack(n_a_h), jnp.stack(n_b_s), jnp.stack(n_c_c), jnp.stack(n_c_n), jnp.stack(n_c_m), jnp.stack(n_d_shift), jnp.stack(n_d_s))


def setup_inputs(seed: int = 0) -> dict:
    key = jax.random.key(seed)
    ks = iter(jax.random.split(key, 64))
    f32 = jnp.float32

    def nrm(shape, scale):
        return jax.random.normal(next(ks), shape, f32) * scale

    def unif(shape, lo, hi):
        return jax.random.uniform(next(ks), shape, f32, lo, hi)

    def gain(shape):
        return 1.0 + nrm(shape, 0.02)

    a_target = unif((N_EVEN, W_A), 0.9, 0.999) ** (1.0 / C_RGLRU)
    return {
        'x_prompt': nrm((BATCH, SEQ, D_MODEL), 1.0),
        'x_sample': nrm((DEC_BATCH, DEC_SEQ, D_MODEL), 1.0),
        'c_prompt': nrm((BATCH, D_MODEL), 1.0),
        'c_sample': nrm((DEC_BATCH, D_MODEL), 1.0),
        'state_a_conv': nrm((N_EVEN, DEC_BATCH, CONV_W - 1, W_A), 1.0),
        'state_a_h': nrm((N_EVEN, DEC_BATCH, W_A), 0.5),
        'state_b_s': nrm((N_EVEN, DEC_BATCH, H_B, DK_B, DV_B), 0.5),
        'state_c_c': nrm((N_ODD, DEC_BATCH, H_C, DK_C, DV_C), 0.5),
        'state_c_n': nrm((N_ODD, DEC_BATCH, H_C, DK_C), 0.5),
        'state_c_m': unif((N_ODD, DEC_BATCH, H_C), 0.0, 3.0),
        'state_d_shift': nrm((N_ODD, DEC_BATCH, P_D), 1.0),
        'state_d_s': nrm((N_ODD, DEC_BATCH, H_D, N_D, N_D), 0.3),
        'w_mod': nrm((DEPTH, D_MODEL, N_MOD * D_MODEL), 0.3 * D_MODEL ** -0.5),
        'b_mod': nrm((DEPTH, N_MOD * D_MODEL), 0.02),
        'ffn_w_gate': nrm((DEPTH, 2, D_MODEL, D_FF), D_MODEL ** -0.5),
        'ffn_w_up': nrm((DEPTH, 2, D_MODEL, D_FF), D_MODEL ** -0.5),
        'ffn_w_down': nrm((DEPTH, 2, D_FF, D_MODEL), D_FF ** -0.5),
        'w_in_even': nrm((N_EVEN, D_MODEL, P_EVEN), D_MODEL ** -0.5),
        'w_out_even': nrm((N_EVEN, MIX_EVEN, D_MODEL), MIX_EVEN ** -0.5),
        'a_conv_w': nrm((N_EVEN, CONV_W, W_A), CONV_W ** -0.5),
        'a_conv_b': nrm((N_EVEN, W_A), 0.02),
        'a_gate_r_w': nrm((N_EVEN, NB_A, BS_A, BS_A), BS_A ** -0.5),
        'a_gate_r_b': nrm((N_EVEN, W_A), 0.02),
        'a_gate_i_w': nrm((N_EVEN, NB_A, BS_A, BS_A), BS_A ** -0.5),
        'a_gate_i_b': nrm((N_EVEN, W_A), 0.02),
        'a_lambda': jnp.log(a_target) - jnp.log1p(-a_target),
        'b_lb_gamma': nrm((N_EVEN, H_B * DK_B), 1.0),
        'b_norm_w': gain((N_EVEN, DV_B)),
        'w_in_odd': nrm((N_ODD, D_MODEL, P_ODD), D_MODEL ** -0.5),
        'w_out_odd': nrm((N_ODD, MIX_ODD, D_MODEL), MIX_ODD ** -0.5),
        'c_igate_b': nrm((N_ODD, H_C), 0.1),
        'c_fgate_b': jnp.linspace(3.0, 6.0, H_C, dtype=f32)[None] + nrm((N_ODD, H_C), 0.1),
        'c_norm_w': gain((N_ODD, DV_C)),
        'd_mu': unif((N_ODD, P_D), 0.0, 1.0),
        'd_w0': unif((N_ODD, W_D), -6.0, 0.0),
        'd_w2': nrm((N_ODD, R_W, W_D), 0.5 * R_W ** -0.5),
        'd_a0': nrm((N_ODD, W_D), 0.1),
        'd_a2': nrm((N_ODD, R_A, W_D), R_A ** -0.5),
        'd_g2': nrm((N_ODD, R_G, W_D), R_G ** -0.5),
        'd_k_k': 0.85 + nrm((N_ODD, W_D), 0.02),
        'd_k_a': gain((N_ODD, W_D)),
        'd_r_k': nrm((N_ODD, H_D, N_D), 0.1),
        'd_ln_w': gain((N_ODD, W_D)),
        'd_ln_b': nrm((N_ODD, W_D), 0.02),
        'final_norm_w': gain((D_MODEL,)),
    }


def reference(x_prompt, x_sample, c_prompt, c_sample, state_a_conv, state_a_h, state_b_s, state_c_c, state_c_n, state_c_m, state_d_shift, state_d_s, w_mod, b_mod, ffn_w_gate, ffn_w_up, ffn_w_down, w_in_even, w_out_even, a_conv_w, a_conv_b, a_gate_r_w, a_gate_r_b, a_gate_i_w, a_gate_i_b, a_lambda, b_lb_gamma, b_norm_w, w_in_odd, w_out_odd, c_igate_b, c_fgate_b, c_norm_w, d_mu, d_w0, d_w2, d_a0, d_a2, d_g2, d_k_k, d_k_a, d_r_k, d_ln_w, d_ln_b, final_norm_w):
    p = dict(w_mod=w_mod, b_mod=b_mod, ffn_w_gate=ffn_w_gate, ffn_w_up=ffn_w_up, ffn_w_down=ffn_w_down,
             w_in_even=w_in_even, w_out_even=w_out_even, a_conv_w=a_conv_w, a_conv_b=a_conv_b,
             a_gate_r_w=a_gate_r_w, a_gate_r_b=a_gate_r_b, a_gate_i_w=a_gate_i_w, a_gate_i_b=a_gate_i_b,
             a_lambda=a_lambda, b_lb_gamma=b_lb_gamma, b_norm_w=b_norm_w, w_in_odd=w_in_odd,
             w_out_odd=w_out_odd, c_igate_b=c_igate_b, c_fgate_b=c_fgate_b, c_norm_w=c_norm_w,
             d_mu=d_mu, d_w0=d_w0, d_w2=d_w2, d_a0=d_a0, d_a2=d_a2, d_g2=d_g2, d_k_k=d_k_k,
             d_k_a=d_k_a, d_r_k=d_r_k, d_ln_w=d_ln_w, d_ln_b=d_ln_b, final_norm_w=final_norm_w)
    f32 = jnp.float32
    bp = x_prompt.shape[0]
    (y_prompt, pa_conv, pa_h, pb_s, pc_c, pc_n, pc_m, pd_shift, pd_s) = _trunk(
        x_prompt, c_prompt,
        jnp.zeros((N_EVEN, bp, CONV_W - 1, W_A), f32), jnp.zeros((N_EVEN, bp, W_A), f32),
        jnp.zeros((N_EVEN, bp, H_B, DK_B, DV_B), f32), jnp.zeros((N_ODD, bp, H_C, DK_C, DV_C), f32),
        jnp.zeros((N_ODD, bp, H_C, DK_C), f32), jnp.zeros((N_ODD, bp, H_C), f32),
        jnp.zeros((N_ODD, bp, P_D), f32), jnp.zeros((N_ODD, bp, H_D, N_D, N_D), f32), p)
    (y_sample, sa_conv, sa_h, sb_s, sc_c, sc_n, sc_m, sd_shift, sd_s) = _trunk(
        x_sample, c_sample, state_a_conv, state_a_h, state_b_s, state_c_c, state_c_n, state_c_m,
        state_d_shift, state_d_s, p)
    return (y_prompt, y_sample, pa_conv, pa_h, pb_s, pc_c, pc_n, pc_m, pd_shift, pd_s, sa_conv, sa_h, sb_s, sc_c, sc_n, sc_m, sd_shift, sd_s)
```

```python
import numpy as np
from contextlib import ExitStack
import concourse.bass as bass
import concourse.mybir as mybir
from concourse.bass_utils import run_bass_kernel_spmd

F32 = mybir.dt.float32
BF16 = mybir.dt.bfloat16
AF = mybir.ActivationFunctionType
ALU = mybir.AluOpType
AX = mybir.AxisListType
NDMA_SEM = 6


class Sched:
    ENGS = ("pe", "dve", "act", "pool", "sp")

    def __init__(self, nc, sems):
        self.nc = nc
        self.sem = sems
        self.ops = {e: [] for e in self.ENGS}
        self.cnt = {k: 0 for k in sems}
        self.seen = {e: {k: 0 for k in sems} for e in self.ENGS}
        self.last_w = {}
        self.readers = {}
        self.dma_rr = {"sp": 0, "pool": 0}
        self.nops = 0

    def _deps(self, reads, writes, skip=None):
        deps = {}

        def add(tok):
            if tok is not None and deps.get(tok[0], 0) < tok[1]:
                deps[tok[0]] = tok[1]
        for k in reads:
            add(self.last_w.get(k))
        for k in writes:
            lw = self.last_w.get(k)
            if not (skip and lw is not None and lw[0] == skip):
                add(lw)
            for r in self.readers.get(k, ()):
                add(r)
        return deps

    def _commit(self, tok, reads, writes):
        for k in reads:
            self.readers.setdefault(k, []).append(tok)
        for k in writes:
            self.last_w[k] = tok
            self.readers[k] = []

    def _waits(self, eng, deps):
        waits = []
        for s, v in deps.items():
            if self.seen[eng][s] < v:
                self.seen[eng][s] = v
                waits.append((s, v))
        return waits

    def op(self, eng, fn, reads=(), writes=(), accum=False):
        deps = self._deps(reads, writes, skip=(eng if accum else None))
        waits = self._waits(eng, deps)
        self.cnt[eng] += 1
        tok = (eng, self.cnt[eng])
        self.ops[eng].append((waits, fn, eng, 1))
        self._commit(tok, reads, writes)
        self.nops += 1
        return tok

    def dma(self, queue, fn, reads=(), writes=()):
        if queue == "pool":
            self.barrier()
        i = self.dma_rr[queue]
        self.dma_rr[queue] = (i + 1) % NDMA_SEM
        semname = f"d{queue}{i}"
        deps = self._deps(reads, writes)
        if self.cnt[semname] > 0 and deps.get(semname, 0) < self.cnt[semname]:
            deps[semname] = self.cnt[semname]
        waits = self._waits(queue, deps)
        self.cnt[semname] += 16
        tok = (semname, self.cnt[semname])
        self.ops[queue].append((waits, fn, semname, 16))
        self._commit(tok, reads, writes)
        self.nops += 1
        return tok

    def barrier(self):
        snap = dict(self.cnt)
        for e in self.ENGS:
            waits = self._waits(e, {s: v for s, v in snap.items() if v > 0})
            if waits:
                self.ops[e].append((waits, None, None, 0))

    def emit(self, block):
        sem = self.sem
        ops = self.ops

        def run(engobj, lst):
            for waits, fn, semname, inc in lst:
                for s, v in waits:
                    engobj.wait_ge(sem[s], v)
                if fn is not None:
                    fn(engobj).then_inc(sem[semname], inc)

        @block.tensor
        def _(e):
            run(e, ops["pe"])

        @block.vector
        def _(e):
            run(e, ops["dve"])

        @block.scalar
        def _(e):
            run(e, ops["act"])

        @block.gpsimd
        def _(e):
            run(e, ops["pool"])

        @block.sync
        def _(e):
            run(e, ops["sp"])


def sem_names():
    names = ["pe", "dve", "act", "pool"]
    for q in ("sp", "pool"):
        for i in range(NDMA_SEM):
            names.append(f"d{q}{i}")
    return names


class Cfg:
    def __init__(self, SEQ=2048, NSEG=2, DFF=5632, DEPTH=4, NS=16):
        self.D = 2048
        self.SEQ = SEQ
        self.NSEG = NSEG
        self.T = SEQ // NSEG
        self.DFF = DFF
        self.DEPTH = DEPTH
        self.NS = NS
        self.NE = (DEPTH + 1) // 2
        self.NO = DEPTH // 2
        self.PE_ = 6144
        self.PO = 6408
        self.PD = 3328


CL = 64
EPS = 1e-6
EV_ROWS = ["a_conv_w0", "a_conv_w1", "a_conv_w2", "a_conv_w3", "a_conv_b", "a_gate_r_b", "a_gate_i_b", "a_lambda", "b_lb_gamma"]
OD_ROWS = ["d_w0", "d_a0", "d_k_k", "d_k_a", "d_r_k", "d_ln_w", "d_ln_b"]


def build(cfg):
    D, T, NS, DFF, DEPTH, NE, NO = cfg.D, cfg.T, cfg.NS, cfg.DFF, cfg.DEPTH, cfg.NE, cfg.NO
    SEQ, NSEG = cfg.SEQ, cfg.NSEG
    NCH = 16
    CMAX = T + NS
    NCK = T // CL
    NFF = DFF // 128
    NR = 9 * NE + 7 * NO + 2
    nc = bass.Bass("TRN2", target_bir_lowering=False)

    def din(name, shape):
        return nc.dram_tensor(name, list(shape), F32, kind="ExternalInput").ap()

    def dout(name, shape):
        return nc.dram_tensor(name, list(shape), F32, kind="ExternalOutput").ap()

    def dscr(name, shape):
        return nc.dram_tensor(name, list(shape), F32, kind="Internal").ap()

    xp = din("xp", [SEQ, D]); xs = din("xs", [NS, D]); cc = din("cc", [1 + NS, D])
    i_a_conv = din("i_a_conv", [NE, NS, 3, 1024]); i_a_h = din("i_a_h", [NE, NS, 1024])
    i_b_s = din("i_b_s", [NE, NS, 8, 128, 128])
    i_c_c = din("i_c_c", [max(NO, 1), NS, 4, 128, 256]); i_c_n = din("i_c_n", [max(NO, 1), NS, 4, 128])
    i_c_m = din("i_c_m", [max(NO, 1), NS, 4]); i_d_shift = din("i_d_shift", [max(NO, 1), NS, 3328])
    i_d_s = din("i_d_s", [max(NO, 1), NS, 16, 64, 64])
    w_mod = din("w_mod", [DEPTH, D, 9 * D]); b_mod = din("b_mod", [DEPTH, 9 * D])
    w_gate = din("ffn_w_gate", [DEPTH, 2, D, DFF]); w_up = din("ffn_w_up", [DEPTH, 2, D, DFF])
    w_down = din("ffn_w_down", [DEPTH, 2, DFF, D])
    w_in_even = din("w_in_even", [NE, D, 6144]); w_out_even = din("w_out_even", [NE, D, D])
    w_in_odd = din("w_in_odd", [max(NO, 1), D, 6408]); w_out_odd = din("w_out_odd", [max(NO, 1), D, D])
    prm = din("prm", [NR, 1024])
    a_gate_r_w = din("a_gate_r_w", [NE, 8, 128, 128]); a_gate_i_w = din("a_gate_i_w", [NE, 8, 128, 128])
    b_norm_w = din("b_norm_w", [NE, 128]); c_norm_w = din("c_norm_w", [max(NO, 1), 256])
    c_gb = din("c_gb", [1, 8 * max(NO, 1)])
    d_mu = din("d_mu", [max(NO, 1), 3328])
    d_w2 = din("d_w2", [max(NO, 1), 64, 1024]); d_a2 = din("d_a2", [max(NO, 1), 64, 1024])
    d_g2 = din("d_g2", [max(NO, 1), 128, 1024])
    c_ident = din("c_ident", [128, 128]); c_mmul = din("c_mmul", [64, 64]); c_madd = din("c_madd", [64, 64])
    c_cmask = din("c_cmask", [128, T]); c_blk = din("c_blk", [128, 128]); c_sel2 = din("c_sel2", [128, 64]); c_sel3 = din("c_sel3", [128, 64 * 128])

    y_p = dout("y_p", [SEQ, D]); y_s = dout("y_s", [NS, D])
    pa_conv = dout("pa_conv", [NE, 3, 1024]); pa_h = dout("pa_h", [NE, 1024]); pb_s = dout("pb_s", [NE, 8, 128, 128])
    pc_c = dout("pc_c", [max(NO, 1), 4, 128, 256]); pc_n = dout("pc_n", [max(NO, 1), 4, 128]); pc_m = dout("pc_m", [max(NO, 1), 4])
    pd_shift = dout("pd_shift", [max(NO, 1), 3328]); pd_s = dout("pd_s", [max(NO, 1), 16, 64, 64])
    sa_conv = dout("sa_conv", [NE, NS, 3, 1024]); sa_h = dout("sa_h", [NE, NS, 1024]); sb_s = dout("sb_s", [NE, NS, 8, 128, 128])
    sc_c = dout("sc_c", [max(NO, 1), NS, 4, 128, 256]); sc_n = dout("sc_n", [max(NO, 1), NS, 4, 128]); sc_m = dout("sc_m", [max(NO, 1), NS, 4])
    sd_shift = dout("sd_shift", [max(NO, 1), NS, 3328]); sd_s = dout("sd_s", [max(NO, 1), NS, 16, 64, 64])

    modscr = dscr("modscr", [DEPTH, 128, 144 * 17])
    xscr = dscr("xscr", [128, NCH * CMAX])
    rwscr = dscr("rwscr", [8, 8, 128, CMAX])

    with ExitStack() as es:
        sems = {n: es.enter_context(nc.semaphore(n)) for n in sem_names()}
        S = Sched(nc, sems)
        scopes = [es]

        uid = [0]

        def sb(name, shape, dt=F32):
            uid[0] += 1
            return scopes[-1].enter_context(nc.sbuf_tensor(f"{name}_u{uid[0]}", list(shape), dt))

        class scope:
            def __enter__(self_):
                self_.st = ExitStack()
                self_.st.__enter__()
                scopes.append(self_.st)

            def __exit__(self_, *a):
                S.barrier()
                scopes.pop()
                self_.st.__exit__(*a)

        PS = [es.enter_context(nc.psum_tensor(f"ps{i}", [128, 512], F32)) for i in range(8)]
        psrr = [0]

        def ps():
            i = psrr[0]
            psrr[0] = (i + 1) % 8
            return PS[i], f"ps{i}"

        def mm(out, lhsT, rhs, start, stop, r, w):
            S.op("pe", lambda e: e.matmul(out, lhsT=lhsT, rhs=rhs, start=start, stop=stop), reads=r, writes=w, accum=not start)

        def act(out, in_, func, r, w, bias=None, scale=None, accum_out=None):
            kw = {}
            if bias is not None:
                kw["bias"] = bias
            if scale is not None:
                kw["scale"] = scale
            if accum_out is not None:
                kw["accum_out"] = accum_out
            S.op("act", lambda e: e.activation(out=out, in_=in_, func=func, **kw), reads=r, writes=w)

        def tt(out, in0, in1, op, r, w, eng="dve"):
            S.op(eng, lambda e: e.tensor_tensor(out=out, in0=in0, in1=in1, op=op), reads=r, writes=w)

        def ts(out, in0, s1, s2, op0, op1, r, w, eng="dve"):
            if op1 is None:
                S.op(eng, lambda e: e.tensor_scalar(out=out, in0=in0, scalar1=s1, scalar2=None, op0=op0), reads=r, writes=w)
            else:
                S.op(eng, lambda e: e.tensor_scalar(out=out, in0=in0, scalar1=s1, scalar2=s2, op0=op0, op1=op1), reads=r, writes=w)

        def stt(out, in0, scalar, in1, op0, op1, r, w, accum_out=None):
            if accum_out is None:
                S.op("dve", lambda e: e.scalar_tensor_tensor(out=out, in0=in0, scalar=scalar, in1=in1, op0=op0, op1=op1), reads=r, writes=w)
            else:
                S.op("dve", lambda e: e.scalar_tensor_tensor(out=out, in0=in0, scalar=scalar, in1=in1, op0=op0, op1=op1, accum_out=accum_out), reads=r, writes=w)

        def cp(out, in_, r, w, eng="act"):
            if eng == "act":
                S.op("act", lambda e: e.activation(out=out, in_=in_, func=AF.Copy), reads=r, writes=w)
            else:
                S.op(eng, lambda e: e.tensor_copy(out=out, in_=in_), reads=r, writes=w)

        def mset(ap, val, w, eng="dve"):
            S.op(eng, lambda e: e.memset(ap, val), writes=w)

        def ld(out, in_, r, w, q="sp"):
            S.dma(q, lambda e: e.dma_start(out=out, in_=in_, allow_slow_non_contiguous=True), reads=r, writes=w)

        def tp(out, in_, npart, r, w):
            S.op("pe", lambda e: e.transpose(out, in_, ident[:npart, :npart]), reads=list(r) + ["ident"], writes=w)

        ident = sb("ident", [128, 128]); mmul = sb("mmul", [64, 64]); madd = sb("madd", [64, 64])
        cmask = sb("cmask", [128, T]); blk = sb("blk", [128, 128]); sel2 = sb("sel2", [128, 64])
        onesf = sb("onesf", [128, 128]); ones128 = sb("ones128", [128, 128]); onesD = sb("onesD", [128, 128], BF16)
        epsc = sb("epsc", [128, 1]); gnepsc = sb("gnepsc", [128, 1]); onec = sb("onec", [128, 1])
        prmT = sb("prmT", [128, 8, NR])
        omlT = sb("omlT", [128, 8, NE]); nspT = sb("nspT", [128, 8, NE])
        bnwT = sb("bnwT", [128, NE]); cnwT = sb("cnwT", [128, 2, max(NO, 1)]); muT = sb("muT", [128, 26, max(NO, 1)])
        cgb = sb("cgb", [1, 8 * max(NO, 1)])
        convst = sb("convst", [128, NE, 8, 3]); hst = sb("hst", [128, NE, 8]); mst = sb("mst", [1, 4 * max(NO, 1)])
        shst = sb("shst", [128, max(NO, 1), 26])
        modT = sb("modT", [128, 144, 17])
        hT = sb("hT", [128, NCH, CMAX], BF16)

        ld(ident[:], c_ident, [], ["ident"]); ld(mmul[:], c_mmul, [], ["mmul"]); ld(madd[:], c_madd, [], ["madd"])
        ld(cmask[:], c_cmask, [], ["cmask"]); ld(blk[:], c_blk, [], ["blk"]); ld(sel2[:], c_sel2, [], ["sel2"])
        mset(onesf[:], 1.0, ["onesf"]); mset(ones128[:], 1.0 / 128.0, ["ones128"]); mset(onesD[:], 1.0 / D, ["onesD"])
        mset(epsc[:], EPS, ["epsc"]); mset(gnepsc[:], 64e-5, ["gnepsc"]); mset(onec[:], 1.0, ["onec"])
        mset(convst[:], 0.0, ["convst"]); mset(hst[:], 0.0, ["hst"]); mset(mst[:], 0.0, ["mst"]); mset(shst[:], 0.0, ["shst"])
        ld(cgb[:], c_gb, [], ["cgb"])

        with scope():
            prow = sb("prow", [NR, 1024])
            ld(prow[:], prm, [], ["prow"])
            for c in range(8):
                p_, k_ = ps()
                tp(p_[:, :NR], prow[:NR, c * 128:(c + 1) * 128], NR, ["prow"], [k_])
                cp(prmT[:, c, :], p_[:, :NR], [k_], ["prmT"])
            if NO:
                murow = sb("murow", [NO, 3328])
                ld(murow[:], d_mu, [], ["murow"])
                for c in range(26):
                    p_, k_ = ps()
                    tp(p_[:, :NO], murow[:NO, c * 128:(c + 1) * 128], NO, ["murow"], [k_])
                    cp(muT[:, c, :], p_[:, :NO], [k_], ["muT"])
                cnrow = sb("cnrow", [NO, 256])
                ld(cnrow[:], c_norm_w, [], ["cnrow"])
                for c in range(2):
                    p_, k_ = ps()
                    tp(p_[:, :NO], cnrow[:NO, c * 128:(c + 1) * 128], NO, ["cnrow"], [k_])
                    cp(cnwT[:, c, :], p_[:, :NO], [k_], ["cnwT"])
            bnrow = sb("bnrow", [NE, 128])
            ld(bnrow[:], b_norm_w, [], ["bnrow"])
            p_, k_ = ps()
            tp(p_[:, :NE], bnrow[:NE, :], NE, ["bnrow"], [k_])
            cp(bnwT[:, :], p_[:, :NE], [k_], ["bnwT"])
            tmpa = sb("tmpa", [128, 8]); tmpz = sb("tmpz", [128, 8]); tmpe = sb("tmpe", [128, 8, NE])
            for j in range(NE):
                act(tmpa[:], prmT[:, :, 9 * j + 7], AF.Exp, ["prmT"], ["tmpa"], scale=-1.0)
                act(tmpa[:], tmpa[:], AF.Ln, ["tmpa", "onec"], ["tmpa"], bias=onec[:, 0:1], scale=1.0)
                ts(nspT[:, :, j], tmpa[:], -8.0, None, ALU.mult, None, ["tmpa"], ["nspT"])
                act(tmpe[:, :, j], prmT[:, :, 9 * j + 8], AF.Exp, ["prmT"], ["tmpe"])
            cp(tmpz[:], tmpe[:, :, 0], ["tmpe"], ["tmpz"], eng="dve")
            for j in range(1, NE):
                tt(tmpz[:], tmpz[:], tmpe[:, :, j], ALU.add, ["tmpz", "tmpe"], ["tmpz"])
            S.op("dve", lambda e: e.reciprocal(out=tmpz[:], in_=tmpz[:]), reads=["tmpz"], writes=["tmpz"])
            mset(omlT[:, :, 0], 1.0, ["omlT"])
            for j in range(1, NE):
                tt(tmpa[:], tmpe[:, :, j], tmpz[:], ALU.mult, ["tmpe", "tmpz"], ["tmpa"])
                tt(omlT[:, :, j], omlT[:, :, j - 1], tmpa[:], ALU.subtract, ["omlT", "tmpa"], ["omlT"])

        def loadw(pool, wsrc, col0, ncol, tag):
            i = pool["rr"]
            pool["rr"] = (i + 1) % len(pool["t"])
            t_, k_ = pool["t"][i], f"{pool['name']}{i}"
            src = wsrc[:, col0:col0 + ncol].rearrange("(c p) n -> p c n", p=128)
            S.dma("pool", lambda e: e.dma_start(out=t_[:, :, :ncol], in_=src), reads=[], writes=[k_])
            return t_, k_

        def mkpool(name, n, width):
            return {"name": name, "rr": 0, "t": [sb(f"{name}{i}", [128, NCH, width], BF16) for i in range(n)]}

        with scope():
            crow = sb("crow", [1 + NS, D]); csT = sb("csT", [128, NCH, 17], BF16)
            brow = sb("brow", [72, 2, 128]); bmT = sb("bmT", [128, 144])
            mo = sb("mo", [128, 144, 17])
            wp = mkpool("wm", 3, 512)
            ld(crow[:], cc, [], ["crow"])
            act(crow[:], crow[:], AF.Silu, ["crow"], ["crow"])
            for c in range(NCH):
                p_, k_ = ps()
                tp(p_[:, :1 + NS], crow[:1 + NS, c * 128:(c + 1) * 128], 1 + NS, ["crow"], [k_])
                cp(csT[:, c, :], p_[:, :1 + NS], [k_], ["csT"])
            for l in range(DEPTH):
                ld(brow[:], b_mod[l].rearrange("(h r p) -> r h p", h=2, p=128), [], ["brow"])
                for h in range(2):
                    p_, k_ = ps()
                    tp(p_[:, :72], brow[:72, h, :], 72, ["brow"], [k_])
                    cp(bmT[:, h * 72:(h + 1) * 72], p_[:, :72], [k_], ["bmT"])
                for g in range(36):
                    wt, wk = loadw(wp, w_mod[l], g * 512, 512, "wm")
                    p_, k_ = ps()
                    for oc in range(4):
                        for c in range(NCH):
                            mm(p_[:, oc * 17:(oc + 1) * 17], wt[:, c, oc * 128:(oc + 1) * 128], csT[:, c, :], c == 0, c == NCH - 1, [wk, "csT"], [k_])
                    tt(mo[:, g * 4:(g + 1) * 4, :], p_[:, :68].rearrange("p (a b) -> p a b", b=17),
                       bmT[:, g * 4:(g + 1) * 4].unsqueeze(2).to_broadcast([128, 4, 17]), ALU.add, [k_, "bmT"], ["mo"])
                ld(modscr[l], mo[:].rearrange("p a b -> p (a b)"), ["mo"], [f"modscr{l}"])

        def col_tiles(C, has_s):
            tl = []
            c0 = 0
            while c0 < T:
                n = min(512, T - c0)
                tl.append((c0, n, False))
                c0 += n
            if has_s:
                tl.append((T, NS, True))
            return tl

        def modv(k, c):
            return modT[:, k * 16 + c, :]

        for seg in range(NSEG):
            has_s = (seg == NSEG - 1)
            last = has_s
            C = T + (NS if has_s else 0)
            ctl = col_tiles(C, has_s)
            tok0 = seg * T

            def xk(c, j):
                return f"xT:{c}:{j}"

            def hk(c, j):
                return f"hT:{c}:{j}"

            allx = [xk(c, j) for c in range(NCH) for j in range(len(ctl))]
            allh = [hk(c, j) for c in range(NCH) for j in range(len(ctl))]

            XT = [None]
            if True:
                def load_x():
                  with scope():
                      xrow = [sb(f"xrow{i}", [128, D]) for i in range(2)]
                      blocks = [(tok0 + b * 128, 128, b * 128, xp) for b in range(T // 128)]
                      if has_s:
                          blocks.append((0, NS, T, xs))
                      for bi, (r0, n, c0, src) in enumerate(blocks):
                          xr = xrow[bi % 2]
                          ld(xr[:n, :], src[r0:r0 + n, :], [], [f"xrow{bi % 2}"])
                          for c in range(NCH):
                              p_, k_ = ps()
                              tp(p_[:, :n], xr[:n, c * 128:(c + 1) * 128], n, [f"xrow{bi % 2}"], [k_])
                              cp(XT[0][:, c, c0:c0 + n], p_[:, :n], [k_], allx, eng=("act" if c % 2 else "dve"))

                def adaln(ksh, ksc, scale_ap=None):
                    with scope():
                        sq = [sb(f"sq{i}", [128, 512], BF16) for i in range(2)]
                        tmp = [sb(f"tmp{i}", [128, 512]) for i in range(2)]
                        rstd = sb("rstd", [128, 512])
                        for j, (cs, cn, is_s) in enumerate(ctl):
                            p_, k_ = ps()
                            for c in range(NCH):
                                act(sq[c % 2][:, :cn], XT[0][:, c, cs:cs + cn], AF.Square, [xk(c, j)], [f"sq{c % 2}"])
                                mm(p_[:, :cn], onesD[:], sq[c % 2][:, :cn], c == 0, c == NCH - 1, [f"sq{c % 2}", "onesD"], [k_])
                            act(rstd[:, :cn], p_[:, :cn], AF.Ln, [k_, "epsc"], ["rstd"], bias=epsc[:, 0:1], scale=1.0)
                            act(rstd[:, :cn], rstd[:, :cn], AF.Exp, ["rstd"], ["rstd"], scale=-0.5)
                            for c in range(NCH):
                                t_ = tmp[c % 2]; tk = f"tmp{c % 2}"
                                tt(t_[:, :cn], XT[0][:, c, cs:cs + cn], rstd[:, :cn], ALU.mult, [xk(c, j), "rstd"], [tk])
                                if not is_s:
                                    ts(hT[:, c, cs:cs + cn], t_[:, :cn], modv(ksc, c)[:, 0:1], modv(ksh, c)[:, 0:1], ALU.mult, ALU.add,
                                       [tk, "modT"], [hk(c, j)], eng="pool")
                                else:
                                    tt(t_[:, :cn], t_[:, :cn], modv(ksc, c)[:, 1:1 + NS], ALU.mult, [tk, "modT"], [tk])
                                    tt(hT[:, c, cs:cs + cn], t_[:, :cn], modv(ksh, c)[:, 1:1 + NS], ALU.add, [tk, "modT"], [hk(c, j)])

                stmp = sb("stmp", [128, NS])

                def resid_add(dc, j, cs, cn, is_s, p_, k_, kg):
                    if not is_s:
                        stt(XT[0][:, dc, cs:cs + cn], p_[:, :cn], modv(kg, dc)[:, 0:1], XT[0][:, dc, cs:cs + cn], ALU.mult, ALU.add,
                            [k_, "modT", xk(dc, j)], [xk(dc, j)])
                    else:
                        tt(stmp[:, :cn], p_[:, :cn], modv(kg, dc)[:, 1:1 + NS], ALU.mult, [k_, "modT"], ["stmp"])
                        tt(XT[0][:, dc, cs:cs + cn], XT[0][:, dc, cs:cs + cn], stmp[:, :cn], ALU.add, ["stmp", xk(dc, j)], [xk(dc, j)])

                def ffn(l, s, kg):
                    G = 2
                    with scope():
                        wgp = mkpool("wg", 2, G * 128); wup = mkpool("wu", 2, G * 128)
                        wdt = [sb(f"wd{i}", [128, G, D], BF16) for i in range(2)]
                        actb = [sb(f"actb{i}", [128, G, CMAX], BF16) for i in range(2)]
                        sg = [sb(f"sg{i}", [128, 512]) for i in range(2)]
                        ngrp = (NFF + G - 1) // G
                        for gi in range(ngrp):
                            f0 = gi * G
                            g_n = min(G, NFF - f0)
                            wg, wgk = loadw(wgp, w_gate[l, s], f0 * 128, g_n * 128, "wg")
                            wu, wuk = loadw(wup, w_up[l, s], f0 * 128, g_n * 128, "wu")
                            wd, wdk = wdt[gi % 2], f"wd{gi % 2}"
                            S.dma("pool", lambda e, wd=wd, f0=f0, g_n=g_n: e.dma_start(
                                out=wd[:, :g_n, :], in_=w_down[l, s, f0 * 128:(f0 + g_n) * 128, :].rearrange("(g p) d -> p g d", p=128)),
                                reads=[], writes=[wdk])
                            ab, abk = actb[gi % 2], f"actb{gi % 2}"
                            for g in range(g_n):
                                for j, (cs, cn, is_s) in enumerate(ctl):
                                    pg, pgk = ps(); pu, puk = ps()
                                    for c in range(NCH):
                                        mm(pg[:, :cn], wg[:, c, g * 128:(g + 1) * 128], hT[:, c, cs:cs + cn], c == 0, c == NCH - 1, [wgk, hk(c, j)], [pgk])
                                    for c in range(NCH):
                                        mm(pu[:, :cn], wu[:, c, g * 128:(g + 1) * 128], hT[:, c, cs:cs + cn], c == 0, c == NCH - 1, [wuk, hk(c, j)], [puk])
                                    sgi = (g * len(ctl) + j) % 2
                                    act(sg[sgi][:, :cn], pg[:, :cn], AF.Silu, [pgk], [f"sg{sgi}"])
                                    tt(ab[:, g, cs:cs + cn], sg[sgi][:, :cn], pu[:, :cn], ALU.mult, [f"sg{sgi}", puk], [f"{abk}:{g}:{j}"])
                            for dc in range(NCH):
                                for j, (cs, cn, is_s) in enumerate(ctl):
                                    pd, pdk = ps()
                                    for g in range(g_n):
                                        mm(pd[:, :cn], wd[:, g, dc * 128:(dc + 1) * 128], ab[:, g, cs:cs + cn], g == 0, g == g_n - 1,
                                           [wdk, f"{abk}:{g}:{j}"], [pdk])
                                    resid_add(dc, j, cs, cn, is_s, pd, pdk, kg)

                def out_proj(wsrc, kg, yT, yk):
                    with scope():
                        wop = mkpool("wo", 2, 512)
                        for dg in range(4):
                            wt, wk = loadw(wop, wsrc, dg * 512, 512, "wo")
                            for d4 in range(4):
                                dc = dg * 4 + d4
                                for j, (cs, cn, is_s) in enumerate(ctl):
                                    p_, k_ = ps()
                                    for m in range(NCH):
                                        mm(p_[:, :cn], wt[:, m, d4 * 128:(d4 + 1) * 128], yT[:, m, cs:cs + cn], m == 0, m == NCH - 1, [wk, yk(m)], [k_])
                                    resid_add(dc, j, cs, cn, is_s, p_, k_, kg)

                def proj(wt, wk, off, n, dst, dk, func=AF.Copy, bias=None, scale=None, extra_r=()):
                    for j, (cs, cn, is_s) in enumerate(ctl):
                        p_, k_ = ps()
                        for c in range(NCH):
                            mm(p_[:n, :cn], wt[:, c, off:off + n], hT[:, c, cs:cs + cn], c == 0, c == NCH - 1, [wk, hk(c, j)], [k_])
                        act(dst(cs, cn), p_[:n, :cn], func, [k_] + list(extra_r), [dk], bias=bias, scale=scale)

                seqs = [("p", 0, [(c * CL, CL, c) for c in range(NCK)])]
                if has_s:
                    seqs += [("s", s, [(T + s, 1, NCK + s)]) for s in range(NS)]
                NCKA = NCK + (NS if has_s else 0)

                def even_mixer(l, yT, yk):
                    j = l // 2
                    Wd = w_in_even[j]
                    R0 = 9 * j

                    def pr(row, n):
                        return prmT[:, n, R0 + row:R0 + row + 1]
                    with scope():
                        wpool = mkpool("wi", 6, 128)
                        xpad = sb("xpad", [128, 3 + T]); u = sb("u", [128, CMAX]); axs = sb("axs", [128, NS])
                        agt = sb("agt", [128, CMAX]); rg = sb("rg", [128, CMAX]); ig = sb("ig", [128, CMAX])
                        aa = sb("aa", [128, CMAX]); bb = sb("bb", [128, CMAX]); hh = sb("hh", [128, CMAX])
                        wr = [sb(f"wr{i}", [128, 128]) for i in range(2)]; wi_ = [sb(f"wig{i}", [128, 128]) for i in range(2)]
                        cvs = sb("cvs", [128, 8, 3, NS]); h0s = sb("h0s", [128, 8, NS])
                        ocv = sb("ocv", [3, 1024]); ohs = sb("ohs", [8, 128])
                        ocs = sb("ocs", [NS, 3, 1024]); ohss = sb("ohss", [NS, 1024])
                        if has_s:
                            crow_ = sb("crow_", [NS, 3, 1024]); hrow_ = sb("hrow_", [NS, 1024])
                            ld(crow_[:], i_a_conv[j], [], ["crow_"]); ld(hrow_[:], i_a_h[j], [], ["hrow_"])
                            for n in range(8):
                                for k in range(3):
                                    p_, k_ = ps()
                                    tp(p_[:, :NS], crow_[:NS, k, n * 128:(n + 1) * 128], NS, ["crow_"], [k_])
                                    cp(cvs[:, n, k, :], p_[:, :NS], [k_], ["cvs"])
                                p_, k_ = ps()
                                tp(p_[:, :NS], hrow_[:NS, n * 128:(n + 1) * 128], NS, ["hrow_"], [k_])
                                cp(h0s[:, n, :], p_[:, :NS], [k_], ["h0s"])
                        for n in range(8):
                            wax, waxk = loadw(wpool, Wd, n * 128, 128, "wi")
                            wag, wagk = loadw(wpool, Wd, 1024 + n * 128, 128, "wi")
                            ld(wr[n % 2][:], a_gate_r_w[j, n], [], [f"wr{n % 2}"]); ld(wi_[n % 2][:], a_gate_i_w[j, n], [], [f"wig{n % 2}"])
                            proj(wax, waxk, 0, 128, lambda cs, cn: (xpad[:, 3 + cs:3 + cs + cn] if cs < T else axs[:, :cn]), "xpad")
                            proj(wag, wagk, 0, 128, lambda cs, cn: agt[:, cs:cs + cn], "agt")
                            cp(xpad[:, 0:3], convst[:, j, n, :], ["convst"], ["xpad"], eng="dve")
                            ts(u[:, 0:T], xpad[:, 3:3 + T], pr(3, n), pr(4, n), ALU.mult, ALU.add, ["xpad", "prmT"], ["u"])
                            for k in range(3):
                                stt(u[:, 0:T], xpad[:, k:k + T], pr(k, n), u[:, 0:T], ALU.mult, ALU.add, ["xpad", "prmT", "u"], ["u"])
                            cp(convst[:, j, n, :], xpad[:, T:T + 3], ["xpad"], ["convst"], eng="dve")
                            if has_s:
                                ts(u[:, T:T + NS], axs[:, :], pr(3, n), pr(4, n), ALU.mult, ALU.add, ["xpad", "prmT"], ["u"])
                                for k in range(3):
                                    stt(u[:, T:T + NS], cvs[:, n, k, :], pr(k, n), u[:, T:T + NS], ALU.mult, ALU.add, ["cvs", "prmT", "u"], ["u"])
                            for jj, (cs, cn, is_s) in enumerate(ctl):
                                p_, k_ = ps()
                                mm(p_[:, :cn], wr[n % 2][:], u[:, cs:cs + cn], True, True, [f"wr{n % 2}", "u"], [k_])
                                act(rg[:, cs:cs + cn], p_[:, :cn], AF.Sigmoid, [k_, "prmT"], ["rg"], bias=pr(5, n), scale=1.0)
                                p_, k_ = ps()
                                mm(p_[:, :cn], wi_[n % 2][:], u[:, cs:cs + cn], True, True, [f"wig{n % 2}", "u"], [k_])
                                act(ig[:, cs:cs + cn], p_[:, :cn], AF.Sigmoid, [k_, "prmT"], ["ig"], bias=pr(6, n), scale=1.0)
                            act(aa[:, :C], rg[:, :C], AF.Exp, ["rg", "nspT"], ["aa"], scale=nspT[:, n, j:j + 1])
                            tt(rg[:, :C], aa[:, :C], aa[:, :C], ALU.mult, ["aa"], ["rg"])
                            act(rg[:, :C], rg[:, :C], AF.Sqrt, ["rg", "onec"], ["rg"], bias=onec[:, 0:1], scale=-1.0)
                            tt(bb[:, :C], ig[:, :C], u[:, :C], ALU.mult, ["ig", "u"], ["bb"])
                            tt(bb[:, :C], bb[:, :C], rg[:, :C], ALU.mult, ["bb", "rg"], ["bb"])
                            S.op("dve", lambda e, n=n: e.tensor_tensor_scan(out=hh[:, 0:T], data0=aa[:, 0:T], data1=bb[:, 0:T],
                                                                         initial=hst[:, j, n:n + 1], op0=ALU.mult, op1=ALU.add),
                                 reads=["aa", "bb", "hst"], writes=["hh"])
                            cp(hst[:, j, n:n + 1], hh[:, T - 1:T], ["hh"], ["hst"], eng="dve")
                            if has_s:
                                tt(hh[:, T:T + NS], aa[:, T:T + NS], h0s[:, n, :], ALU.mult, ["aa", "h0s"], ["hh"])
                                tt(hh[:, T:T + NS], hh[:, T:T + NS], bb[:, T:T + NS], ALU.add, ["hh", "bb"], ["hh"])
                            tt(ig[:, :C], agt[:, :C], agt[:, :C], ALU.mult, ["agt"], ["ig"])
                            ts(ig[:, :C], ig[:, :C], 0.044715, 1.0, ALU.mult, ALU.add, ["ig"], ["ig"])
                            tt(ig[:, :C], ig[:, :C], agt[:, :C], ALU.mult, ["ig", "agt"], ["ig"])
                            act(ig[:, :C], ig[:, :C], AF.Sigmoid, ["ig"], ["ig"], scale=1.5957691216057308)
                            tt(ig[:, :C], ig[:, :C], agt[:, :C], ALU.mult, ["ig", "agt"], ["ig"])
                            tt(yT[:, n, :C], ig[:, :C], hh[:, :C], ALU.mult, ["ig", "hh"], [yk(n)])
                            if last:
                                p_, k_ = ps()
                                tp(p_[:3, :128], convst[:, j, n, :], 128, ["convst"], [k_])
                                cp(ocv[:3, n * 128:(n + 1) * 128], p_[:3, :128], [k_], ["ocv"])
                                srcs = [cvs[:, n, 1, :], cvs[:, n, 2, :], axs[:, :]]
                                for k in range(3):
                                    p_, k_ = ps()
                                    tp(p_[:NS, :128], srcs[k], 128, ["cvs", "xpad"], [k_])
                                    cp(ocs[:NS, k, n * 128:(n + 1) * 128], p_[:NS, :128], [k_], ["ocs"])
                                p_, k_ = ps()
                                tp(p_[:NS, :128], hh[:, T:T + NS], 128, ["hh"], [k_])
                                cp(ohss[:NS, n * 128:(n + 1) * 128], p_[:NS, :128], [k_], ["ohss"])
                        if last:
                            p_, k_ = ps()
                            tp(p_[:8, :128], hst[:, j, :], 128, ["hst"], [k_])
                            cp(ohs[:8, :], p_[:8, :128], [k_], ["ohs"])
                            ld(pa_conv[j], ocv[:], ["ocv"], [f"pa_conv{j}"])
                            ld(pa_h[j].rearrange("(a b) -> a b", b=128), ohs[:], ["ohs"], [f"pa_h{j}"])
                            ld(sa_conv[j], ocs[:], ["ocs"], [f"sa_conv{j}"])
                            ld(sa_h[j], ohss[:], ["ohss"], [f"sa_h{j}"])
                    with scope():
                        wpool = mkpool("wi", 6, 128)
                        q = sb("q", [128, CMAX]); kk = sb("kk", [128, CMAX]); iv = sb("iv", [128, CMAX]); gs = sb("gs", [128, CMAX])
                        bc = sb("bc", [128, CMAX]); d1 = sb("d1", [128, CMAX]); e2 = sb("e2", [128, CMAX])
                        qin = sb("qin", [128, CMAX]); qt = sb("qt", [128, CMAX]); kt = sb("kt", [128, CMAX]); ko = sb("ko", [128, CMAX])
                        eg = sb("eg", [128, NCK + NS]); o = sb("o", [128, CMAX])
                        Sst = [sb(f"Sst{i}", [128, 128]) for i in range(2)]
                        attT = sb("attT", [64, 64]); ivtok = sb("ivtok", [64, 128]); kotok = sb("kotok", [64, 128])

                        def v3(t_):
                            return t_[:, 0:T].rearrange("p (c t) -> p c t", t=CL)
                        for hd in range(8):
                            wq, wqk = loadw(wpool, Wd, 2048 + hd * 128, 128, "wi")
                            wf, wfk = loadw(wpool, Wd, 3072 + hd * 128, 128, "wi")
                            wv, wvk = loadw(wpool, Wd, 4096 + hd * 128, 128, "wi")
                            wg_, wgk_ = loadw(wpool, Wd, 5120 + hd * 128, 128, "wi")
                            proj(wq, wqk, 0, 128, lambda cs, cn: q[:, cs:cs + cn], "q")
                            proj(wf, wfk, 0, 128, lambda cs, cn: kk[:, cs:cs + cn], "kk", func=AF.Sigmoid, scale=-1.0)
                            proj(wv, wvk, 0, 128, lambda cs, cn: iv[:, cs:cs + cn], "iv")
                            proj(wg_, wgk_, 0, 128, lambda cs, cn: gs[:, cs:cs + cn], "gs", func=AF.Silu)
                            ts(kk[:, :C], kk[:, :C], omlT[:, hd, j:j + 1], None, ALU.mult, None, ["kk", "omlT"], ["kk"])
                            act(d1[:, :C], kk[:, :C], AF.Ln, ["kk", "onec"], ["d1"], bias=onec[:, 0:1], scale=-1.0)
                            S.op("dve", lambda e: e.tensor_tensor_scan(out=bc[:, 0:T], data0=cmask[:, 0:T], data1=d1[:, 0:T], initial=0.0,
                                                                     op0=ALU.mult, op1=ALU.add), reads=["cmask", "d1"], writes=["bc"])
                            if has_s:
                                cp(bc[:, T:T + NS], d1[:, T:T + NS], ["d1"], ["bc"], eng="dve")
                            act(e2[:, :C], bc[:, :C], AF.Exp, ["bc"], ["e2"])
                            tt(qin[:, :C], q[:, :C], e2[:, :C], ALU.mult, ["q", "e2"], ["qin"])
                            act(eg[:, 0:NCK], v3(bc)[:, :, CL - 1], AF.Exp, ["bc"], ["eg"])
                            if has_s:
                                cp(eg[:, NCK:NCK + NS], e2[:, T:T + NS], ["e2"], ["eg"], eng="dve")
                            tt(v3(d1), v3(bc), v3(bc)[:, :, CL // 2 - 1:CL // 2].to_broadcast([128, NCK, CL]), ALU.subtract, ["bc"], ["d1"])
                            if has_s:
                                mset(d1[:, T:T + NS], 0.0, ["d1"])
                            act(e2[:, :C], d1[:, :C], AF.Exp, ["d1"], ["e2"])
                            tt(qt[:, :C], q[:, :C], e2[:, :C], ALU.mult, ["q", "e2"], ["qt"])
                            act(e2[:, :C], d1[:, :C], AF.Exp, ["d1"], ["e2"], scale=-1.0)
                            tt(kt[:, :C], kk[:, :C], e2[:, :C], ALU.mult, ["kk", "e2"], ["kt"])
                            tt(v3(d1), v3(bc)[:, :, CL - 1:CL].to_broadcast([128, NCK, CL]), v3(bc), ALU.subtract, ["bc"], ["d1"])
                            act(e2[:, :C], d1[:, :C], AF.Exp, ["d1"], ["e2"])
                            tt(ko[:, :C], kk[:, :C], e2[:, :C], ALU.mult, ["kk", "e2"], ["ko"])
                            for si, (kind, s_, chunks) in enumerate(seqs):
                                St, Sk = Sst[si % 2], f"Sst{si % 2}"
                                if kind == "p":
                                    if seg == 0:
                                        mset(St[:], 0.0, [Sk])
                                    else:
                                        ld(St[:], pb_s[j, hd], [f"pb_s{j}:{hd}"], [Sk])
                                else:
                                    ld(St[:], i_b_s[j, s_, hd], [], [Sk])
                                for (c0, cl, ci) in chunks:
                                    pa, pak = ps()
                                    mm(pa[:cl, :cl], kt[:, c0:c0 + cl], qt[:, c0:c0 + cl], True, True, ["kt", "qt"], [pak])
                                    tt(attT[:cl, :cl], pa[:cl, :cl], mmul[:cl, :cl], ALU.mult, [pak, "mmul"], ["attT"])
                                    pb, pbk = ps()
                                    tp(pb[:cl, :128], iv[:, c0:c0 + cl], 128, ["iv"], [pbk])
                                    cp(ivtok[:cl, :], pb[:cl, :128], [pbk], ["ivtok"])
                                    po, pok = ps()
                                    mm(po[:, :cl], St[:, :], qin[:, c0:c0 + cl], True, False, [Sk, "qin"], [pok])
                                    mm(po[:, :cl], ivtok[:cl, :], attT[:cl, :cl], False, True, ["ivtok", "attT"], [pok])
                                    cp(o[:, c0:c0 + cl], po[:, :cl], [pok], ["o"])
                                    pk_, pkk = ps()
                                    tp(pk_[:cl, :128], ko[:, c0:c0 + cl], 128, ["ko"], [pkk])
                                    cp(kotok[:cl, :], pk_[:cl, :128], [pkk], ["kotok"])
                                    pu, puk = ps()
                                    mm(pu[:, :128], kotok[:cl, :], ivtok[:cl, :], True, True, ["kotok", "ivtok"], [puk])
                                    stt(St[:, :], St[:, :], eg[:, ci:ci + 1], pu[:, :128], ALU.mult, ALU.add, [Sk, "eg", puk], [Sk])
                                if kind == "p":
                                    ld(pb_s[j, hd], St[:], [Sk], [f"pb_s{j}:{hd}"])
                                else:
                                    ld(sb_s[j, s_, hd], St[:], [Sk], [f"sb_s{j}:{s_}:{hd}"])
                            act(e2[:, :C], o[:, :C], AF.Square, ["o"], ["e2"])
                            for jj, (cs, cn, is_s) in enumerate(ctl):
                                p_, k_ = ps()
                                mm(p_[:, :cn], ones128[:], e2[:, cs:cs + cn], True, True, ["ones128", "e2"], [k_])
                                act(d1[:, cs:cs + cn], p_[:, :cn], AF.Ln, [k_, "epsc"], ["d1"], bias=epsc[:, 0:1], scale=1.0)
                            act(d1[:, :C], d1[:, :C], AF.Exp, ["d1"], ["d1"], scale=-0.5)
                            tt(o[:, :C], o[:, :C], d1[:, :C], ALU.mult, ["o", "d1"], ["o"])
                            stt(yT[:, 8 + hd, :C], o[:, :C], bnwT[:, j:j + 1], gs[:, :C], ALU.mult, ALU.mult, ["o", "bnwT", "gs"], [yk(8 + hd)])

                def odd_mixer(l, yT, yk):
                    j = l // 2
                    Wd = w_in_odd[j]
                    R0 = 9 * NE + 7 * j

                    def pr(row, n):
                        return prmT[:, n, R0 + row:R0 + row + 1]
                    with scope():
                        wpool = mkpool("wi", 6, 128)
                        wgi = sb("wgi", [128, NCH, 8], BF16)
                        S.dma("pool", lambda e: e.dma_start(out=wgi[:], in_=Wd[:, 3072:3080].rearrange("(c p) n -> p c n", p=128)), reads=[], writes=["wgi"])
                        q = sb("q", [128, CMAX]); k = sb("k", [128, CMAX]); v2 = sb("v2", [128, 2, CMAX]); so = sb("so", [128, 2, CMAX])
                        igr = sb("igr", [1, CMAX]); lfr = sb("lfr", [1, CMAX]); br = sb("br", [1, CMAX]); igb = sb("igb", [1, CMAX])
                        inr = sb("inr", [1, CMAX]); wkr = sb("wkr", [1, CMAX])
                        Mr = sb("Mr", [1, NCK + NS]); gr = sb("gr", [1, NCK + NS]); mr = sb("mr", [1, NCK + NS]); mpv = sb("mpv", [1, NCK + NS])
                        fsr = sb("fsr", [1, NCK + NS]); t1r = sb("t1r", [1, NCK + NS]); fsb = sb("fsb", [128, NCK + NS])
                        Cn = [sb(f"Cn{i}", [128, 257]) for i in range(2)]
                        dm = sb("dm", [64, 64]); Pm = sb("Pm", [64, 64]); sm = sb("sm", [64, 64]); smT = sb("smT", [64, 64])
                        vtok = sb("vtok", [64, 257]); ktok = sb("ktok", [64, 128]); sv = sb("sv", [64, 256]); num = sb("num", [64, 256])
                        junk = sb("junk", [64, 256])
                        cols = sb("cols", [64, 8])
                        icol = sb("icol", [64, 2])
                        mset(vtok[:, 256:257], 1.0, ["vtok"])

                        def r3(t_):
                            return t_[0:1, 0:T].rearrange("p (c t) -> p c t", t=CL)
                        for hc in range(4):
                            wq, wqk = loadw(wpool, Wd, hc * 128, 128, "wi")
                            wk_, wkk = loadw(wpool, Wd, 512 + hc * 128, 128, "wi")
                            proj(wq, wqk, 0, 128, lambda cs, cn: q[:, cs:cs + cn], "q")
                            proj(wk_, wkk, 0, 128, lambda cs, cn: k[:, cs:cs + cn], "k", scale=128.0 ** -0.5)
                            for i in range(2):
                                wv, wvk = loadw(wpool, Wd, 1024 + hc * 256 + i * 128, 128, "wi")
                                proj(wv, wvk, 0, 128, lambda cs, cn, i=i: v2[:, i, cs:cs + cn], "v2")
                                wo_, wok = loadw(wpool, Wd, 2048 + hc * 256 + i * 128, 128, "wi")
                                proj(wo_, wok, 0, 128, lambda cs, cn, i=i: so[:, i, cs:cs + cn], "so", func=AF.Sigmoid)
                            proj(wgi, "wgi", hc, 1, lambda cs, cn: igr[0:1, cs:cs + cn], "igr", func=AF.Identity,
                                 bias=cgb[0:1, 8 * j + hc:8 * j + hc + 1], scale=1.0, extra_r=["cgb"])
                            proj(wgi, "wgi", 4 + hc, 1, lambda cs, cn: lfr[0:1, cs:cs + cn], "lfr", func=AF.Sigmoid,
                                 bias=cgb[0:1, 8 * j + 4 + hc:8 * j + 5 + hc], scale=1.0, extra_r=["cgb"])
                            act(lfr[0:1, :C], lfr[0:1, :C], AF.Ln, ["lfr"], ["lfr"])
                            S.op("dve", lambda e: e.tensor_tensor_scan(out=br[0:1, 0:T], data0=cmask[0:1, 0:T], data1=lfr[0:1, 0:T], initial=0.0,
                                                                     op0=ALU.mult, op1=ALU.add), reads=["cmask", "lfr"], writes=["br"])
                            if has_s:
                                cp(br[0:1, T:T + NS], lfr[0:1, T:T + NS], ["lfr"], ["br"], eng="dve")
                            tt(igb[0:1, :C], igr[0:1, :C], br[0:1, :C], ALU.subtract, ["igr", "br"], ["igb"])
                            S.op("dve", lambda e: e.tensor_reduce(out=Mr[0:1, 0:NCK], in_=r3(igb), axis=AX.X, op=ALU.max), reads=["igb"], writes=["Mr"])
                            cp(gr[0:1, 0:NCK], r3(br)[:, :, CL - 1], ["br"], ["gr"], eng="dve")
                            S.op("dve", lambda e, hc=hc: e.tensor_tensor_scan(out=mr[0:1, 0:NCK], data0=Mr[0:1, 0:NCK], data1=gr[0:1, 0:NCK],
                                                                            initial=mst[0:1, 4 * j + hc:4 * j + hc + 1], op0=ALU.max, op1=ALU.add),
                                 reads=["Mr", "gr", "mst"], writes=["mr"])
                            cp(mpv[0:1, 0:1], mst[0:1, 4 * j + hc:4 * j + hc + 1], ["mst"], ["mpv"], eng="dve")
                            if NCK > 1:
                                cp(mpv[0:1, 1:NCK], mr[0:1, 0:NCK - 1], ["mr"], ["mpv"], eng="dve")
                            cp(mst[0:1, 4 * j + hc:4 * j + hc + 1], mr[0:1, NCK - 1:NCK], ["mr", "mpv"], ["mst"], eng="dve")
                            if has_s:
                                cp(Mr[0:1, NCK:NCK + NS], igb[0:1, T:T + NS], ["igb"], ["Mr"], eng="dve")
                                cp(gr[0:1, NCK:NCK + NS], br[0:1, T:T + NS], ["br"], ["gr"], eng="dve")
                                ld(mpv[0:1, NCK:NCK + NS], i_c_m[j, :, hc:hc + 1].rearrange("s o -> o s"), [], ["mpv"])
                                tt(mr[0:1, NCK:NCK + NS], Mr[0:1, NCK:NCK + NS], mpv[0:1, NCK:NCK + NS], ALU.max, ["Mr", "mpv"], ["mr"])
                                tt(mr[0:1, NCK:NCK + NS], mr[0:1, NCK:NCK + NS], gr[0:1, NCK:NCK + NS], ALU.add, ["mr", "gr"], ["mr"])
                                ld(sc_m[j, :, hc:hc + 1].rearrange("s o -> o s"), mr[0:1, NCK:NCK + NS], ["mr"], [f"sc_m{j}:{hc}"])
                            if last:
                                ld(pc_m[j:j + 1, hc:hc + 1], mr[0:1, NCK - 1:NCK], ["mr"], [f"pc_m{j}:{hc}"])
                            tt(r3(inr), r3(br), mpv[0:1, 0:NCK].unsqueeze(2).to_broadcast([1, NCK, CL]), ALU.add, ["br", "mpv"], ["inr"])
                            tt(t1r[0:1, :NCKA], gr[0:1, :NCKA], mr[0:1, :NCKA], ALU.subtract, ["gr", "mr"], ["t1r"])
                            tt(r3(wkr), r3(igb), t1r[0:1, 0:NCK].unsqueeze(2).to_broadcast([1, NCK, CL]), ALU.add, ["igb", "t1r"], ["wkr"])
                            if has_s:
                                tt(inr[0:1, T:T + NS], br[0:1, T:T + NS], mpv[0:1, NCK:NCK + NS], ALU.add, ["br", "mpv"], ["inr"])
                                tt(wkr[0:1, T:T + NS], igb[0:1, T:T + NS], t1r[0:1, NCK:NCK + NS], ALU.add, ["igb", "t1r"], ["wkr"])
                            act(wkr[0:1, :C], wkr[0:1, :C], AF.Exp, ["wkr"], ["wkr"])
                            tt(fsr[0:1, :NCKA], t1r[0:1, :NCKA], mpv[0:1, :NCKA], ALU.add, ["t1r", "mpv"], ["fsr"])
                            act(fsr[0:1, :NCKA], fsr[0:1, :NCKA], AF.Exp, ["fsr"], ["fsr"])
                            p_, k_ = ps()
                            mm(p_[:, :NCKA], onesf[0:1, :], fsr[0:1, :NCKA], True, True, ["onesf", "fsr"], [k_])
                            cp(fsb[:, :NCKA], p_[:, :NCKA], [k_], ["fsb"])
                            for si, (kind, s_, chunks) in enumerate(seqs):
                                Ct, Ck = Cn[si % 2], f"Cn{si % 2}"
                                if kind == "p":
                                    if seg == 0:
                                        mset(Ct[:], 0.0, [Ck])
                                    else:
                                        ld(Ct[:, 0:256], pc_c[j, hc], [f"pc_c{j}:{hc}"], [Ck])
                                        ld(Ct[:, 256:257], pc_n[j, hc].rearrange("(k o) -> k o", o=1), [f"pc_n{j}:{hc}"], [Ck])
                                else:
                                    ld(Ct[:, 0:256], i_c_c[j, s_, hc], [], [Ck])
                                    ld(Ct[:, 256:257], i_c_n[j, s_, hc].rearrange("(k o) -> k o", o=1), [], [Ck])
                                for (c0, cl, ci) in chunks:
                                    pdm, pdmk = ps()
                                    mm(pdm[:cl, :cl], onesf[0:1, :cl], igb[0:1, c0:c0 + cl], True, False, ["onesf", "igb"], [pdmk])
                                    mm(pdm[:cl, :cl], br[0:1, c0:c0 + cl], onesf[0:1, :cl], False, True, ["br", "onesf"], [pdmk])
                                    tt(dm[:cl, :cl], pdm[:cl, :cl], madd[:cl, :cl], ALU.add, [pdmk, "madd"], ["dm"])
                                    S.op("dve", lambda e, cl=cl: e.tensor_reduce(out=cols[:cl, 0:1], in_=dm[:cl, :cl], axis=AX.X, op=ALU.max), reads=["dm"], writes=["c0"])
                                    pic, pick = ps()
                                    mm(pic[:cl, 0:1], inr[0:1, c0:c0 + cl], onesf[0:1, 0:1], True, True, ["inr", "onesf"], [pick])
                                    mm(pic[:cl, 1:2], wkr[0:1, c0:c0 + cl], onesf[0:1, 0:1], True, True, ["wkr", "onesf"], [pick])
                                    cp(icol[:cl, :], pic[:cl, 0:2], [pick], ["icol"])
                                    tt(cols[:cl, 1:2], cols[:cl, 0:1], icol[:cl, 0:1], ALU.max, ["c0", "icol"], ["c1"])
                                    ts(cols[:cl, 2:3], cols[:cl, 1:2], -1.0, None, ALU.mult, None, ["c1"], ["c2"])
                                    act(cols[:cl, 3:4], icol[:cl, 0:1], AF.Exp, ["icol", "c2"], ["c3"], bias=cols[:cl, 2:3], scale=1.0)
                                    act(cols[:cl, 4:5], cols[:cl, 2:3], AF.Exp, ["c2"], ["c4"])
                                    act(Pm[:cl, :cl], dm[:cl, :cl], AF.Exp, ["dm", "c2"], ["Pm"], bias=cols[:cl, 2:3], scale=1.0)
                                    pqk, pqkk = ps()
                                    mm(pqk[:cl, :cl], q[:, c0:c0 + cl], k[:, c0:c0 + cl], True, True, ["q", "k"], [pqkk])
                                    tt(sm[:cl, :cl], pqk[:cl, :cl], Pm[:cl, :cl], ALU.mult, [pqkk, "Pm"], ["sm"])
                                    S.op("dve", lambda e, cl=cl: e.tensor_reduce(out=cols[:cl, 5:6], in_=sm[:cl, :cl], axis=AX.X, op=ALU.add), reads=["sm"], writes=["c5"])
                                    pst, pstk = ps()
                                    tp(pst[:cl, :cl], sm[:cl, :cl], cl, ["sm"], [pstk])
                                    cp(smT[:cl, :cl], pst[:cl, :cl], [pstk], ["smT"])
                                    pv, pvk = ps()
                                    for i in range(2):
                                        tp(pv[:cl, i * 128:(i + 1) * 128], v2[:, i, c0:c0 + cl], 128, ["v2"], [pvk])
                                    cp(vtok[:cl, 0:256], pv[:cl, 0:256], [pvk], ["vtok"])
                                    pqc, pqck = ps()
                                    mm(pqc[:cl, :257], q[:, c0:c0 + cl], Ct[:, :257], True, True, ["q", Ck], [pqck])
                                    psv, psvk = ps()
                                    mm(psv[:cl, :256], smT[:cl, :cl], vtok[:cl, :256], True, True, ["smT", "vtok"], [psvk])
                                    cp(sv[:cl, :], psv[:cl, :256], [psvk], ["sv"])
                                    stt(num[:cl, :], pqc[:cl, :256], cols[:cl, 3:4], sv[:cl, :], ALU.mult, ALU.add, [pqck, "c3", "sv"], ["num"])
                                    stt(cols[:cl, 6:7], pqc[:cl, 256:257], cols[:cl, 3:4], cols[:cl, 5:6], ALU.mult, ALU.add, [pqck, "c3", "c5"], ["c6"])
                                    act(cols[:cl, 6:7], cols[:cl, 6:7], AF.Abs, ["c6"], ["c6"])
                                    tt(cols[:cl, 6:7], cols[:cl, 6:7], cols[:cl, 4:5], ALU.max, ["c6", "c4"], ["c6"])
                                    S.op("dve", lambda e, cl=cl: e.reciprocal(out=cols[:cl, 6:7], in_=cols[:cl, 6:7]), reads=["c6"], writes=["c6"])
                                    ts(num[:cl, :], num[:cl, :], cols[:cl, 6:7], None, ALU.mult, None, ["num", "c6"], ["num"])
                                    tt(junk[:cl, :], num[:cl, :], num[:cl, :], ALU.mult, ["num"], ["junk"])
                                    S.op("dve", lambda e, cl=cl: e.tensor_reduce(out=cols[:cl, 7:8], in_=junk[:cl, :], axis=AX.X, op=ALU.add), reads=["junk"], writes=["c7"])
                                    act(cols[:cl, 7:8], cols[:cl, 7:8], AF.Ln, ["c7", "epsc"], ["c7"], bias=epsc[:cl, 0:1], scale=1.0 / 256.0)
                                    act(cols[:cl, 7:8], cols[:cl, 7:8], AF.Exp, ["c7"], ["c7"], scale=-0.5)
                                    ts(num[:cl, :], num[:cl, :], cols[:cl, 7:8], None, ALU.mult, None, ["num", "c7"], ["num"])
                                    for i in range(2):
                                        ph, phk = ps()
                                        tp(ph[:, :cl], num[:cl, i * 128:(i + 1) * 128], cl, ["num"], [phk])
                                        stt(yT[:, hc * 2 + i, c0:c0 + cl], ph[:, :cl], cnwT[:, i, j:j + 1], so[:, i, c0:c0 + cl], ALU.mult, ALU.mult,
                                            [phk, "cnwT", "so"], [yk(hc * 2 + i)])
                                    pkt, pktk = ps()
                                    tp(pkt[:cl, :128], k[:, c0:c0 + cl], 128, ["k"], [pktk])
                                    ts(ktok[:cl, :], pkt[:cl, :128], icol[:cl, 1:2], None, ALU.mult, None, [pktk, "icol"], ["ktok"])
                                    pcu, pcuk = ps()
                                    mm(pcu[:, :257], ktok[:cl, :], vtok[:cl, :257], True, True, ["ktok", "vtok"], [pcuk])
                                    stt(Ct[:, :257], Ct[:, :257], fsb[:, ci:ci + 1], pcu[:, :257], ALU.mult, ALU.add, [Ck, "fsb", pcuk], [Ck])
                                if kind == "p":
                                    ld(pc_c[j, hc], Ct[:, 0:256], [Ck], [f"pc_c{j}:{hc}"])
                                    ld(pc_n[j, hc].rearrange("(k o) -> k o", o=1), Ct[:, 256:257], [Ck], [f"pc_n{j}:{hc}"])
                                else:
                                    ld(sc_c[j, s_, hc], Ct[:, 0:256], [Ck], [f"sc_c{j}:{s_}:{hc}"])
                                    ld(sc_n[j, s_, hc].rearrange("(k o) -> k o", o=1), Ct[:, 256:257], [Ck], [f"sc_n{j}:{s_}:{hc}"])

                    with scope():
                        wpool = mkpool("wi", 6, 128)
                        zp = sb("zp", [128, 1 + T]); zss = sb("zss", [128, NS]); tz = sb("tz", [128, CMAX])
                        prevT = sb("prevT", [128, 26, NS]); zdl = sb("zdl", [128, 26, NS])
                        twl = sb("twl", [128, CMAX]); sgl = sb("sgl", [128, CMAX])
                        w2t = sb("w2t", [128, 1024]); g2t = sb("g2t", [128, 1024])
                        rr = sb("rr", [128, CMAX]); kx = sb("kx", [128, CMAX]); vv = sb("vv", [128, CMAX]); aa = sb("aa", [128, CMAX])
                        t1 = sb("t1", [128, CMAX]); t2 = sb("t2", [128, CMAX]); t3 = sb("t3", [128, CMAX])
                        ld(w2t[0:64, :], d_w2[j], [], ["w2t"]); ld(w2t[64:128, :], d_a2[j], [], ["w2t"]); ld(g2t[:], d_g2[j], [], ["g2t"])
                        if has_s:
                            prow_ = sb("prow_", [NS, 3328]); orow_ = sb("orow_", [NS, 3328])
                            ld(prow_[:], i_d_shift[j], [], ["prow_"])
                            for fc in range(26):
                                p_, k_ = ps()
                                tp(p_[:, :NS], prow_[:NS, fc * 128:(fc + 1) * 128], NS, ["prow_"], [k_])
                                cp(prevT[:, fc, :], p_[:, :NS], [k_], ["prevT"])

                        def shifted(fc, dst, dk):
                            wt, wk = loadw(wpool, Wd, 3080 + fc * 128, 128, "wi")
                            proj(wt, wk, 0, 128, lambda cs, cn: (zp[:, 1 + cs:1 + cs + cn] if cs < T else zss[:, :cn]), "zp")
                            cp(zp[:, 0:1], shst[:, j, fc:fc + 1], ["shst"], ["zp"], eng="dve")
                            tt(tz[:, 0:T], zp[:, 0:T], zp[:, 1:1 + T], ALU.subtract, ["zp"], ["tz"])
                            stt(dst[:, 0:T], tz[:, 0:T], muT[:, fc, j:j + 1], zp[:, 1:1 + T], ALU.mult, ALU.add, ["tz", "muT", "zp"], [dk])
                            cp(shst[:, j, fc:fc + 1], zp[:, T:T + 1], ["zp"], ["shst"], eng="dve")
                            if has_s:
                                tt(tz[:, T:T + NS], prevT[:, fc, :], zss[:, :], ALU.subtract, ["prevT", "zp"], ["tz"])
                                stt(dst[:, T:T + NS], tz[:, T:T + NS], muT[:, fc, j:j + 1], zss[:, :], ALU.mult, ALU.add, ["tz", "muT", "zp"], [dk])
                                cp(zdl[:, fc, :], zss[:, :], ["zp"], ["zdl"], eng="dve")
                        shifted(24, twl, "twl")
                        act(twl[0:64, :C], twl[0:64, :C], AF.Tanh, ["twl"], ["twl"])
                        shifted(25, sgl, "sgl")
                        act(sgl[:, :C], sgl[:, :C], AF.Sigmoid, ["sgl"], ["sgl"])
                        for fc in range(8):
                            shifted(fc, rr, "rr"); shifted(8 + fc, kx, "kx"); shifted(16 + fc, vv, "vv")
                            for jj, (cs, cn, is_s) in enumerate(ctl):
                                p_, k_ = ps()
                                mm(p_[:, :cn], w2t[0:64, fc * 128:(fc + 1) * 128], twl[0:64, cs:cs + cn], True, True, ["w2t", "twl"], [k_])
                                act(t1[:, cs:cs + cn], p_[:, :cn], AF.Sigmoid, [k_, "prmT"], ["t1"], bias=pr(0, fc), scale=1.0)
                                p_, k_ = ps()
                                mm(p_[:, :cn], w2t[64:128, fc * 128:(fc + 1) * 128], twl[64:128, cs:cs + cn], True, True, ["w2t", "twl"], [k_])
                                act(aa[:, cs:cs + cn], p_[:, :cn], AF.Sigmoid, [k_, "prmT"], ["aa"], bias=pr(1, fc), scale=1.0)
                                p_, k_ = ps()
                                mm(p_[:, :cn], g2t[:, fc * 128:(fc + 1) * 128], sgl[:, cs:cs + cn], True, True, ["g2t", "sgl"], [k_])
                                cp(t3[:, cs:cs + cn], p_[:, :cn], [k_], ["t3"])
                            ld(rwscr[7, fc, :, :C], t3[:, :C], ["t3"], [f"rw7:{fc}"])
                            act(t1[:, :C], t1[:, :C], AF.Exp, ["t1"], ["t1"], scale=-0.6065306597126334)
                            ld(rwscr[1, fc, :, :C], t1[:, :C], ["t1"], [f"rw1:{fc}"])
                            ts(t2[:, :C], kx[:, :C], pr(2, fc), None, ALU.mult, None, ["kx", "prmT"], ["t2"])
                            tt(t3[:, :C], t2[:, :C], t2[:, :C], ALU.mult, ["t2"], ["t3"])
                            for jj, (cs, cn, is_s) in enumerate(ctl):
                                p_, k_ = ps()
                                mm(p_[:, :cn], blk[:], t3[:, cs:cs + cn], True, True, ["blk", "t3"], [k_])
                                act(t1[:, cs:cs + cn], p_[:, :cn], AF.Sqrt, [k_], ["t1"])
                            ts(t1[:, :C], t1[:, :C], 1e-12, None, ALU.max, None, ["t1"], ["t1"])
                            S.op("dve", lambda e, t1=t1, C=C: e.reciprocal(out=t1[:, :C], in_=t1[:, :C]), reads=["t1"], writes=["t1"])
                            tt(t2[:, :C], t2[:, :C], t1[:, :C], ALU.mult, ["t2", "t1"], ["t2"])
                            ld(rwscr[0, fc, :, :C], t2[:, :C], ["t2"], [f"rw0:{fc}"])
                            tt(t3[:, :C], t2[:, :C], aa[:, :C], ALU.mult, ["t2", "aa"], ["t3"])
                            ld(rwscr[2, fc, :, :C], t3[:, :C], ["t3"], [f"rw2:{fc}"])
                            ts(t1[:, :C], aa[:, :C], -1.0, pr(3, fc), ALU.add, ALU.mult, ["aa", "prmT"], ["t1"])
                            ts(t1[:, :C], t1[:, :C], 1.0, None, ALU.add, None, ["t1"], ["t1"])
                            tt(t1[:, :C], t1[:, :C], kx[:, :C], ALU.mult, ["t1", "kx"], ["t1"])
                            ld(rwscr[3, fc, :, :C], t1[:, :C], ["t1"], [f"rw3:{fc}"])
                            ld(rwscr[4, fc, :, :C], rr[:, :C], ["rr"], [f"rw4:{fc}"])
                            ld(rwscr[5, fc, :, :C], vv[:, :C], ["vv"], [f"rw5:{fc}"])
                            tt(t2[:, :C], t1[:, :C], rr[:, :C], ALU.mult, ["t1", "rr"], ["t2"])
                            ts(t2[:, :C], t2[:, :C], pr(4, fc), None, ALU.mult, None, ["t2", "prmT"], ["t2"])
                            for jj, (cs, cn, is_s) in enumerate(ctl):
                                p_, k_ = ps()
                                mm(p_[:, :cn], blk[:], t2[:, cs:cs + cn], True, True, ["blk", "t2"], [k_])
                                tt(t3[:, cs:cs + cn], p_[:, :cn], vv[:, cs:cs + cn], ALU.mult, [k_, "vv"], ["t3"])
                            ld(rwscr[6, fc, :, :C], t3[:, :C], ["t3"], [f"rw6:{fc}"])
                        if has_s:
                            for fc in range(26):
                                p_, k_ = ps()
                                tp(p_[:NS, :128], zdl[:, fc, :], 128, ["zdl"], [k_])
                                cp(orow_[:NS, fc * 128:(fc + 1) * 128], p_[:NS, :128], [k_], ["orow_"])
                            ld(sd_shift[j], orow_[:], ["orow_"], [f"sd_shift{j}"])
                        if last:
                            oshp = sb("oshp", [26, 128])
                            p_, k_ = ps()
                            tp(p_[:26, :128], shst[:, j, :], 128, ["shst"], [k_])
                            cp(oshp[:26, :], p_[:26, :128], [k_], ["oshp"])
                            ld(pd_shift[j].rearrange("(a b) -> a b", b=128), oshp[:], ["oshp"], [f"pd_shift{j}"])
                    with scope():
                      yv = sb("yv", [128, 8, CMAX])
                      rwkeys = [f"rw{a}:{fc}" for a in range(8) for fc in range(8)]
                      with scope():
                        sel = sb("sel", [128, 64, 128])
                        ld(sel[:].rearrange("p a b -> p (a b)"), c_sel3, [], ["sel"])
                        fmb = [sb(f"fmb{i}", [128, 8, 2, 64]) for i in range(2)]
                        vblk = [sb(f"vblk{i}", [128, 8, 64]) for i in range(2)]
                        tokm = [sb(f"tokm_{a}", [128, 512]) for a in range(5)]
                        Sd = [sb(f"Sd{i}", [128, 512]) for i in range(2)]
                        w1 = sb("w1", [128, 512]); w2_ = sb("w2_", [128, 512]); sa = sb("sa", [128, 8])
                        for i in range(2):
                            mset(fmb[i][:], 0.0, [f"fmb{i}"])

                        def sdram(base):
                            return [base.rearrange("(j h) v k -> h v j k", h=2)[hh] for hh in range(2)]
                        blocks = [("p", b * 64, 64) for b in range(T // 64)]
                        if has_s:
                            blocks.append(("s", T, NS))
                        Sp, Spk = Sd[0], "Sd0"
                        if seg == 0:
                            mset(Sp[:], 0.0, [Spk])
                        else:
                            for hh, v_ in enumerate(sdram(pd_s[j])):
                                ld(Sp[hh * 64:(hh + 1) * 64, :].rearrange("p (j k) -> p j k", k=64), v_, [f"pd_s{j}"], [Spk])
                        for bi, (kind, c0, nt) in enumerate(blocks):
                            vb, vbk = vblk[bi % 2], f"vblk{bi % 2}"
                            for a in range(5):
                                fb, fbk = fmb[a % 2], f"fmb{a % 2}"
                                for dd in range(2):
                                    ld(fb[:, :, dd, :nt], rwscr[a, :, :, c0:c0 + nt].rearrange("c p t -> p c t"), rwkeys, [fbk])
                                for g4 in range(2):
                                    p_, k_ = ps()
                                    for i in range(4):
                                        fc = g4 * 4 + i
                                        tp(p_[:, i * 128:(i + 1) * 128], fb[:, fc, :, :].rearrange("p a b -> p (a b)"), 128, [fbk], [k_])
                                    pv4 = p_[:, :].rearrange("p (i h k) -> p i h k", h=2, k=64)
                                    cp(tokm[a][0:64, g4 * 256:(g4 + 1) * 256].rearrange("p (i k) -> p i k", k=64), pv4[0:64, :, 0, :], [k_], [f"tokm_{a}"])
                                    cp(tokm[a][64:128, g4 * 256:(g4 + 1) * 256].rearrange("p (i k) -> p i k", k=64), pv4[64:128, :, 1, :], [k_], [f"tokm_{a}"], eng="dve")
                            ld(vb[:, :, :nt], rwscr[5, :, :, c0:c0 + nt].rearrange("c p t -> p c t"), rwkeys, [vbk])
                            for ti in range(nt):
                                if kind == "p":
                                    St, Sk = Sp, Spk
                                else:
                                    St, Sk = Sd[1], "Sd1"
                                    for hh, v_ in enumerate(sdram(i_d_s[j, ti])):
                                        ld(St[hh * 64:(hh + 1) * 64, :].rearrange("p (j k) -> p j k", k=64), v_, [], [Sk])
                                col = c0 + ti
                                bcs = []
                                for a in range(5):
                                    p_, k_ = ps()
                                    mm(p_[:, :], sel[:, ti, :], tokm[a][:, :], True, True, [f"tokm_{a}", "sel"], [k_])
                                    bcs.append((p_, k_))
                                (bkk, kkk), (bd, bdk), (bka, bkak), (bk, bkk_), (br_, brk) = bcs
                                w13 = w1[:, :].rearrange("p (j k) -> p j k", k=64)
                                w23 = w2_[:, :].rearrange("p (j k) -> p j k", k=64)
                                tt(w1[:, :], St[:, :], bkk[:, :], ALU.mult, [Sk, kkk], ["w1"])
                                S.op("dve", lambda e: e.tensor_reduce(out=sa[:, :], in_=w13, axis=AX.X, op=ALU.add, negate=True), reads=["w1"], writes=["sa"])
                                tt(St[:, :], St[:, :], bd[:, :], ALU.mult, [Sk, bdk], [Sk])
                                tt(w13, bka[:, :].rearrange("p (j k) -> p j k", k=64), sa[:, :].unsqueeze(2).to_broadcast([128, 8, 64]), ALU.mult, [bkak, "sa"], ["w1"])
                                tt(St[:, :], St[:, :], w1[:, :], ALU.add, [Sk, "w1"], [Sk])
                                tt(w23, bk[:, :].rearrange("p (j k) -> p j k", k=64), vb[:, :, ti:ti + 1].to_broadcast([128, 8, 64]), ALU.mult, [bkk_, vbk], ["w2_"])
                                tt(St[:, :], St[:, :], w2_[:, :], ALU.add, [Sk, "w2_"], [Sk])
                                tt(w1[:, :], St[:, :], br_[:, :], ALU.mult, [Sk, brk], ["w1"])
                                S.op("dve", lambda e, col=col: e.tensor_reduce(out=yv[:, :, col], in_=w13, axis=AX.X, op=ALU.add), reads=["w1"], writes=["yv"])
                                if kind == "s":
                                    for hh, v_ in enumerate(sdram(sd_s[j, ti])):
                                        ld(v_, St[hh * 64:(hh + 1) * 64, :].rearrange("p (j k) -> p j k", k=64), [Sk], [f"sd_s{j}:{ti}"])
                        for hh, v_ in enumerate(sdram(pd_s[j])):
                            ld(v_, Sp[hh * 64:(hh + 1) * 64, :].rearrange("p (j k) -> p j k", k=64), [Spk], [f"pd_s{j}"])
                      with scope():
                        t1 = sb("t1", [128, CMAX]); t2 = sb("t2", [128, CMAX]); t3 = sb("t3", [128, CMAX]); t4 = sb("t4", [128, CMAX])
                        for fc in range(8):
                            ld(t3[:, :C], rwscr[6, fc, :, :C], rwkeys, ["t3"]); ld(t4[:, :C], rwscr[7, fc, :, :C], rwkeys, ["t4"])
                            for jj, (cs, cn, is_s) in enumerate(ctl):
                                p_, k_ = ps()
                                mm(p_[:, :cn], blk[:], yv[:, fc, cs:cs + cn], True, True, ["blk", "yv"], [k_])
                                ts(t1[:, cs:cs + cn], p_[:, :cn], -1.0 / 64.0, None, ALU.mult, None, [k_], ["t1"])
                            tt(t1[:, :C], t1[:, :C], yv[:, fc, :C], ALU.add, ["t1", "yv"], ["t1"])
                            tt(t2[:, :C], t1[:, :C], t1[:, :C], ALU.mult, ["t1"], ["t2"])
                            for jj, (cs, cn, is_s) in enumerate(ctl):
                                p_, k_ = ps()
                                mm(p_[:, :cn], blk[:], t2[:, cs:cs + cn], True, True, ["blk", "t2"], [k_])
                                act(t2[:, cs:cs + cn], p_[:, :cn], AF.Ln, [k_, "gnepsc", "t2"], ["t2"], bias=gnepsc[:, 0:1], scale=1.0 / 64.0)
                            act(t2[:, :C], t2[:, :C], AF.Exp, ["t2"], ["t2"], scale=-0.5)
                            tt(t1[:, :C], t1[:, :C], t2[:, :C], ALU.mult, ["t1", "t2"], ["t1"])
                            ts(t1[:, :C], t1[:, :C], pr(5, fc), pr(6, fc), ALU.mult, ALU.add, ["t1", "prmT"], ["t1"])
                            tt(t1[:, :C], t1[:, :C], t3[:, :C], ALU.add, ["t1", "t3"], ["t1"])
                            tt(yT[:, 8 + fc, :C], t1[:, :C], t4[:, :C], ALU.mult, ["t1", "t4"], [yk(8 + fc)])

                def final_norm():
                  with scope():
                      fnT = sb("fnT", [128, NCH])
                      frow = sb("frow", [2, 1024])
                      ld(frow[:], prm[NR - 2:NR, :], [], ["frow"])
                      for c in range(8):
                          p_, k_ = ps()
                          tp(p_[:, :2], frow[:2, c * 128:(c + 1) * 128], 2, ["frow"], [k_])
                          cp(fnT[:, c:c + 1], p_[:, 0:1], [k_], ["fnT"]); cp(fnT[:, 8 + c:9 + c], p_[:, 1:2], [k_], ["fnT"])
                      sq = [sb(f"sq{i}", [128, 512], BF16) for i in range(2)]
                      rstd = sb("rstd", [128, 512])
                      yf = sb("yf", [128, NCH, 128]); orow = [sb(f"orow{i}", [128, D]) for i in range(2)]
                      bidx = 0
                      for jx, (cs, cn, is_s) in enumerate(ctl):
                          p_, k_ = ps()
                          for c in range(NCH):
                              act(sq[c % 2][:, :cn], XT[0][:, c, cs:cs + cn], AF.Square, [xk(c, jx)], [f"sq{c % 2}"])
                              mm(p_[:, :cn], onesD[:], sq[c % 2][:, :cn], c == 0, c == NCH - 1, [f"sq{c % 2}", "onesD"], [k_])
                          act(rstd[:, :cn], p_[:, :cn], AF.Ln, [k_, "epsc"], ["rstd"], bias=epsc[:, 0:1], scale=1.0)
                          act(rstd[:, :cn], rstd[:, :cn], AF.Exp, ["rstd"], ["rstd"], scale=-0.5)
                          for b0 in range(0, cn, 128):
                              bn = min(128, cn - b0)
                              for c in range(NCH):
                                  stt(yf[:, c, :bn], XT[0][:, c, cs + b0:cs + b0 + bn], fnT[:, c:c + 1], rstd[:, b0:b0 + bn], ALU.mult, ALU.mult,
                                      [xk(c, jx), "fnT", "rstd"], ["yf"])
                              orw, ork = orow[bidx % 2], f"orow{bidx % 2}"
                              bidx += 1
                              for g4 in range(4):
                                  p2, k2 = ps()
                                  for i in range(4):
                                      c = g4 * 4 + i
                                      tp(p2[:bn, i * 128:(i + 1) * 128], yf[:, c, :bn], 128, ["yf"], [k2])
                                  cp(orw[:bn, g4 * 512:(g4 + 1) * 512], p2[:bn, :512], [k2], [ork])
                              if is_s:
                                  ld(y_s[0:bn, :], orw[:bn, :], [ork], ["y_s"])
                              else:
                                  r0 = tok0 + cs + b0
                                  ld(y_p[r0:r0 + bn, :], orw[:bn, :], [ork], [f"y_p{r0}"])


                def load_mod(l):
                    ld(modT[:].rearrange("p a b -> p (a b)"), modscr[l], [f"modscr{l}"], ["modT"])
                    for k_ in (1, 4, 5, 7):
                        ts(modT[:, k_ * 16:(k_ + 1) * 16, :], modT[:, k_ * 16:(k_ + 1) * 16, :], 1.0, None, ALU.add, None, ["modT"], ["modT"])
                    for k_ in (2, 8):
                        ts(modT[:, k_ * 16:(k_ + 1) * 16, :], modT[:, k_ * 16:(k_ + 1) * 16, :], 1.0, 0.5, ALU.add, ALU.mult, ["modT"], ["modT"])

                def spill():
                    ld(xscr[:, :], XT[0][:].rearrange("p a b -> p (a b)"), allx, ["xscr"])

                def reload():
                    ld(XT[0][:].rearrange("p a b -> p (a b)"), xscr[:, :], ["xscr"], allx)

                for l in range(DEPTH):
                    with scope():
                        XT[0] = sb("xT", [128, NCH, CMAX])
                        if l == 0:
                            load_x()
                        else:
                            reload()
                        load_mod(l)
                        adaln(0, 1)
                        ffn(l, 0, 2)
                        adaln(3, 4)
                        spill()
                    with scope():
                        yT = sb("yT", [128, NCH, CMAX], BF16)

                        def yk(m):
                            return f"yT:{m}"
                        if l % 2 == 0:
                            even_mixer(l, yT, yk)
                        else:
                            odd_mixer(l, yT, yk)
                        with scope():
                            XT[0] = sb("xT", [128, NCH, CMAX])
                            reload()
                            out_proj((w_out_even if l % 2 == 0 else w_out_odd)[l // 2], 5, yT, yk)
                            spill()
                    with scope():
                        XT[0] = sb("xT", [128, NCH, CMAX])
                        reload()
                        adaln(6, 7)
                        ffn(l, 1, 8)
                        if l == DEPTH - 1:
                            final_norm()
                        else:
                            spill()

        snap = {s: v for s, v in S.cnt.items() if v > 0}
        S.ops["sp"].append((S._waits("sp", snap), None, None, 0))
        block = es.enter_context(nc.Block())
        S.emit(block)
    return nc, S.nops


def make_inputs(cfg, inp, core):
    NS, NE, NO = cfg.NS, cfg.NE, cfg.NO
    f = lambda a: np.ascontiguousarray(a, dtype=np.float32)
    b = core % inp["x_prompt"].shape[0]
    s0, s1 = core * NS, (core + 1) * NS
    m = {}
    m["xp"] = f(inp["x_prompt"][b])
    m["xs"] = f(inp["x_sample"][s0:s1, 0])
    m["cc"] = f(np.concatenate([inp["c_prompt"][b:b + 1], inp["c_sample"][s0:s1]], axis=0))
    m["i_a_conv"] = f(inp["state_a_conv"][:, s0:s1]); m["i_a_h"] = f(inp["state_a_h"][:, s0:s1])
    m["i_b_s"] = f(inp["state_b_s"][:, s0:s1])
    m["i_c_c"] = f(inp["state_c_c"][:, s0:s1]); m["i_c_n"] = f(inp["state_c_n"][:, s0:s1]); m["i_c_m"] = f(inp["state_c_m"][:, s0:s1])
    m["i_d_shift"] = f(inp["state_d_shift"][:, s0:s1]); m["i_d_s"] = f(inp["state_d_s"][:, s0:s1])
    return m


def shared_inputs(cfg, inp):
    NE, NO, T = cfg.NE, cfg.NO, cfg.T
    f = lambda a: np.ascontiguousarray(a, dtype=np.float32)
    m = {}
    for k in ("w_mod", "b_mod", "ffn_w_gate", "ffn_w_up", "ffn_w_down", "w_in_even", "w_out_even", "w_in_odd", "w_out_odd",
              "a_gate_r_w", "a_gate_i_w", "b_norm_w", "c_norm_w", "d_mu", "d_w2", "d_a2", "d_g2"):
        m[k] = f(inp[k])
    rows = []
    for j in range(NE):
        rows += [inp["a_conv_w"][j, 0], inp["a_conv_w"][j, 1], inp["a_conv_w"][j, 2], inp["a_conv_w"][j, 3], inp["a_conv_b"][j],
                 inp["a_gate_r_b"][j], inp["a_gate_i_b"][j], inp["a_lambda"][j], inp["b_lb_gamma"][j]]
    for j in range(NO):
        rows += [inp["d_w0"][j], inp["d_a0"][j], inp["d_k_k"][j], inp["d_k_a"][j], np.reshape(inp["d_r_k"][j], (1024,)),
                 inp["d_ln_w"][j], inp["d_ln_b"][j]]
    fn = np.reshape(inp["final_norm_w"], (2, 1024))
    rows += [fn[0], fn[1]]
    m["prm"] = f(np.stack(rows, axis=0))
    m["c_gb"] = f(np.concatenate([np.concatenate([inp["c_igate_b"][j], inp["c_fgate_b"][j]]) for j in range(NO)])[None, :])
    m["c_ident"] = np.eye(128, dtype=np.float32)
    m["c_mmul"] = np.triu(np.ones((64, 64), dtype=np.float32))
    m["c_madd"] = np.where(np.tril(np.ones((64, 64), dtype=bool)), 0.0, -1e30).astype(np.float32)
    cm = np.ones((128, T), dtype=np.float32); cm[:, ::CL] = 0.0
    m["c_cmask"] = cm
    blk = np.zeros((128, 128), dtype=np.float32); blk[:64, :64] = 1.0; blk[64:, 64:] = 1.0
    m["c_blk"] = blk
    m["c_sel2"] = np.concatenate([np.eye(64, dtype=np.float32)] * 2, axis=0)
    s3 = np.zeros((128, 64, 128), dtype=np.float32)
    for r in range(128):
        s3[r, r % 64, (r // 64) * 64:(r // 64) * 64 + 64] = 1.0
    m["c_sel3"] = s3.reshape(128, 64 * 128)
    return m


def run(cfg, inp, ncores=8):
    nc, nops = build(cfg)
    sh = shared_inputs(cfg, inp)
    in_maps = []
    for c in range(ncores):
        m = dict(sh)
        m.update(make_inputs(cfg, inp, c))
        in_maps.append(m)
    res = run_bass_kernel_spmd(nc, in_maps, core_ids=list(range(ncores)))
    R = res.results
    B = inp["x_prompt"].shape[0]
    cat_p = lambda k, ax=0: np.stack([R[b][k] for b in range(B)], axis=ax)
    cat_s = lambda k, ax: np.concatenate([R[c][k] for c in range(ncores)], axis=ax)
    outs = (
        cat_p("y_p"), cat_s("y_s", 0)[:, None, :],
        cat_p("pa_conv", 1), cat_p("pa_h", 1), cat_p("pb_s", 1), cat_p("pc_c", 1), cat_p("pc_n", 1), cat_p("pc_m", 1),
        cat_p("pd_shift", 1), cat_p("pd_s", 1),
        cat_s("sa_conv", 1), cat_s("sa_h", 1), cat_s("sb_s", 1), cat_s("sc_c", 1), cat_s("sc_n", 1), cat_s("sc_m", 1),
        cat_s("sd_shift", 1), cat_s("sd_s", 1),
    )
    return tuple(np.ascontiguousarray(o, dtype=np.float32) for o in outs)


def kernel(**inputs):
    inp = {k: np.asarray(v) for k, v in inputs.items()}
    cfg = Cfg()
    return run(cfg, inp)
```

```python
import numpy as np
from contextlib import ExitStack
import concourse.bass as bass
import concourse.mybir as mybir
from concourse.bass_utils import run_bass_kernel_spmd

F32 = mybir.dt.float32
BF16 = mybir.dt.bfloat16
AF = mybir.ActivationFunctionType
ALU = mybir.AluOpType
AX = mybir.AxisListType
NDMA_SEM = 6


class Sched:
    ENGS = ("pe", "dve", "act", "pool", "sp")

    def __init__(self, nc, sems):
        self.nc = nc
        self.sem = sems
        self.ops = {e: [] for e in self.ENGS}
        self.cnt = {k: 0 for k in sems}
        self.seen = {e: {k: 0 for k in sems} for e in self.ENGS}
        self.last_w = {}
        self.readers = {}
        self.dma_rr = {"sp": 0, "pool": 0}
        self.nops = 0

    def _deps(self, reads, writes, skip=None):
        deps = {}

        def add(tok):
            if tok is not None and deps.get(tok[0], 0) < tok[1]:
                deps[tok[0]] = tok[1]
        for k in reads:
            add(self.last_w.get(k))
        for k in writes:
            lw = self.last_w.get(k)
            if not (skip and lw is not None and lw[0] == skip):
                add(lw)
            for r in self.readers.get(k, ()):
                add(r)
        return deps

    def _commit(self, tok, reads, writes):
        for k in reads:
            self.readers.setdefault(k, []).append(tok)
        for k in writes:
            self.last_w[k] = tok
            self.readers[k] = []

    def _waits(self, eng, deps):
        waits = []
        for s, v in deps.items():
            if self.seen[eng][s] < v:
                self.seen[eng][s] = v
                waits.append((s, v))
        return waits

    def op(self, eng, fn, reads=(), writes=(), accum=False):
        self.last_pool_dma = False
        deps = self._deps(reads, writes, skip=(eng if accum else None))
        waits = self._waits(eng, deps)
        self.cnt[eng] += 1
        tok = (eng, self.cnt[eng])
        self.ops[eng].append((waits, fn, eng, 1))
        self._commit(tok, reads, writes)
        self.nops += 1
        return tok

    def dma(self, queue, fn, reads=(), writes=()):
        if queue == "pool":
            if not getattr(self, "last_pool_dma", False):
                self.barrier()
            self.last_pool_dma = True
        else:
            self.last_pool_dma = False
        i = self.dma_rr[queue]
        self.dma_rr[queue] = (i + 1) % NDMA_SEM
        semname = f"d{queue}{i}"
        deps = self._deps(reads, writes)
        if self.cnt[semname] > 0 and deps.get(semname, 0) < self.cnt[semname]:
            deps[semname] = self.cnt[semname]
        waits = self._waits(queue, deps)
        self.cnt[semname] += 16
        tok = (semname, self.cnt[semname])
        self.ops[queue].append((waits, fn, semname, 16))
        self._commit(tok, reads, writes)
        self.nops += 1
        return tok

    def barrier(self):
        snap = dict(self.cnt)
        for e in self.ENGS:
            waits = self._waits(e, {s: v for s, v in snap.items() if v > 0})
            if waits:
                self.ops[e].append((waits, None, None, 0))

    def emit(self, block):
        sem = self.sem
        ops = self.ops

        def run(engobj, lst):
            for waits, fn, semname, inc in lst:
                for s, v in waits:
                    engobj.wait_ge(sem[s], v)
                if fn is not None:
                    fn(engobj).then_inc(sem[semname], inc)

        @block.tensor
        def _(e):
            run(e, ops["pe"])

        @block.vector
        def _(e):
            run(e, ops["dve"])

        @block.scalar
        def _(e):
            run(e, ops["act"])

        @block.gpsimd
        def _(e):
            run(e, ops["pool"])

        @block.sync
        def _(e):
            run(e, ops["sp"])


def sem_names():
    names = ["pe", "dve", "act", "pool"]
    for q in ("sp", "pool"):
        for i in range(NDMA_SEM):
            names.append(f"d{q}{i}")
    return names


class Cfg:
    def __init__(self, SEQ=2048, NSEG=2, DFF=5632, DEPTH=4, NS=16):
        self.D = 2048
        self.SEQ = SEQ
        self.NSEG = NSEG
        self.T = SEQ // NSEG
        self.DFF = DFF
        self.DEPTH = DEPTH
        self.NS = NS
        self.NE = (DEPTH + 1) // 2
        self.NO = DEPTH // 2
        self.PE_ = 6144
        self.PO = 6408
        self.PD = 3328


CL = 64
EPS = 1e-6
EV_ROWS = ["a_conv_w0", "a_conv_w1", "a_conv_w2", "a_conv_w3", "a_conv_b", "a_gate_r_b", "a_gate_i_b", "a_lambda", "b_lb_gamma"]
OD_ROWS = ["d_w0", "d_a0", "d_k_k", "d_k_a", "d_r_k", "d_ln_w", "d_ln_b"]


def build(cfg):
    D, T, NS, DFF, DEPTH, NE, NO = cfg.D, cfg.T, cfg.NS, cfg.DFF, cfg.DEPTH, cfg.NE, cfg.NO
    SEQ, NSEG = cfg.SEQ, cfg.NSEG
    NCH = 16
    CMAX = T + NS
    NCK = T // CL
    NFF = DFF // 128
    NR = 9 * NE + 7 * NO + 2
    nc = bass.Bass("TRN2", target_bir_lowering=False)

    def din(name, shape):
        return nc.dram_tensor(name, list(shape), F32, kind="ExternalInput").ap()

    def dout(name, shape):
        return nc.dram_tensor(name, list(shape), F32, kind="ExternalOutput").ap()

    def dscr(name, shape):
        return nc.dram_tensor(name, list(shape), F32, kind="Internal").ap()

    xp = din("xp", [SEQ, D]); xs = din("xs", [NS, D]); cc = din("cc", [1 + NS, D])
    i_a_conv = din("i_a_conv", [NE, NS, 3, 1024]); i_a_h = din("i_a_h", [NE, NS, 1024])
    i_b_s = din("i_b_s", [NE, NS, 8, 128, 128])
    i_c_c = din("i_c_c", [max(NO, 1), NS, 4, 128, 256]); i_c_n = din("i_c_n", [max(NO, 1), NS, 4, 128])
    i_c_m = din("i_c_m", [max(NO, 1), NS, 4]); i_d_shift = din("i_d_shift", [max(NO, 1), NS, 3328])
    i_d_s = din("i_d_s", [max(NO, 1), NS, 16, 64, 64])
    w_mod = din("w_mod", [DEPTH, D, 9 * D]); b_mod = din("b_mod", [DEPTH, 9 * D])
    w_gate = din("ffn_w_gate", [DEPTH, 2, D, DFF]); w_up = din("ffn_w_up", [DEPTH, 2, D, DFF])
    w_down = din("ffn_w_down", [DEPTH, 2, DFF, D])
    w_in_even = din("w_in_even", [NE, D, 6144]); w_out_even = din("w_out_even", [NE, D, D])
    w_in_odd = din("w_in_odd", [max(NO, 1), D, 6408]); w_out_odd = din("w_out_odd", [max(NO, 1), D, D])
    prm = din("prm", [NR, 1024])
    a_gate_r_w = din("a_gate_r_w", [NE, 8, 128, 128]); a_gate_i_w = din("a_gate_i_w", [NE, 8, 128, 128])
    b_norm_w = din("b_norm_w", [NE, 128]); c_norm_w = din("c_norm_w", [max(NO, 1), 256])
    c_gb = din("c_gb", [1, 8 * max(NO, 1)])
    d_mu = din("d_mu", [max(NO, 1), 3328])
    d_w2 = din("d_w2", [max(NO, 1), 64, 1024]); d_a2 = din("d_a2", [max(NO, 1), 64, 1024])
    d_g2 = din("d_g2", [max(NO, 1), 128, 1024])
    c_ident = din("c_ident", [128, 128]); c_mmul = din("c_mmul", [64, 64]); c_madd = din("c_madd", [64, 64])
    c_cmask = din("c_cmask", [128, T]); c_blk = din("c_blk", [128, 128]); c_sel2 = din("c_sel2", [128, 64]); c_sel3 = din("c_sel3", [128, 64 * 128])

    y_p = dout("y_p", [SEQ, D]); y_s = dout("y_s", [NS, D])
    pa_conv = dout("pa_conv", [NE, 3, 1024]); pa_h = dout("pa_h", [NE, 1024]); pb_s = dout("pb_s", [NE, 8, 128, 128])
    pc_c = dout("pc_c", [max(NO, 1), 4, 128, 256]); pc_n = dout("pc_n", [max(NO, 1), 4, 128]); pc_m = dout("pc_m", [max(NO, 1), 4])
    pd_shift = dout("pd_shift", [max(NO, 1), 3328]); pd_s = dout("pd_s", [max(NO, 1), 16, 64, 64])
    sa_conv = dout("sa_conv", [NE, NS, 3, 1024]); sa_h = dout("sa_h", [NE, NS, 1024]); sb_s = dout("sb_s", [NE, NS, 8, 128, 128])
    sc_c = dout("sc_c", [max(NO, 1), NS, 4, 128, 256]); sc_n = dout("sc_n", [max(NO, 1), NS, 4, 128]); sc_m = dout("sc_m", [max(NO, 1), NS, 4])
    sd_shift = dout("sd_shift", [max(NO, 1), NS, 3328]); sd_s = dout("sd_s", [max(NO, 1), NS, 16, 64, 64])

    modscr = dscr("modscr", [DEPTH, 128, 144 * 17])
    xscr = dscr("xscr", [128, NCH * CMAX])
    rwscr = dscr("rwscr", [8, 8, 128, CMAX])

    with ExitStack() as es:
        sems = {n: es.enter_context(nc.semaphore(n)) for n in sem_names()}
        S = Sched(nc, sems)
        scopes = [es]

        uid = [0]

        def sb(name, shape, dt=F32):
            uid[0] += 1
            return scopes[-1].enter_context(nc.sbuf_tensor(f"{name}_u{uid[0]}", list(shape), dt))

        class scope:
            def __enter__(self_):
                self_.st = ExitStack()
                self_.st.__enter__()
                scopes.append(self_.st)

            def __exit__(self_, *a):
                S.barrier()
                scopes.pop()
                self_.st.__exit__(*a)

        PS = [es.enter_context(nc.psum_tensor(f"ps{i}", [128, 512], F32)) for i in range(8)]
        psrr = [0]

        def ps():
            i = psrr[0]
            psrr[0] = (i + 1) % 8
            return PS[i], f"ps{i}"

        def mm(out, lhsT, rhs, start, stop, r, w):
            S.op("pe", lambda e: e.matmul(out, lhsT=lhsT, rhs=rhs, start=start, stop=stop), reads=r, writes=w, accum=not start)

        def act(out, in_, func, r, w, bias=None, scale=None, accum_out=None):
            kw = {}
            if bias is not None:
                kw["bias"] = bias
            if scale is not None:
                kw["scale"] = scale
            if accum_out is not None:
                kw["accum_out"] = accum_out
            S.op("act", lambda e: e.activation(out=out, in_=in_, func=func, **kw), reads=r, writes=w)

        def tt(out, in0, in1, op, r, w, eng="dve"):
            S.op(eng, lambda e: e.tensor_tensor(out=out, in0=in0, in1=in1, op=op), reads=r, writes=w)

        def ts(out, in0, s1, s2, op0, op1, r, w, eng="dve"):
            if op1 is None:
                S.op(eng, lambda e: e.tensor_scalar(out=out, in0=in0, scalar1=s1, scalar2=None, op0=op0), reads=r, writes=w)
            else:
                S.op(eng, lambda e: e.tensor_scalar(out=out, in0=in0, scalar1=s1, scalar2=s2, op0=op0, op1=op1), reads=r, writes=w)

        def stt(out, in0, scalar, in1, op0, op1, r, w, accum_out=None):
            if accum_out is None:
                S.op("dve", lambda e: e.scalar_tensor_tensor(out=out, in0=in0, scalar=scalar, in1=in1, op0=op0, op1=op1), reads=r, writes=w)
            else:
                S.op("dve", lambda e: e.scalar_tensor_tensor(out=out, in0=in0, scalar=scalar, in1=in1, op0=op0, op1=op1, accum_out=accum_out), reads=r, writes=w)

        def cp(out, in_, r, w, eng="act"):
            if eng == "act":
                S.op("act", lambda e: e.activation(out=out, in_=in_, func=AF.Copy), reads=r, writes=w)
            else:
                S.op(eng, lambda e: e.tensor_copy(out=out, in_=in_), reads=r, writes=w)

        def mset(ap, val, w, eng="dve"):
            S.op(eng, lambda e: e.memset(ap, val), writes=w)

        def ld(out, in_, r, w, q="sp"):
            S.dma(q, lambda e: e.dma_start(out=out, in_=in_, allow_slow_non_contiguous=True), reads=r, writes=w)

        def tp(out, in_, npart, r, w):
            S.op("pe", lambda e: e.transpose(out, in_, ident[:npart, :npart]), reads=list(r) + ["ident"], writes=w)

        ident = sb("ident", [128, 128]); mmul = sb("mmul", [64, 64]); madd = sb("madd", [64, 64])
        cmask = sb("cmask", [128, T]); blk = sb("blk", [128, 128]); sel2 = sb("sel2", [128, 64])
        onesf = sb("onesf", [128, 128]); ones128 = sb("ones128", [128, 128]); onesD = sb("onesD", [128, 128], BF16)
        epsc = sb("epsc", [128, 1]); gnepsc = sb("gnepsc", [128, 1]); onec = sb("onec", [128, 1])
        prmT = sb("prmT", [128, 8, NR])
        omlT = sb("omlT", [128, 8, NE]); nspT = sb("nspT", [128, 8, NE])
        bnwT = sb("bnwT", [128, NE]); cnwT = sb("cnwT", [128, 2, max(NO, 1)]); muT = sb("muT", [128, 26, max(NO, 1)])
        cgb = sb("cgb", [1, 8 * max(NO, 1)])
        convst = sb("convst", [128, NE, 8, 3]); hst = sb("hst", [128, NE, 8]); mst = sb("mst", [1, 4 * max(NO, 1)])
        shst = sb("shst", [128, max(NO, 1), 26])
        modT = sb("modT", [128, 144, 17])
        hT = sb("hT", [128, NCH, CMAX], BF16)

        ld(ident[:], c_ident, [], ["ident"]); ld(mmul[:], c_mmul, [], ["mmul"]); ld(madd[:], c_madd, [], ["madd"])
        ld(cmask[:], c_cmask, [], ["cmask"]); ld(blk[:], c_blk, [], ["blk"]); ld(sel2[:], c_sel2, [], ["sel2"])
        mset(onesf[:], 1.0, ["onesf"]); mset(ones128[:], 1.0 / 128.0, ["ones128"]); mset(onesD[:], 1.0 / D, ["onesD"])
        mset(epsc[:], EPS, ["epsc"]); mset(gnepsc[:], 64e-5, ["gnepsc"]); mset(onec[:], 1.0, ["onec"])
        mset(convst[:], 0.0, ["convst"]); mset(hst[:], 0.0, ["hst"]); mset(mst[:], 0.0, ["mst"]); mset(shst[:], 0.0, ["shst"])
        ld(cgb[:], c_gb, [], ["cgb"])

        with scope():
            prow = sb("prow", [NR, 1024])
            ld(prow[:], prm, [], ["prow"])
            for c in range(8):
                p_, k_ = ps()
                tp(p_[:, :NR], prow[:NR, c * 128:(c + 1) * 128], NR, ["prow"], [k_])
                cp(prmT[:, c, :], p_[:, :NR], [k_], ["prmT"])
            if NO:
                murow = sb("murow", [NO, 3328])
                ld(murow[:], d_mu, [], ["murow"])
                for c in range(26):
                    p_, k_ = ps()
                    tp(p_[:, :NO], murow[:NO, c * 128:(c + 1) * 128], NO, ["murow"], [k_])
                    cp(muT[:, c, :], p_[:, :NO], [k_], ["muT"])
                cnrow = sb("cnrow", [NO, 256])
                ld(cnrow[:], c_norm_w, [], ["cnrow"])
                for c in range(2):
                    p_, k_ = ps()
                    tp(p_[:, :NO], cnrow[:NO, c * 128:(c + 1) * 128], NO, ["cnrow"], [k_])
                    cp(cnwT[:, c, :], p_[:, :NO], [k_], ["cnwT"])
            bnrow = sb("bnrow", [NE, 128])
            ld(bnrow[:], b_norm_w, [], ["bnrow"])
            p_, k_ = ps()
            tp(p_[:, :NE], bnrow[:NE, :], NE, ["bnrow"], [k_])
            cp(bnwT[:, :], p_[:, :NE], [k_], ["bnwT"])
            tmpa = sb("tmpa", [128, 8]); tmpz = sb("tmpz", [128, 8]); tmpe = sb("tmpe", [128, 8, NE])
            for j in range(NE):
                act(tmpa[:], prmT[:, :, 9 * j + 7], AF.Exp, ["prmT"], ["tmpa"], scale=-1.0)
                act(tmpa[:], tmpa[:], AF.Ln, ["tmpa", "onec"], ["tmpa"], bias=onec[:, 0:1], scale=1.0)
                ts(nspT[:, :, j], tmpa[:], -8.0, None, ALU.mult, None, ["tmpa"], ["nspT"])
                act(tmpe[:, :, j], prmT[:, :, 9 * j + 8], AF.Exp, ["prmT"], ["tmpe"])
            cp(tmpz[:], tmpe[:, :, 0], ["tmpe"], ["tmpz"], eng="dve")
            for j in range(1, NE):
                tt(tmpz[:], tmpz[:], tmpe[:, :, j], ALU.add, ["tmpz", "tmpe"], ["tmpz"])
            S.op("dve", lambda e: e.reciprocal(out=tmpz[:], in_=tmpz[:]), reads=["tmpz"], writes=["tmpz"])
            mset(omlT[:, :, 0], 1.0, ["omlT"])
            for j in range(1, NE):
                tt(tmpa[:], tmpe[:, :, j], tmpz[:], ALU.mult, ["tmpe", "tmpz"], ["tmpa"])
                tt(omlT[:, :, j], omlT[:, :, j - 1], tmpa[:], ALU.subtract, ["omlT", "tmpa"], ["omlT"])

        def loadw(pool, wsrc, col0, ncol, tag):
            i = pool["rr"]
            pool["rr"] = (i + 1) % len(pool["t"])
            t_, k_ = pool["t"][i], f"{pool['name']}{i}"
            src = wsrc[:, col0:col0 + ncol].rearrange("(c p) n -> p c n", p=128)
            S.dma("pool", lambda e: e.dma_start(out=t_[:, :, :ncol], in_=src), reads=[], writes=[k_])
            return t_, k_

        def mkpool(name, n, width):
            return {"name": name, "rr": 0, "t": [sb(f"{name}{i}", [128, NCH, width], BF16) for i in range(n)]}

        class WS:
            def __init__(self, pool, specs, n):
                self.pool, self.specs, self.n, self.i, self.q = pool, specs, n, 0, []
                self._issue()

            def _issue(self):
                if self.i < len(self.specs):
                    batch = self.specs[self.i:self.i + self.n]
                    self.i += self.n
                    self.q.append([loadw(self.pool, w_, c_, n_, "ws") for (w_, c_, n_) in batch])

            def next(self):
                tiles = self.q.pop(0)
                self._issue()
                return tiles

        with scope():
            crow = sb("crow", [1 + NS, D]); csT = sb("csT", [128, NCH, 17], BF16)
            brow = sb("brow", [72, 2, 128]); bmT = sb("bmT", [128, 144])
            mo = sb("mo", [128, 144, 17])
            wp = mkpool("wm", 3, 512)
            ld(crow[:], cc, [], ["crow"])
            act(crow[:], crow[:], AF.Silu, ["crow"], ["crow"])
            for c in range(NCH):
                p_, k_ = ps()
                tp(p_[:, :1 + NS], crow[:1 + NS, c * 128:(c + 1) * 128], 1 + NS, ["crow"], [k_])
                cp(csT[:, c, :], p_[:, :1 + NS], [k_], ["csT"])
            for l in range(DEPTH):
                ld(brow[:], b_mod[l].rearrange("(h r p) -> r h p", h=2, p=128), [], ["brow"])
                for h in range(2):
                    p_, k_ = ps()
                    tp(p_[:, :72], brow[:72, h, :], 72, ["brow"], [k_])
                    cp(bmT[:, h * 72:(h + 1) * 72], p_[:, :72], [k_], ["bmT"])
                for g in range(36):
                    wt, wk = loadw(wp, w_mod[l], g * 512, 512, "wm")
                    p_, k_ = ps()
                    for oc in range(4):
                        for c in range(NCH):
                            mm(p_[:, oc * 17:(oc + 1) * 17], wt[:, c, oc * 128:(oc + 1) * 128], csT[:, c, :], c == 0, c == NCH - 1, [wk, "csT"], [k_])
                    tt(mo[:, g * 4:(g + 1) * 4, :], p_[:, :68].rearrange("p (a b) -> p a b", b=17),
                       bmT[:, g * 4:(g + 1) * 4].unsqueeze(2).to_broadcast([128, 4, 17]), ALU.add, [k_, "bmT"], ["mo"])
                ld(modscr[l], mo[:].rearrange("p a b -> p (a b)"), ["mo"], [f"modscr{l}"])

        def col_tiles(C, has_s):
            tl = []
            c0 = 0
            while c0 < T:
                n = min(512, T - c0)
                tl.append((c0, n, False))
                c0 += n
            if has_s:
                tl.append((T, NS, True))
            return tl

        def modv(k, c):
            return modT[:, k * 16 + c, :]

        for seg in range(NSEG):
            has_s = (seg == NSEG - 1)
            last = has_s
            C = T + (NS if has_s else 0)
            ctl = col_tiles(C, has_s)
            tok0 = seg * T

            def xk(c, j):
                return f"xT:{c}:{j}"

            def hk(c, j):
                return f"hT:{c}:{j}"

            allx = [xk(c, j) for c in range(NCH) for j in range(len(ctl))]
            allh = [hk(c, j) for c in range(NCH) for j in range(len(ctl))]

            XT = [None]
            if True:
                def load_x():
                  with scope():
                      xrow = [sb(f"xrow{i}", [128, D]) for i in range(2)]
                      blocks = [(tok0 + b * 128, 128, b * 128, xp) for b in range(T // 128)]
                      if has_s:
                          blocks.append((0, NS, T, xs))
                      for bi, (r0, n, c0, src) in enumerate(blocks):
                          xr = xrow[bi % 2]
                          ld(xr[:n, :], src[r0:r0 + n, :], [], [f"xrow{bi % 2}"])
                          for c in range(NCH):
                              p_, k_ = ps()
                              tp(p_[:, :n], xr[:n, c * 128:(c + 1) * 128], n, [f"xrow{bi % 2}"], [k_])
                              cp(XT[0][:, c, c0:c0 + n], p_[:, :n], [k_], allx, eng=("act" if c % 2 else "dve"))

                def adaln(ksh, ksc, scale_ap=None):
                    with scope():
                        sq = [sb(f"sq{i}", [128, 512], BF16) for i in range(2)]
                        tmp = [sb(f"tmp{i}", [128, 512]) for i in range(2)]
                        rstd = sb("rstd", [128, 512])
                        for j, (cs, cn, is_s) in enumerate(ctl):
                            p_, k_ = ps()
                            for c in range(NCH):
                                act(sq[c % 2][:, :cn], XT[0][:, c, cs:cs + cn], AF.Square, [xk(c, j)], [f"sq{c % 2}"])
                                mm(p_[:, :cn], onesD[:], sq[c % 2][:, :cn], c == 0, c == NCH - 1, [f"sq{c % 2}", "onesD"], [k_])
                            act(rstd[:, :cn], p_[:, :cn], AF.Ln, [k_, "epsc"], ["rstd"], bias=epsc[:, 0:1], scale=1.0)
                            act(rstd[:, :cn], rstd[:, :cn], AF.Exp, ["rstd"], ["rstd"], scale=-0.5)
                            for c in range(NCH):
                                t_ = tmp[c % 2]; tk = f"tmp{c % 2}"
                                tt(t_[:, :cn], XT[0][:, c, cs:cs + cn], rstd[:, :cn], ALU.mult, [xk(c, j), "rstd"], [tk])
                                if not is_s:
                                    ts(hT[:, c, cs:cs + cn], t_[:, :cn], modv(ksc, c)[:, 0:1], modv(ksh, c)[:, 0:1], ALU.mult, ALU.add,
                                       [tk, "modT"], [hk(c, j)], eng="pool")
                                else:
                                    tt(t_[:, :cn], t_[:, :cn], modv(ksc, c)[:, 1:1 + NS], ALU.mult, [tk, "modT"], [tk])
                                    tt(hT[:, c, cs:cs + cn], t_[:, :cn], modv(ksh, c)[:, 1:1 + NS], ALU.add, [tk, "modT"], [hk(c, j)])

                stmp = sb("stmp", [128, NS])

                def resid_add(dc, j, cs, cn, is_s, p_, k_, kg):
                    if not is_s:
                        stt(XT[0][:, dc, cs:cs + cn], p_[:, :cn], modv(kg, dc)[:, 0:1], XT[0][:, dc, cs:cs + cn], ALU.mult, ALU.add,
                            [k_, "modT", xk(dc, j)], [xk(dc, j)])
                    else:
                        tt(stmp[:, :cn], p_[:, :cn], modv(kg, dc)[:, 1:1 + NS], ALU.mult, [k_, "modT"], ["stmp"])
                        tt(XT[0][:, dc, cs:cs + cn], XT[0][:, dc, cs:cs + cn], stmp[:, :cn], ALU.add, ["stmp", xk(dc, j)], [xk(dc, j)])

                def ffn(l, s, kg):
                    G = 2
                    with scope():
                        wgp = mkpool("wg", 2, G * 128); wup = mkpool("wu", 2, G * 128)
                        wdt = [sb(f"wd{i}", [128, G, D], BF16) for i in range(2)]
                        actb = [sb(f"actb{i}", [128, G, CMAX], BF16) for i in range(2)]
                        sg = [sb(f"sg{i}", [128, 512]) for i in range(2)]
                        ngrp = (NFF + G - 1) // G
                        def ffn_loads(gi):
                            f0 = gi * G
                            g_n = min(G, NFF - f0)
                            wg, wgk = loadw(wgp, w_gate[l, s], f0 * 128, g_n * 128, "wg")
                            wu, wuk = loadw(wup, w_up[l, s], f0 * 128, g_n * 128, "wu")
                            wd, wdk = wdt[gi % 2], f"wd{gi % 2}"
                            S.dma("pool", lambda e, wd=wd, f0=f0, g_n=g_n: e.dma_start(
                                out=wd[:, :g_n, :], in_=w_down[l, s, f0 * 128:(f0 + g_n) * 128, :].rearrange("(g p) d -> p g d", p=128)),
                                reads=[], writes=[wdk])
                            return (g_n, wg, wgk, wu, wuk, wd, wdk)
                        pend = ffn_loads(0)
                        for gi in range(ngrp):
                            g_n, wg, wgk, wu, wuk, wd, wdk = pend
                            if gi + 1 < ngrp:
                                pend = ffn_loads(gi + 1)
                            ab, abk = actb[gi % 2], f"actb{gi % 2}"
                            for g in range(g_n):
                                for j, (cs, cn, is_s) in enumerate(ctl):
                                    pg, pgk = ps(); pu, puk = ps()
                                    for c in range(NCH):
                                        mm(pg[:, :cn], wg[:, c, g * 128:(g + 1) * 128], hT[:, c, cs:cs + cn], c == 0, c == NCH - 1, [wgk, hk(c, j)], [pgk])
                                    for c in range(NCH):
                                        mm(pu[:, :cn], wu[:, c, g * 128:(g + 1) * 128], hT[:, c, cs:cs + cn], c == 0, c == NCH - 1, [wuk, hk(c, j)], [puk])
                                    sgi = (g * len(ctl) + j) % 2
                                    act(sg[sgi][:, :cn], pg[:, :cn], AF.Silu, [pgk], [f"sg{sgi}"])
                                    tt(ab[:, g, cs:cs + cn], sg[sgi][:, :cn], pu[:, :cn], ALU.mult, [f"sg{sgi}", puk], [f"{abk}:{g}:{j}"])
                            for dc in range(NCH):
                                for j, (cs, cn, is_s) in enumerate(ctl):
                                    pd, pdk = ps()
                                    for g in range(g_n):
                                        mm(pd[:, :cn], wd[:, g, dc * 128:(dc + 1) * 128], ab[:, g, cs:cs + cn], g == 0, g == g_n - 1,
                                           [wdk, f"{abk}:{g}:{j}"], [pdk])
                                    resid_add(dc, j, cs, cn, is_s, pd, pdk, kg)

                def out_proj(wsrc, kg, yT, yk):
                    with scope():
                        wop = mkpool("wo", 2, 512)
                        for dg in range(4):
                            wt, wk = loadw(wop, wsrc, dg * 512, 512, "wo")
                            for d4 in range(4):
                                dc = dg * 4 + d4
                                for j, (cs, cn, is_s) in enumerate(ctl):
                                    p_, k_ = ps()
                                    for m in range(NCH):
                                        mm(p_[:, :cn], wt[:, m, d4 * 128:(d4 + 1) * 128], yT[:, m, cs:cs + cn], m == 0, m == NCH - 1, [wk, yk(m)], [k_])
                                    resid_add(dc, j, cs, cn, is_s, p_, k_, kg)

                def proj(wt, wk, off, n, dst, dk, func=AF.Copy, bias=None, scale=None, extra_r=()):
                    for j, (cs, cn, is_s) in enumerate(ctl):
                        p_, k_ = ps()
                        for c in range(NCH):
                            mm(p_[:n, :cn], wt[:, c, off:off + n], hT[:, c, cs:cs + cn], c == 0, c == NCH - 1, [wk, hk(c, j)], [k_])
                        act(dst(cs, cn), p_[:n, :cn], func, [k_] + list(extra_r), [dk], bias=bias, scale=scale)

                seqs = [("p", 0, [(c * CL, CL, c) for c in range(NCK)])]
                if has_s:
                    seqs += [("s", s, [(T + s, 1, NCK + s)]) for s in range(NS)]
                NCKA = NCK + (NS if has_s else 0)

                def even_mixer(l, yT, yk):
                    j = l // 2
                    Wd = w_in_even[j]
                    R0 = 9 * j

                    def pr(row, n):
                        return prmT[:, n, R0 + row:R0 + row + 1]
                    with scope():
                        wpool = mkpool("wi", 6, 128)
                        xpad = sb("xpad", [128, 3 + T]); u = sb("u", [128, CMAX]); axs = sb("axs", [128, NS])
                        agt = sb("agt", [128, CMAX]); rg = sb("rg", [128, CMAX]); ig = sb("ig", [128, CMAX])
                        aa = sb("aa", [128, CMAX]); bb = sb("bb", [128, CMAX]); hh = sb("hh", [128, CMAX])
                        wr = [sb(f"wr{i}", [128, 128]) for i in range(2)]; wi_ = [sb(f"wig{i}", [128, 128]) for i in range(2)]
                        cvs = sb("cvs", [128, 8, 3, NS]); h0s = sb("h0s", [128, 8, NS])
                        ocv = sb("ocv", [3, 1024]); ohs = sb("ohs", [8, 128])
                        ocs = sb("ocs", [NS, 3, 1024]); ohss = sb("ohss", [NS, 1024])
                        if has_s:
                            crow_ = sb("crow_", [NS, 3, 1024]); hrow_ = sb("hrow_", [NS, 1024])
                            ld(crow_[:], i_a_conv[j], [], ["crow_"]); ld(hrow_[:], i_a_h[j], [], ["hrow_"])
                            for n in range(8):
                                for k in range(3):
                                    p_, k_ = ps()
                                    tp(p_[:, :NS], crow_[:NS, k, n * 128:(n + 1) * 128], NS, ["crow_"], [k_])
                                    cp(cvs[:, n, k, :], p_[:, :NS], [k_], ["cvs"])
                                p_, k_ = ps()
                                tp(p_[:, :NS], hrow_[:NS, n * 128:(n + 1) * 128], NS, ["hrow_"], [k_])
                                cp(h0s[:, n, :], p_[:, :NS], [k_], ["h0s"])
                        ws = WS(wpool, [sp_ for n in range(8) for sp_ in ((Wd, n * 128, 128), (Wd, 1024 + n * 128, 128))], 2)
                        for n in range(8):
                            (wax, waxk), (wag, wagk) = ws.next()
                            ld(wr[n % 2][:], a_gate_r_w[j, n], [], [f"wr{n % 2}"]); ld(wi_[n % 2][:], a_gate_i_w[j, n], [], [f"wig{n % 2}"])
                            proj(wax, waxk, 0, 128, lambda cs, cn: (xpad[:, 3 + cs:3 + cs + cn] if cs < T else axs[:, :cn]), "xpad")
                            proj(wag, wagk, 0, 128, lambda cs, cn: agt[:, cs:cs + cn], "agt")
                            cp(xpad[:, 0:3], convst[:, j, n, :], ["convst"], ["xpad"], eng="dve")
                            ts(u[:, 0:T], xpad[:, 3:3 + T], pr(3, n), pr(4, n), ALU.mult, ALU.add, ["xpad", "prmT"], ["u"])
                            for k in range(3):
                                stt(u[:, 0:T], xpad[:, k:k + T], pr(k, n), u[:, 0:T], ALU.mult, ALU.add, ["xpad", "prmT", "u"], ["u"])
                            cp(convst[:, j, n, :], xpad[:, T:T + 3], ["xpad"], ["convst"], eng="dve")
                            if has_s:
                                ts(u[:, T:T + NS], axs[:, :], pr(3, n), pr(4, n), ALU.mult, ALU.add, ["xpad", "prmT"], ["u"])
                                for k in range(3):
                                    stt(u[:, T:T + NS], cvs[:, n, k, :], pr(k, n), u[:, T:T + NS], ALU.mult, ALU.add, ["cvs", "prmT", "u"], ["u"])
                            for jj, (cs, cn, is_s) in enumerate(ctl):
                                p_, k_ = ps()
                                mm(p_[:, :cn], wr[n % 2][:], u[:, cs:cs + cn], True, True, [f"wr{n % 2}", "u"], [k_])
                                act(rg[:, cs:cs + cn], p_[:, :cn], AF.Sigmoid, [k_, "prmT"], ["rg"], bias=pr(5, n), scale=1.0)
                                p_, k_ = ps()
                                mm(p_[:, :cn], wi_[n % 2][:], u[:, cs:cs + cn], True, True, [f"wig{n % 2}", "u"], [k_])
                                act(ig[:, cs:cs + cn], p_[:, :cn], AF.Sigmoid, [k_, "prmT"], ["ig"], bias=pr(6, n), scale=1.0)
                            act(aa[:, :C], rg[:, :C], AF.Exp, ["rg", "nspT"], ["aa"], scale=nspT[:, n, j:j + 1])
                            tt(rg[:, :C], aa[:, :C], aa[:, :C], ALU.mult, ["aa"], ["rg"])
                            act(rg[:, :C], rg[:, :C], AF.Sqrt, ["rg", "onec"], ["rg"], bias=onec[:, 0:1], scale=-1.0)
                            tt(bb[:, :C], ig[:, :C], u[:, :C], ALU.mult, ["ig", "u"], ["bb"])
                            tt(bb[:, :C], bb[:, :C], rg[:, :C], ALU.mult, ["bb", "rg"], ["bb"])
                            S.op("dve", lambda e, n=n: e.tensor_tensor_scan(out=hh[:, 0:T], data0=aa[:, 0:T], data1=bb[:, 0:T],
                                                                         initial=hst[:, j, n:n + 1], op0=ALU.mult, op1=ALU.add),
                                 reads=["aa", "bb", "hst"], writes=["hh"])
                            cp(hst[:, j, n:n + 1], hh[:, T - 1:T], ["hh"], ["hst"], eng="dve")
                            if has_s:
                                tt(hh[:, T:T + NS], aa[:, T:T + NS], h0s[:, n, :], ALU.mult, ["aa", "h0s"], ["hh"])
                                tt(hh[:, T:T + NS], hh[:, T:T + NS], bb[:, T:T + NS], ALU.add, ["hh", "bb"], ["hh"])
                            tt(ig[:, :C], agt[:, :C], agt[:, :C], ALU.mult, ["agt"], ["ig"])
                            ts(ig[:, :C], ig[:, :C], 0.044715, 1.0, ALU.mult, ALU.add, ["ig"], ["ig"])
                            tt(ig[:, :C], ig[:, :C], agt[:, :C], ALU.mult, ["ig", "agt"], ["ig"])
                            act(ig[:, :C], ig[:, :C], AF.Sigmoid, ["ig"], ["ig"], scale=1.5957691216057308)
                            tt(ig[:, :C], ig[:, :C], agt[:, :C], ALU.mult, ["ig", "agt"], ["ig"])
                            tt(yT[:, n, :C], ig[:, :C], hh[:, :C], ALU.mult, ["ig", "hh"], [yk(n)])
                            if last:
                                p_, k_ = ps()
                                tp(p_[:3, :128], convst[:, j, n, :], 128, ["convst"], [k_])
                                cp(ocv[:3, n * 128:(n + 1) * 128], p_[:3, :128], [k_], ["ocv"])
                                srcs = [cvs[:, n, 1, :], cvs[:, n, 2, :], axs[:, :]]
                                for k in range(3):
                                    p_, k_ = ps()
                                    tp(p_[:NS, :128], srcs[k], 128, ["cvs", "xpad"], [k_])
                                    cp(ocs[:NS, k, n * 128:(n + 1) * 128], p_[:NS, :128], [k_], ["ocs"])
                                p_, k_ = ps()
                                tp(p_[:NS, :128], hh[:, T:T + NS], 128, ["hh"], [k_])
                                cp(ohss[:NS, n * 128:(n + 1) * 128], p_[:NS, :128], [k_], ["ohss"])
                        if last:
                            p_, k_ = ps()
                            tp(p_[:8, :128], hst[:, j, :], 128, ["hst"], [k_])
                            cp(ohs[:8, :], p_[:8, :128], [k_], ["ohs"])
                            ld(pa_conv[j], ocv[:], ["ocv"], [f"pa_conv{j}"])
                            ld(pa_h[j].rearrange("(a b) -> a b", b=128), ohs[:], ["ohs"], [f"pa_h{j}"])
                            ld(sa_conv[j], ocs[:], ["ocs"], [f"sa_conv{j}"])
                            ld(sa_h[j], ohss[:], ["ohss"], [f"sa_h{j}"])
                    with scope():
                        wpool = mkpool("wi", 8, 128)
                        ws = WS(wpool, [sp_ for hd in range(8) for sp_ in ((Wd, 2048 + hd * 128, 128), (Wd, 3072 + hd * 128, 128),
                                                                          (Wd, 4096 + hd * 128, 128), (Wd, 5120 + hd * 128, 128))], 4)
                        q = sb("q", [128, CMAX]); kk = sb("kk", [128, CMAX]); iv = sb("iv", [128, CMAX]); gs = sb("gs", [128, CMAX])
                        bc = sb("bc", [128, CMAX]); d1 = sb("d1", [128, CMAX]); e2 = sb("e2", [128, CMAX])
                        qin = sb("qin", [128, CMAX]); qt = sb("qt", [128, CMAX]); kt = sb("kt", [128, CMAX]); ko = sb("ko", [128, CMAX])
                        eg = sb("eg", [128, NCK + NS]); o = sb("o", [128, CMAX])
                        Sst = [sb(f"Sst{i}", [128, 128]) for i in range(2)]
                        attT = sb("attT", [64, 64]); ivtok = sb("ivtok", [64, 128]); kotok = sb("kotok", [64, 128])

                        def v3(t_):
                            return t_[:, 0:T].rearrange("p (c t) -> p c t", t=CL)
                        for hd in range(8):
                            (wq, wqk), (wf, wfk), (wv, wvk), (wg_, wgk_) = ws.next()
                            proj(wq, wqk, 0, 128, lambda cs, cn: q[:, cs:cs + cn], "q")
                            proj(wf, wfk, 0, 128, lambda cs, cn: kk[:, cs:cs + cn], "kk", func=AF.Sigmoid, scale=-1.0)
                            proj(wv, wvk, 0, 128, lambda cs, cn: iv[:, cs:cs + cn], "iv")
                            proj(wg_, wgk_, 0, 128, lambda cs, cn: gs[:, cs:cs + cn], "gs", func=AF.Silu)
                            ts(kk[:, :C], kk[:, :C], omlT[:, hd, j:j + 1], None, ALU.mult, None, ["kk", "omlT"], ["kk"])
                            act(d1[:, :C], kk[:, :C], AF.Ln, ["kk", "onec"], ["d1"], bias=onec[:, 0:1], scale=-1.0)
                            S.op("dve", lambda e: e.tensor_tensor_scan(out=bc[:, 0:T], data0=cmask[:, 0:T], data1=d1[:, 0:T], initial=0.0,
                                                                     op0=ALU.mult, op1=ALU.add), reads=["cmask", "d1"], writes=["bc"])
                            if has_s:
                                cp(bc[:, T:T + NS], d1[:, T:T + NS], ["d1"], ["bc"], eng="dve")
                            act(e2[:, :C], bc[:, :C], AF.Exp, ["bc"], ["e2"])
                            tt(qin[:, :C], q[:, :C], e2[:, :C], ALU.mult, ["q", "e2"], ["qin"])
                            act(eg[:, 0:NCK], v3(bc)[:, :, CL - 1], AF.Exp, ["bc"], ["eg"])
                            if has_s:
                                cp(eg[:, NCK:NCK + NS], e2[:, T:T + NS], ["e2"], ["eg"], eng="dve")
                            tt(v3(d1), v3(bc), v3(bc)[:, :, CL // 2 - 1:CL // 2].to_broadcast([128, NCK, CL]), ALU.subtract, ["bc"], ["d1"])
                            if has_s:
                                mset(d1[:, T:T + NS], 0.0, ["d1"])
                            act(e2[:, :C], d1[:, :C], AF.Exp, ["d1"], ["e2"])
                            tt(qt[:, :C], q[:, :C], e2[:, :C], ALU.mult, ["q", "e2"], ["qt"])
                            act(e2[:, :C], d1[:, :C], AF.Exp, ["d1"], ["e2"], scale=-1.0)
                            tt(kt[:, :C], kk[:, :C], e2[:, :C], ALU.mult, ["kk", "e2"], ["kt"])
                            tt(v3(d1), v3(bc)[:, :, CL - 1:CL].to_broadcast([128, NCK, CL]), v3(bc), ALU.subtract, ["bc"], ["d1"])
                            act(e2[:, :C], d1[:, :C], AF.Exp, ["d1"], ["e2"])
                            tt(ko[:, :C], kk[:, :C], e2[:, :C], ALU.mult, ["kk", "e2"], ["ko"])
                            for si, (kind, s_, chunks) in enumerate(seqs):
                                St, Sk = Sst[si % 2], f"Sst{si % 2}"
                                if kind == "p":
                                    if seg == 0:
                                        mset(St[:], 0.0, [Sk])
                                    else:
                                        ld(St[:], pb_s[j, hd], [f"pb_s{j}:{hd}"], [Sk])
                                else:
                                    ld(St[:], i_b_s[j, s_, hd], [], [Sk])
                                for (c0, cl, ci) in chunks:
                                    pa, pak = ps()
                                    mm(pa[:cl, :cl], kt[:, c0:c0 + cl], qt[:, c0:c0 + cl], True, True, ["kt", "qt"], [pak])
                                    tt(attT[:cl, :cl], pa[:cl, :cl], mmul[:cl, :cl], ALU.mult, [pak, "mmul"], ["attT"])
                                    pb, pbk = ps()
                                    tp(pb[:cl, :128], iv[:, c0:c0 + cl], 128, ["iv"], [pbk])
                                    cp(ivtok[:cl, :], pb[:cl, :128], [pbk], ["ivtok"])
                                    po, pok = ps()
                                    mm(po[:, :cl], St[:, :], qin[:, c0:c0 + cl], True, False, [Sk, "qin"], [pok])
                                    mm(po[:, :cl], ivtok[:cl, :], attT[:cl, :cl], False, True, ["ivtok", "attT"], [pok])
                                    cp(o[:, c0:c0 + cl], po[:, :cl], [pok], ["o"])
                                    pk_, pkk = ps()
                                    tp(pk_[:cl, :128], ko[:, c0:c0 + cl], 128, ["ko"], [pkk])
                                    cp(kotok[:cl, :], pk_[:cl, :128], [pkk], ["kotok"])
                                    pu, puk = ps()
                                    mm(pu[:, :128], kotok[:cl, :], ivtok[:cl, :], True, True, ["kotok", "ivtok"], [puk])
                                    stt(St[:, :], St[:, :], eg[:, ci:ci + 1], pu[:, :128], ALU.mult, ALU.add, [Sk, "eg", puk], [Sk])
                                if kind == "p":
                                    ld(pb_s[j, hd], St[:], [Sk], [f"pb_s{j}:{hd}"])
                                else:
                                    ld(sb_s[j, s_, hd], St[:], [Sk], [f"sb_s{j}:{s_}:{hd}"])
                            act(e2[:, :C], o[:, :C], AF.Square, ["o"], ["e2"])
                            for jj, (cs, cn, is_s) in enumerate(ctl):
                                p_, k_ = ps()
                                mm(p_[:, :cn], ones128[:], e2[:, cs:cs + cn], True, True, ["ones128", "e2"], [k_])
                                act(d1[:, cs:cs + cn], p_[:, :cn], AF.Ln, [k_, "epsc"], ["d1"], bias=epsc[:, 0:1], scale=1.0)
                            act(d1[:, :C], d1[:, :C], AF.Exp, ["d1"], ["d1"], scale=-0.5)
                            tt(o[:, :C], o[:, :C], d1[:, :C], ALU.mult, ["o", "d1"], ["o"])
                            stt(yT[:, 8 + hd, :C], o[:, :C], bnwT[:, j:j + 1], gs[:, :C], ALU.mult, ALU.mult, ["o", "bnwT", "gs"], [yk(8 + hd)])

                def odd_mixer(l, yT, yk):
                    j = l // 2
                    Wd = w_in_odd[j]
                    R0 = 9 * NE + 7 * j

                    def pr(row, n):
                        return prmT[:, n, R0 + row:R0 + row + 1]
                    with scope():
                        wpool = mkpool("wi", 12, 128)
                        wgi = sb("wgi", [128, NCH, 8], BF16)
                        S.dma("pool", lambda e: e.dma_start(out=wgi[:], in_=Wd[:, 3072:3080].rearrange("(c p) n -> p c n", p=128)), reads=[], writes=["wgi"])
                        q = sb("q", [128, CMAX]); k = sb("k", [128, CMAX]); v2 = sb("v2", [128, 2, CMAX]); so = sb("so", [128, 2, CMAX])
                        igr = sb("igr", [1, CMAX]); lfr = sb("lfr", [1, CMAX]); br = sb("br", [1, CMAX]); igb = sb("igb", [1, CMAX])
                        inr = sb("inr", [1, CMAX]); wkr = sb("wkr", [1, CMAX])
                        Mr = sb("Mr", [1, NCK + NS]); gr = sb("gr", [1, NCK + NS]); mr = sb("mr", [1, NCK + NS]); mpv = sb("mpv", [1, NCK + NS])
                        fsr = sb("fsr", [1, NCK + NS]); t1r = sb("t1r", [1, NCK + NS]); fsb = sb("fsb", [128, NCK + NS])
                        Cn = [sb(f"Cn{i}", [128, 257]) for i in range(2)]
                        dm = sb("dm", [64, 64]); Pm = sb("Pm", [64, 64]); sm = sb("sm", [64, 64]); smT = sb("smT", [64, 64])
                        vtok = sb("vtok", [64, 257]); ktok = sb("ktok", [64, 128]); sv = sb("sv", [64, 256]); num = sb("num", [64, 256])
                        junk = sb("junk", [64, 256])
                        cols = sb("cols", [64, 8])
                        icol = sb("icol", [64, 2])
                        mset(vtok[:, 256:257], 1.0, ["vtok"])

                        def r3(t_):
                            return t_[0:1, 0:T].rearrange("p (c t) -> p c t", t=CL)
                        ws = WS(wpool, [sp_ for hc in range(4) for sp_ in ((Wd, hc * 128, 128), (Wd, 512 + hc * 128, 128),
                                                                          (Wd, 1024 + hc * 256, 128), (Wd, 2048 + hc * 256, 128),
                                                                          (Wd, 1024 + hc * 256 + 128, 128), (Wd, 2048 + hc * 256 + 128, 128))], 6)
                        for hc in range(4):
                            (wq, wqk), (wk_, wkk), wv0, wo0, wv1, wo1 = ws.next()
                            wvo = [(wv0, wo0), (wv1, wo1)]
                            proj(wq, wqk, 0, 128, lambda cs, cn: q[:, cs:cs + cn], "q")
                            proj(wk_, wkk, 0, 128, lambda cs, cn: k[:, cs:cs + cn], "k", scale=128.0 ** -0.5)
                            for i in range(2):
                                (wv, wvk), (wo_, wok) = wvo[i]
                                proj(wv, wvk, 0, 128, lambda cs, cn, i=i: v2[:, i, cs:cs + cn], "v2")
                                proj(wo_, wok, 0, 128, lambda cs, cn, i=i: so[:, i, cs:cs + cn], "so", func=AF.Sigmoid)
                            proj(wgi, "wgi", hc, 1, lambda cs, cn: igr[0:1, cs:cs + cn], "igr", func=AF.Identity,
                                 bias=cgb[0:1, 8 * j + hc:8 * j + hc + 1], scale=1.0, extra_r=["cgb"])
                            proj(wgi, "wgi", 4 + hc, 1, lambda cs, cn: lfr[0:1, cs:cs + cn], "lfr", func=AF.Sigmoid,
                                 bias=cgb[0:1, 8 * j + 4 + hc:8 * j + 5 + hc], scale=1.0, extra_r=["cgb"])
                            act(lfr[0:1, :C], lfr[0:1, :C], AF.Ln, ["lfr"], ["lfr"])
                            S.op("dve", lambda e: e.tensor_tensor_scan(out=br[0:1, 0:T], data0=cmask[0:1, 0:T], data1=lfr[0:1, 0:T], initial=0.0,
                                                                     op0=ALU.mult, op1=ALU.add), reads=["cmask", "lfr"], writes=["br"])
                            if has_s:
                                cp(br[0:1, T:T + NS], lfr[0:1, T:T + NS], ["lfr"], ["br"], eng="dve")
                            tt(igb[0:1, :C], igr[0:1, :C], br[0:1, :C], ALU.subtract, ["igr", "br"], ["igb"])
                            S.op("dve", lambda e: e.tensor_reduce(out=Mr[0:1, 0:NCK], in_=r3(igb), axis=AX.X, op=ALU.max), reads=["igb"], writes=["Mr"])
                            cp(gr[0:1, 0:NCK], r3(br)[:, :, CL - 1], ["br"], ["gr"], eng="dve")
                            S.op("dve", lambda e, hc=hc: e.tensor_tensor_scan(out=mr[0:1, 0:NCK], data0=Mr[0:1, 0:NCK], data1=gr[0:1, 0:NCK],
                                                                            initial=mst[0:1, 4 * j + hc:4 * j + hc + 1], op0=ALU.max, op1=ALU.add),
                                 reads=["Mr", "gr", "mst"], writes=["mr"])
                            cp(mpv[0:1, 0:1], mst[0:1, 4 * j + hc:4 * j + hc + 1], ["mst"], ["mpv"], eng="dve")
                            if NCK > 1:
                                cp(mpv[0:1, 1:NCK], mr[0:1, 0:NCK - 1], ["mr"], ["mpv"], eng="dve")
                            cp(mst[0:1, 4 * j + hc:4 * j + hc + 1], mr[0:1, NCK - 1:NCK], ["mr", "mpv"], ["mst"], eng="dve")
                            if has_s:
                                cp(Mr[0:1, NCK:NCK + NS], igb[0:1, T:T + NS], ["igb"], ["Mr"], eng="dve")
                                cp(gr[0:1, NCK:NCK + NS], br[0:1, T:T + NS], ["br"], ["gr"], eng="dve")
                                ld(mpv[0:1, NCK:NCK + NS], i_c_m[j, :, hc:hc + 1].rearrange("s o -> o s"), [], ["mpv"])
                                tt(mr[0:1, NCK:NCK + NS], Mr[0:1, NCK:NCK + NS], mpv[0:1, NCK:NCK + NS], ALU.max, ["Mr", "mpv"], ["mr"])
                                tt(mr[0:1, NCK:NCK + NS], mr[0:1, NCK:NCK + NS], gr[0:1, NCK:NCK + NS], ALU.add, ["mr", "gr"], ["mr"])
                                ld(sc_m[j, :, hc:hc + 1].rearrange("s o -> o s"), mr[0:1, NCK:NCK + NS], ["mr"], [f"sc_m{j}:{hc}"])
                            if last:
                                ld(pc_m[j:j + 1, hc:hc + 1], mr[0:1, NCK - 1:NCK], ["mr"], [f"pc_m{j}:{hc}"])
                            tt(r3(inr), r3(br), mpv[0:1, 0:NCK].unsqueeze(2).to_broadcast([1, NCK, CL]), ALU.add, ["br", "mpv"], ["inr"])
                            tt(t1r[0:1, :NCKA], gr[0:1, :NCKA], mr[0:1, :NCKA], ALU.subtract, ["gr", "mr"], ["t1r"])
                            tt(r3(wkr), r3(igb), t1r[0:1, 0:NCK].unsqueeze(2).to_broadcast([1, NCK, CL]), ALU.add, ["igb", "t1r"], ["wkr"])
                            if has_s:
                                tt(inr[0:1, T:T + NS], br[0:1, T:T + NS], mpv[0:1, NCK:NCK + NS], ALU.add, ["br", "mpv"], ["inr"])
                                tt(wkr[0:1, T:T + NS], igb[0:1, T:T + NS], t1r[0:1, NCK:NCK + NS], ALU.add, ["igb", "t1r"], ["wkr"])
                            act(wkr[0:1, :C], wkr[0:1, :C], AF.Exp, ["wkr"], ["wkr"])
                            tt(fsr[0:1, :NCKA], t1r[0:1, :NCKA], mpv[0:1, :NCKA], ALU.add, ["t1r", "mpv"], ["fsr"])
                            act(fsr[0:1, :NCKA], fsr[0:1, :NCKA], AF.Exp, ["fsr"], ["fsr"])
                            p_, k_ = ps()
                            mm(p_[:, :NCKA], onesf[0:1, :], fsr[0:1, :NCKA], True, True, ["onesf", "fsr"], [k_])
                            cp(fsb[:, :NCKA], p_[:, :NCKA], [k_], ["fsb"])
                            for si, (kind, s_, chunks) in enumerate(seqs):
                                Ct, Ck = Cn[si % 2], f"Cn{si % 2}"
                                if kind == "p":
                                    if seg == 0:
                                        mset(Ct[:], 0.0, [Ck])
                                    else:
                                        ld(Ct[:, 0:256], pc_c[j, hc], [f"pc_c{j}:{hc}"], [Ck])
                                        ld(Ct[:, 256:257], pc_n[j, hc].rearrange("(k o) -> k o", o=1), [f"pc_n{j}:{hc}"], [Ck])
                                else:
                                    ld(Ct[:, 0:256], i_c_c[j, s_, hc], [], [Ck])
                                    ld(Ct[:, 256:257], i_c_n[j, s_, hc].rearrange("(k o) -> k o", o=1), [], [Ck])
                                for (c0, cl, ci) in chunks:
                                    pdm, pdmk = ps()
                                    mm(pdm[:cl, :cl], onesf[0:1, :cl], igb[0:1, c0:c0 + cl], True, False, ["onesf", "igb"], [pdmk])
                                    mm(pdm[:cl, :cl], br[0:1, c0:c0 + cl], onesf[0:1, :cl], False, True, ["br", "onesf"], [pdmk])
                                    tt(dm[:cl, :cl], pdm[:cl, :cl], madd[:cl, :cl], ALU.add, [pdmk, "madd"], ["dm"])
                                    S.op("dve", lambda e, cl=cl: e.tensor_reduce(out=cols[:cl, 0:1], in_=dm[:cl, :cl], axis=AX.X, op=ALU.max), reads=["dm"], writes=["c0"])
                                    pic, pick = ps()
                                    mm(pic[:cl, 0:1], inr[0:1, c0:c0 + cl], onesf[0:1, 0:1], True, True, ["inr", "onesf"], [pick])
                                    mm(pic[:cl, 1:2], wkr[0:1, c0:c0 + cl], onesf[0:1, 0:1], True, True, ["wkr", "onesf"], [pick])
                                    cp(icol[:cl, :], pic[:cl, 0:2], [pick], ["icol"])
                                    tt(cols[:cl, 1:2], cols[:cl, 0:1], icol[:cl, 0:1], ALU.max, ["c0", "icol"], ["c1"])
                                    ts(cols[:cl, 2:3], cols[:cl, 1:2], -1.0, None, ALU.mult, None, ["c1"], ["c2"])
                                    act(cols[:cl, 3:4], icol[:cl, 0:1], AF.Exp, ["icol", "c2"], ["c3"], bias=cols[:cl, 2:3], scale=1.0)
                                    act(cols[:cl, 4:5], cols[:cl, 2:3], AF.Exp, ["c2"], ["c4"])
                                    act(Pm[:cl, :cl], dm[:cl, :cl], AF.Exp, ["dm", "c2"], ["Pm"], bias=cols[:cl, 2:3], scale=1.0)
                                    pqk, pqkk = ps()
                                    mm(pqk[:cl, :cl], q[:, c0:c0 + cl], k[:, c0:c0 + cl], True, True, ["q", "k"], [pqkk])
                                    tt(sm[:cl, :cl], pqk[:cl, :cl], Pm[:cl, :cl], ALU.mult, [pqkk, "Pm"], ["sm"])
                                    S.op("dve", lambda e, cl=cl: e.tensor_reduce(out=cols[:cl, 5:6], in_=sm[:cl, :cl], axis=AX.X, op=ALU.add), reads=["sm"], writes=["c5"])
                                    pst, pstk = ps()
                                    tp(pst[:cl, :cl], sm[:cl, :cl], cl, ["sm"], [pstk])
                                    cp(smT[:cl, :cl], pst[:cl, :cl], [pstk], ["smT"])
                                    pv, pvk = ps()
                                    for i in range(2):
                                        tp(pv[:cl, i * 128:(i + 1) * 128], v2[:, i, c0:c0 + cl], 128, ["v2"], [pvk])
                                    cp(vtok[:cl, 0:256], pv[:cl, 0:256], [pvk], ["vtok"])
                                    pqc, pqck = ps()
                                    mm(pqc[:cl, :257], q[:, c0:c0 + cl], Ct[:, :257], True, True, ["q", Ck], [pqck])
                                    psv, psvk = ps()
                                    mm(psv[:cl, :256], smT[:cl, :cl], vtok[:cl, :256], True, True, ["smT", "vtok"], [psvk])
                                    cp(sv[:cl, :], psv[:cl, :256], [psvk], ["sv"])
                                    stt(num[:cl, :], pqc[:cl, :256], cols[:cl, 3:4], sv[:cl, :], ALU.mult, ALU.add, [pqck, "c3", "sv"], ["num"])
                                    stt(cols[:cl, 6:7], pqc[:cl, 256:257], cols[:cl, 3:4], cols[:cl, 5:6], ALU.mult, ALU.add, [pqck, "c3", "c5"], ["c6"])
                                    act(cols[:cl, 6:7], cols[:cl, 6:7], AF.Abs, ["c6"], ["c6"])
                                    tt(cols[:cl, 6:7], cols[:cl, 6:7], cols[:cl, 4:5], ALU.max, ["c6", "c4"], ["c6"])
                                    S.op("dve", lambda e, cl=cl: e.reciprocal(out=cols[:cl, 6:7], in_=cols[:cl, 6:7]), reads=["c6"], writes=["c6"])
                                    ts(num[:cl, :], num[:cl, :], cols[:cl, 6:7], None, ALU.mult, None, ["num", "c6"], ["num"])
                                    tt(junk[:cl, :], num[:cl, :], num[:cl, :], ALU.mult, ["num"], ["junk"])
                                    S.op("dve", lambda e, cl=cl: e.tensor_reduce(out=cols[:cl, 7:8], in_=junk[:cl, :], axis=AX.X, op=ALU.add), reads=["junk"], writes=["c7"])
                                    act(cols[:cl, 7:8], cols[:cl, 7:8], AF.Ln, ["c7", "epsc"], ["c7"], bias=epsc[:cl, 0:1], scale=1.0 / 256.0)
                                    act(cols[:cl, 7:8], cols[:cl, 7:8], AF.Exp, ["c7"], ["c7"], scale=-0.5)
                                    ts(num[:cl, :], num[:cl, :], cols[:cl, 7:8], None, ALU.mult, None, ["num", "c7"], ["num"])
                                    for i in range(2):
                                        ph, phk = ps()
                                        tp(ph[:, :cl], num[:cl, i * 128:(i + 1) * 128], cl, ["num"], [phk])
                                        stt(yT[:, hc * 2 + i, c0:c0 + cl], ph[:, :cl], cnwT[:, i, j:j + 1], so[:, i, c0:c0 + cl], ALU.mult, ALU.mult,
                                            [phk, "cnwT", "so"], [yk(hc * 2 + i)])
                                    pkt, pktk = ps()
                                    tp(pkt[:cl, :128], k[:, c0:c0 + cl], 128, ["k"], [pktk])
                                    ts(ktok[:cl, :], pkt[:cl, :128], icol[:cl, 1:2], None, ALU.mult, None, [pktk, "icol"], ["ktok"])
                                    pcu, pcuk = ps()
                                    mm(pcu[:, :257], ktok[:cl, :], vtok[:cl, :257], True, True, ["ktok", "vtok"], [pcuk])
                                    stt(Ct[:, :257], Ct[:, :257], fsb[:, ci:ci + 1], pcu[:, :257], ALU.mult, ALU.add, [Ck, "fsb", pcuk], [Ck])
                                if kind == "p":
                                    ld(pc_c[j, hc], Ct[:, 0:256], [Ck], [f"pc_c{j}:{hc}"])
                                    ld(pc_n[j, hc].rearrange("(k o) -> k o", o=1), Ct[:, 256:257], [Ck], [f"pc_n{j}:{hc}"])
                                else:
                                    ld(sc_c[j, s_, hc], Ct[:, 0:256], [Ck], [f"sc_c{j}:{s_}:{hc}"])
                                    ld(sc_n[j, s_, hc].rearrange("(k o) -> k o", o=1), Ct[:, 256:257], [Ck], [f"sc_n{j}:{s_}:{hc}"])

                    with scope():
                        wpool = mkpool("wi", 6, 128)
                        zp = sb("zp", [128, 1 + T]); zss = sb("zss", [128, NS]); tz = sb("tz", [128, CMAX])
                        prevT = sb("prevT", [128, 26, NS]); zdl = sb("zdl", [128, 26, NS])
                        twl = sb("twl", [128, CMAX]); sgl = sb("sgl", [128, CMAX])
                        w2t = sb("w2t", [128, 1024]); g2t = sb("g2t", [128, 1024])
                        rr = sb("rr", [128, CMAX]); kx = sb("kx", [128, CMAX]); vv = sb("vv", [128, CMAX]); aa = sb("aa", [128, CMAX])
                        t1 = sb("t1", [128, CMAX]); t2 = sb("t2", [128, CMAX]); t3 = sb("t3", [128, CMAX])
                        ld(w2t[0:64, :], d_w2[j], [], ["w2t"]); ld(w2t[64:128, :], d_a2[j], [], ["w2t"]); ld(g2t[:], d_g2[j], [], ["g2t"])
                        if has_s:
                            prow_ = sb("prow_", [NS, 3328]); orow_ = sb("orow_", [NS, 3328])
                            ld(prow_[:], i_d_shift[j], [], ["prow_"])
                            for fc in range(26):
                                p_, k_ = ps()
                                tp(p_[:, :NS], prow_[:NS, fc * 128:(fc + 1) * 128], NS, ["prow_"], [k_])
                                cp(prevT[:, fc, :], p_[:, :NS], [k_], ["prevT"])

                        def shifted(fc, dst, dk):
                            ((wt, wk),) = wsr.next()
                            proj(wt, wk, 0, 128, lambda cs, cn: (zp[:, 1 + cs:1 + cs + cn] if cs < T else zss[:, :cn]), "zp")
                            cp(zp[:, 0:1], shst[:, j, fc:fc + 1], ["shst"], ["zp"], eng="dve")
                            tt(tz[:, 0:T], zp[:, 0:T], zp[:, 1:1 + T], ALU.subtract, ["zp"], ["tz"])
                            stt(dst[:, 0:T], tz[:, 0:T], muT[:, fc, j:j + 1], zp[:, 1:1 + T], ALU.mult, ALU.add, ["tz", "muT", "zp"], [dk])
                            cp(shst[:, j, fc:fc + 1], zp[:, T:T + 1], ["zp"], ["shst"], eng="dve")
                            if has_s:
                                tt(tz[:, T:T + NS], prevT[:, fc, :], zss[:, :], ALU.subtract, ["prevT", "zp"], ["tz"])
                                stt(dst[:, T:T + NS], tz[:, T:T + NS], muT[:, fc, j:j + 1], zss[:, :], ALU.mult, ALU.add, ["tz", "muT", "zp"], [dk])
                                cp(zdl[:, fc, :], zss[:, :], ["zp"], ["zdl"], eng="dve")
                        order = [24, 25] + [x_ for fc in range(8) for x_ in (fc, 8 + fc, 16 + fc)]
                        wsr = WS(wpool, [(Wd, 3080 + fc_ * 128, 128) for fc_ in order], 1)
                        shifted(24, twl, "twl")
                        act(twl[0:64, :C], twl[0:64, :C], AF.Tanh, ["twl"], ["twl"])
                        shifted(25, sgl, "sgl")
                        act(sgl[:, :C], sgl[:, :C], AF.Sigmoid, ["sgl"], ["sgl"])
                        for fc in range(8):
                            shifted(fc, rr, "rr"); shifted(8 + fc, kx, "kx"); shifted(16 + fc, vv, "vv")
                            for jj, (cs, cn, is_s) in enumerate(ctl):
                                p_, k_ = ps()
                                mm(p_[:, :cn], w2t[0:64, fc * 128:(fc + 1) * 128], twl[0:64, cs:cs + cn], True, True, ["w2t", "twl"], [k_])
                                act(t1[:, cs:cs + cn], p_[:, :cn], AF.Sigmoid, [k_, "prmT"], ["t1"], bias=pr(0, fc), scale=1.0)
                                p_, k_ = ps()
                                mm(p_[:, :cn], w2t[64:128, fc * 128:(fc + 1) * 128], twl[64:128, cs:cs + cn], True, True, ["w2t", "twl"], [k_])
                                act(aa[:, cs:cs + cn], p_[:, :cn], AF.Sigmoid, [k_, "prmT"], ["aa"], bias=pr(1, fc), scale=1.0)
                                p_, k_ = ps()
                                mm(p_[:, :cn], g2t[:, fc * 128:(fc + 1) * 128], sgl[:, cs:cs + cn], True, True, ["g2t", "sgl"], [k_])
                                cp(t3[:, cs:cs + cn], p_[:, :cn], [k_], ["t3"])
                            ld(rwscr[7, fc, :, :C], t3[:, :C], ["t3"], [f"rw7:{fc}"])
                            act(t1[:, :C], t1[:, :C], AF.Exp, ["t1"], ["t1"], scale=-0.6065306597126334)
                            ld(rwscr[1, fc, :, :C], t1[:, :C], ["t1"], [f"rw1:{fc}"])
                            ts(t2[:, :C], kx[:, :C], pr(2, fc), None, ALU.mult, None, ["kx", "prmT"], ["t2"])
                            tt(t3[:, :C], t2[:, :C], t2[:, :C], ALU.mult, ["t2"], ["t3"])
                            for jj, (cs, cn, is_s) in enumerate(ctl):
                                p_, k_ = ps()
                                mm(p_[:, :cn], blk[:], t3[:, cs:cs + cn], True, True, ["blk", "t3"], [k_])
                                act(t1[:, cs:cs + cn], p_[:, :cn], AF.Sqrt, [k_], ["t1"])
                            ts(t1[:, :C], t1[:, :C], 1e-12, None, ALU.max, None, ["t1"], ["t1"])
                            S.op("dve", lambda e, t1=t1, C=C: e.reciprocal(out=t1[:, :C], in_=t1[:, :C]), reads=["t1"], writes=["t1"])
                            tt(t2[:, :C], t2[:, :C], t1[:, :C], ALU.mult, ["t2", "t1"], ["t2"])
                            ld(rwscr[0, fc, :, :C], t2[:, :C], ["t2"], [f"rw0:{fc}"])
                            tt(t3[:, :C], t2[:, :C], aa[:, :C], ALU.mult, ["t2", "aa"], ["t3"])
                            ld(rwscr[2, fc, :, :C], t3[:, :C], ["t3"], [f"rw2:{fc}"])
                            ts(t1[:, :C], aa[:, :C], -1.0, pr(3, fc), ALU.add, ALU.mult, ["aa", "prmT"], ["t1"])
                            ts(t1[:, :C], t1[:, :C], 1.0, None, ALU.add, None, ["t1"], ["t1"])
                            tt(t1[:, :C], t1[:, :C], kx[:, :C], ALU.mult, ["t1", "kx"], ["t1"])
                            ld(rwscr[3, fc, :, :C], t1[:, :C], ["t1"], [f"rw3:{fc}"])
                            ld(rwscr[4, fc, :, :C], rr[:, :C], ["rr"], [f"rw4:{fc}"])
                            ld(rwscr[5, fc, :, :C], vv[:, :C], ["vv"], [f"rw5:{fc}"])
                            tt(t2[:, :C], t1[:, :C], rr[:, :C], ALU.mult, ["t1", "rr"], ["t2"])
                            ts(t2[:, :C], t2[:, :C], pr(4, fc), None, ALU.mult, None, ["t2", "prmT"], ["t2"])
                            for jj, (cs, cn, is_s) in enumerate(ctl):
                                p_, k_ = ps()
                                mm(p_[:, :cn], blk[:], t2[:, cs:cs + cn], True, True, ["blk", "t2"], [k_])
                                tt(t3[:, cs:cs + cn], p_[:, :cn], vv[:, cs:cs + cn], ALU.mult, [k_, "vv"], ["t3"])
                            ld(rwscr[6, fc, :, :C], t3[:, :C], ["t3"], [f"rw6:{fc}"])
                        if has_s:
                            for fc in range(26):
                                p_, k_ = ps()
                                tp(p_[:NS, :128], zdl[:, fc, :], 128, ["zdl"], [k_])
                                cp(orow_[:NS, fc * 128:(fc + 1) * 128], p_[:NS, :128], [k_], ["orow_"])
                            ld(sd_shift[j], orow_[:], ["orow_"], [f"sd_shift{j}"])
                        if last:
                            oshp = sb("oshp", [26, 128])
                            p_, k_ = ps()
                            tp(p_[:26, :128], shst[:, j, :], 128, ["shst"], [k_])
                            cp(oshp[:26, :], p_[:26, :128], [k_], ["oshp"])
                            ld(pd_shift[j].rearrange("(a b) -> a b", b=128), oshp[:], ["oshp"], [f"pd_shift{j}"])
                    with scope():
                      yv = sb("yv", [128, 8, CMAX])
                      rwkeys = [f"rw{a}:{fc}" for a in range(8) for fc in range(8)]
                      with scope():
                        sel = sb("sel", [128, 64, 128])
                        ld(sel[:].rearrange("p a b -> p (a b)"), c_sel3, [], ["sel"])
                        fmb = [sb(f"fmb{i}", [128, 8, 2, 64]) for i in range(2)]
                        vblk = [sb(f"vblk{i}", [128, 8, 64]) for i in range(2)]
                        tokm = [sb(f"tokm_{a}", [128, 512]) for a in range(5)]
                        Sd = [sb(f"Sd{i}", [128, 512]) for i in range(2)]
                        w1 = sb("w1", [128, 512]); w2_ = sb("w2_", [128, 512]); sa = sb("sa", [128, 8])
                        for i in range(2):
                            mset(fmb[i][:], 0.0, [f"fmb{i}"])

                        def sdram(base):
                            return [base.rearrange("(j h) v k -> h v j k", h=2)[hh] for hh in range(2)]
                        blocks = [("p", b * 64, 64) for b in range(T // 64)]
                        if has_s:
                            blocks.append(("s", T, NS))
                        Sp, Spk = Sd[0], "Sd0"
                        if seg == 0:
                            mset(Sp[:], 0.0, [Spk])
                        else:
                            for hh, v_ in enumerate(sdram(pd_s[j])):
                                ld(Sp[hh * 64:(hh + 1) * 64, :].rearrange("p (j k) -> p j k", k=64), v_, [f"pd_s{j}"], [Spk])
                        for bi, (kind, c0, nt) in enumerate(blocks):
                            vb, vbk = vblk[bi % 2], f"vblk{bi % 2}"
                            for a in range(5):
                                fb, fbk = fmb[a % 2], f"fmb{a % 2}"
                                for dd in range(2):
                                    ld(fb[:, :, dd, :nt], rwscr[a, :, :, c0:c0 + nt].rearrange("c p t -> p c t"), rwkeys, [fbk])
                                for g4 in range(2):
                                    p_, k_ = ps()
                                    for i in range(4):
                                        fc = g4 * 4 + i
                                        tp(p_[:, i * 128:(i + 1) * 128], fb[:, fc, :, :].rearrange("p a b -> p (a b)"), 128, [fbk], [k_])
                                    pv4 = p_[:, :].rearrange("p (i h k) -> p i h k", h=2, k=64)
                                    cp(tokm[a][0:64, g4 * 256:(g4 + 1) * 256].rearrange("p (i k) -> p i k", k=64), pv4[0:64, :, 0, :], [k_], [f"tokm_{a}"])
                                    cp(tokm[a][64:128, g4 * 256:(g4 + 1) * 256].rearrange("p (i k) -> p i k", k=64), pv4[64:128, :, 1, :], [k_], [f"tokm_{a}"], eng="dve")
                            ld(vb[:, :, :nt], rwscr[5, :, :, c0:c0 + nt].rearrange("c p t -> p c t"), rwkeys, [vbk])
                            for ti in range(nt):
                                if kind == "p":
                                    St, Sk = Sp, Spk
                                else:
                                    St, Sk = Sd[1], "Sd1"
                                    for hh, v_ in enumerate(sdram(i_d_s[j, ti])):
                                        ld(St[hh * 64:(hh + 1) * 64, :].rearrange("p (j k) -> p j k", k=64), v_, [], [Sk])
                                col = c0 + ti
                                bcs = []
                                for a in range(5):
                                    p_, k_ = ps()
                                    mm(p_[:, :], sel[:, ti, :], tokm[a][:, :], True, True, [f"tokm_{a}", "sel"], [k_])
                                    bcs.append((p_, k_))
                                (bkk, kkk), (bd, bdk), (bka, bkak), (bk, bkk_), (br_, brk) = bcs
                                w13 = w1[:, :].rearrange("p (j k) -> p j k", k=64)
                                w23 = w2_[:, :].rearrange("p (j k) -> p j k", k=64)
                                tt(w1[:, :], St[:, :], bkk[:, :], ALU.mult, [Sk, kkk], ["w1"])
                                S.op("dve", lambda e: e.tensor_reduce(out=sa[:, :], in_=w13, axis=AX.X, op=ALU.add, negate=True), reads=["w1"], writes=["sa"])
                                tt(St[:, :], St[:, :], bd[:, :], ALU.mult, [Sk, bdk], [Sk])
                                tt(w13, bka[:, :].rearrange("p (j k) -> p j k", k=64), sa[:, :].unsqueeze(2).to_broadcast([128, 8, 64]), ALU.mult, [bkak, "sa"], ["w1"])
                                tt(St[:, :], St[:, :], w1[:, :], ALU.add, [Sk, "w1"], [Sk])
                                tt(w23, bk[:, :].rearrange("p (j k) -> p j k", k=64), vb[:, :, ti:ti + 1].to_broadcast([128, 8, 64]), ALU.mult, [bkk_, vbk], ["w2_"])
                                tt(St[:, :], St[:, :], w2_[:, :], ALU.add, [Sk, "w2_"], [Sk])
                                tt(w1[:, :], St[:, :], br_[:, :], ALU.mult, [Sk, brk], ["w1"])
                                S.op("dve", lambda e, col=col: e.tensor_reduce(out=yv[:, :, col], in_=w13, axis=AX.X, op=ALU.add), reads=["w1"], writes=["yv"])
                                if kind == "s":
                                    for hh, v_ in enumerate(sdram(sd_s[j, ti])):
                                        ld(v_, St[hh * 64:(hh + 1) * 64, :].rearrange("p (j k) -> p j k", k=64), [Sk], [f"sd_s{j}:{ti}"])
                        for hh, v_ in enumerate(sdram(pd_s[j])):
                            ld(v_, Sp[hh * 64:(hh + 1) * 64, :].rearrange("p (j k) -> p j k", k=64), [Spk], [f"pd_s{j}"])
                      with scope():
                        t1 = sb("t1", [128, CMAX]); t2 = sb("t2", [128, CMAX]); t3 = sb("t3", [128, CMAX]); t4 = sb("t4", [128, CMAX])
                        for fc in range(8):
                            ld(t3[:, :C], rwscr[6, fc, :, :C], rwkeys, ["t3"]); ld(t4[:, :C], rwscr[7, fc, :, :C], rwkeys, ["t4"])
                            for jj, (cs, cn, is_s) in enumerate(ctl):
                                p_, k_ = ps()
                                mm(p_[:, :cn], blk[:], yv[:, fc, cs:cs + cn], True, True, ["blk", "yv"], [k_])
                                ts(t1[:, cs:cs + cn], p_[:, :cn], -1.0 / 64.0, None, ALU.mult, None, [k_], ["t1"])
                            tt(t1[:, :C], t1[:, :C], yv[:, fc, :C], ALU.add, ["t1", "yv"], ["t1"])
                            tt(t2[:, :C], t1[:, :C], t1[:, :C], ALU.mult, ["t1"], ["t2"])
                            for jj, (cs, cn, is_s) in enumerate(ctl):
                                p_, k_ = ps()
                                mm(p_[:, :cn], blk[:], t2[:, cs:cs + cn], True, True, ["blk", "t2"], [k_])
                                act(t2[:, cs:cs + cn], p_[:, :cn], AF.Ln, [k_, "gnepsc", "t2"], ["t2"], bias=gnepsc[:, 0:1], scale=1.0 / 64.0)
                            act(t2[:, :C], t2[:, :C], AF.Exp, ["t2"], ["t2"], scale=-0.5)
                            tt(t1[:, :C], t1[:, :C], t2[:, :C], ALU.mult, ["t1", "t2"], ["t1"])
                            ts(t1[:, :C], t1[:, :C], pr(5, fc), pr(6, fc), ALU.mult, ALU.add, ["t1", "prmT"], ["t1"])
                            tt(t1[:, :C], t1[:, :C], t3[:, :C], ALU.add, ["t1", "t3"], ["t1"])
                            tt(yT[:, 8 + fc, :C], t1[:, :C], t4[:, :C], ALU.mult, ["t1", "t4"], [yk(8 + fc)])

                def final_norm():
                  with scope():
                      fnT = sb("fnT", [128, NCH])
                      frow = sb("frow", [2, 1024])
                      ld(frow[:], prm[NR - 2:NR, :], [], ["frow"])
                      for c in range(8):
                          p_, k_ = ps()
                          tp(p_[:, :2], frow[:2, c * 128:(c + 1) * 128], 2, ["frow"], [k_])
                          cp(fnT[:, c:c + 1], p_[:, 0:1], [k_], ["fnT"]); cp(fnT[:, 8 + c:9 + c], p_[:, 1:2], [k_], ["fnT"])
                      sq = [sb(f"sq{i}", [128, 512], BF16) for i in range(2)]
                      rstd = sb("rstd", [128, 512])
                      yf = sb("yf", [128, NCH, 128]); orow = [sb(f"orow{i}", [128, D]) for i in range(2)]
                      bidx = 0
                      for jx, (cs, cn, is_s) in enumerate(ctl):
                          p_, k_ = ps()
                          for c in range(NCH):
                              act(sq[c % 2][:, :cn], XT[0][:, c, cs:cs + cn], AF.Square, [xk(c, jx)], [f"sq{c % 2}"])
                              mm(p_[:, :cn], onesD[:], sq[c % 2][:, :cn], c == 0, c == NCH - 1, [f"sq{c % 2}", "onesD"], [k_])
                          act(rstd[:, :cn], p_[:, :cn], AF.Ln, [k_, "epsc"], ["rstd"], bias=epsc[:, 0:1], scale=1.0)
                          act(rstd[:, :cn], rstd[:, :cn], AF.Exp, ["rstd"], ["rstd"], scale=-0.5)
                          for b0 in range(0, cn, 128):
                              bn = min(128, cn - b0)
                              for c in range(NCH):
                                  stt(yf[:, c, :bn], XT[0][:, c, cs + b0:cs + b0 + bn], fnT[:, c:c + 1], rstd[:, b0:b0 + bn], ALU.mult, ALU.mult,
                                      [xk(c, jx), "fnT", "rstd"], ["yf"])
                              orw, ork = orow[bidx % 2], f"orow{bidx % 2}"
                              bidx += 1
                              for g4 in range(4):
                                  p2, k2 = ps()
                                  for i in range(4):
                                      c = g4 * 4 + i
                                      tp(p2[:bn, i * 128:(i + 1) * 128], yf[:, c, :bn], 128, ["yf"], [k2])
                                  cp(orw[:bn, g4 * 512:(g4 + 1) * 512], p2[:bn, :512], [k2], [ork])
                              if is_s:
                                  ld(y_s[0:bn, :], orw[:bn, :], [ork], ["y_s"])
                              else:
                                  r0 = tok0 + cs + b0
                                  ld(y_p[r0:r0 + bn, :], orw[:bn, :], [ork], [f"y_p{r0}"])


                def load_mod(l):
                    ld(modT[:].rearrange("p a b -> p (a b)"), modscr[l], [f"modscr{l}"], ["modT"])
                    for k_ in (1, 4, 5, 7):
                        ts(modT[:, k_ * 16:(k_ + 1) * 16, :], modT[:, k_ * 16:(k_ + 1) * 16, :], 1.0, None, ALU.add, None, ["modT"], ["modT"])
                    for k_ in (2, 8):
                        ts(modT[:, k_ * 16:(k_ + 1) * 16, :], modT[:, k_ * 16:(k_ + 1) * 16, :], 1.0, 0.5, ALU.add, ALU.mult, ["modT"], ["modT"])

                def spill():
                    ld(xscr[:, :], XT[0][:].rearrange("p a b -> p (a b)"), allx, ["xscr"])

                def reload():
                    ld(XT[0][:].rearrange("p a b -> p (a b)"), xscr[:, :], ["xscr"], allx)

                for l in range(DEPTH):
                    with scope():
                        XT[0] = sb("xT", [128, NCH, CMAX])
                        if l == 0:
                            load_x()
                        else:
                            reload()
                        load_mod(l)
                        adaln(0, 1)
                        ffn(l, 0, 2)
                        adaln(3, 4)
                        spill()
                    with scope():
                        yT = sb("yT", [128, NCH, CMAX], BF16)

                        def yk(m):
                            return f"yT:{m}"
                        if l % 2 == 0:
                            even_mixer(l, yT, yk)
                        else:
                            odd_mixer(l, yT, yk)
                        with scope():
                            XT[0] = sb("xT", [128, NCH, CMAX])
                            reload()
                            out_proj((w_out_even if l % 2 == 0 else w_out_odd)[l // 2], 5, yT, yk)
                            spill()
                    with scope():
                        XT[0] = sb("xT", [128, NCH, CMAX])
                        reload()
                        adaln(6, 7)
                        ffn(l, 1, 8)
                        if l == DEPTH - 1:
                            final_norm()
                        else:
                            spill()

        snap = {s: v for s, v in S.cnt.items() if v > 0}
        S.ops["sp"].append((S._waits("sp", snap), None, None, 0))
        block = es.enter_context(nc.Block())
        S.emit(block)
    return nc, S.nops


def make_inputs(cfg, inp, core):
    NS, NE, NO = cfg.NS, cfg.NE, cfg.NO
    f = lambda a: np.ascontiguousarray(a, dtype=np.float32)
    b = core % inp["x_prompt"].shape[0]
    s0, s1 = core * NS, (core + 1) * NS
    m = {}
    m["xp"] = f(inp["x_prompt"][b])
    m["xs"] = f(inp["x_sample"][s0:s1, 0])
    m["cc"] = f(np.concatenate([inp["c_prompt"][b:b + 1], inp["c_sample"][s0:s1]], axis=0))
    m["i_a_conv"] = f(inp["state_a_conv"][:, s0:s1]); m["i_a_h"] = f(inp["state_a_h"][:, s0:s1])
    m["i_b_s"] = f(inp["state_b_s"][:, s0:s1])
    m["i_c_c"] = f(inp["state_c_c"][:, s0:s1]); m["i_c_n"] = f(inp["state_c_n"][:, s0:s1]); m["i_c_m"] = f(inp["state_c_m"][:, s0:s1])
    m["i_d_shift"] = f(inp["state_d_shift"][:, s0:s1]); m["i_d_s"] = f(inp["state_d_s"][:, s0:s1])
    return m


def shared_inputs(cfg, inp):
    NE, NO, T = cfg.NE, cfg.NO, cfg.T
    f = lambda a: np.ascontiguousarray(a, dtype=np.float32)
    m = {}
    for k in ("w_mod", "b_mod", "ffn_w_gate", "ffn_w_up", "ffn_w_down", "w_in_even", "w_out_even", "w_in_odd", "w_out_odd",
              "a_gate_r_w", "a_gate_i_w", "b_norm_w", "c_norm_w", "d_mu", "d_w2", "d_a2", "d_g2"):
        m[k] = f(inp[k])
    rows = []
    for j in range(NE):
        rows += [inp["a_conv_w"][j, 0], inp["a_conv_w"][j, 1], inp["a_conv_w"][j, 2], inp["a_conv_w"][j, 3], inp["a_conv_b"][j],
                 inp["a_gate_r_b"][j], inp["a_gate_i_b"][j], inp["a_lambda"][j], inp["b_lb_gamma"][j]]
    for j in range(NO):
        rows += [inp["d_w0"][j], inp["d_a0"][j], inp["d_k_k"][j], inp["d_k_a"][j], np.reshape(inp["d_r_k"][j], (1024,)),
                 inp["d_ln_w"][j], inp["d_ln_b"][j]]
    fn = np.reshape(inp["final_norm_w"], (2, 1024))
    rows += [fn[0], fn[1]]
    m["prm"] = f(np.stack(rows, axis=0))
    m["c_gb"] = f(np.concatenate([np.concatenate([inp["c_igate_b"][j], inp["c_fgate_b"][j]]) for j in range(NO)])[None, :])
    m["c_ident"] = np.eye(128, dtype=np.float32)
    m["c_mmul"] = np.triu(np.ones((64, 64), dtype=np.float32))
    m["c_madd"] = np.where(np.tril(np.ones((64, 64), dtype=bool)), 0.0, -1e30).astype(np.float32)
    cm = np.ones((128, T), dtype=np.float32); cm[:, ::CL] = 0.0
    m["c_cmask"] = cm
    blk = np.zeros((128, 128), dtype=np.float32); blk[:64, :64] = 1.0; blk[64:, 64:] = 1.0
    m["c_blk"] = blk
    m["c_sel2"] = np.concatenate([np.eye(64, dtype=np.float32)] * 2, axis=0)
    s3 = np.zeros((128, 64, 128), dtype=np.float32)
    for r in range(128):
        s3[r, r % 64, (r // 64) * 64:(r // 64) * 64 + 64] = 1.0
    m["c_sel3"] = s3.reshape(128, 64 * 128)
    return m


def run(cfg, inp, ncores=8):
    nc, nops = build(cfg)
    sh = shared_inputs(cfg, inp)
    in_maps = []
    for c in range(ncores):
        m = dict(sh)
        m.update(make_inputs(cfg, inp, c))
        in_maps.append(m)
    res = run_bass_kernel_spmd(nc, in_maps, core_ids=list(range(ncores)))
    R = res.results
    B = inp["x_prompt"].shape[0]
    cat_p = lambda k, ax=0: np.stack([R[b][k] for b in range(B)], axis=ax)
    cat_s = lambda k, ax: np.concatenate([R[c][k] for c in range(ncores)], axis=ax)
    outs = (
        cat_p("y_p"), cat_s("y_s", 0)[:, None, :],
        cat_p("pa_conv", 1), cat_p("pa_h", 1), cat_p("pb_s", 1), cat_p("pc_c", 1), cat_p("pc_n", 1), cat_p("pc_m", 1),
        cat_p("pd_shift", 1), cat_p("pd_s", 1),
        cat_s("sa_conv", 1), cat_s("sa_h", 1), cat_s("sb_s", 1), cat_s("sc_c", 1), cat_s("sc_n", 1), cat_s("sc_m", 1),
        cat_s("sd_shift", 1), cat_s("sd_s", 1),
    )
    return tuple(np.ascontiguousarray(o, dtype=np.float32) for o in outs)


def kernel(**inputs):
    inp = {k: np.asarray(v) for k, v in inputs.items()}
    cfg = Cfg()
    return run(cfg, inp)
```

```python
import numpy as np
from contextlib import ExitStack
import concourse.bass as bass
import concourse.mybir as mybir
from concourse.bass_utils import run_bass_kernel_spmd

F32 = mybir.dt.float32
BF16 = mybir.dt.bfloat16
AF = mybir.ActivationFunctionType
ALU = mybir.AluOpType
AX = mybir.AxisListType
NDMA_SEM = 6


class Sched:
    ENGS = ("pe", "dve", "act", "pool", "sp")

    def __init__(self, nc, sems):
        self.nc = nc
        self.sem = sems
        self.ops = {e: [] for e in self.ENGS}
        self.cnt = {k: 0 for k in sems}
        self.seen = {e: {k: 0 for k in sems} for e in self.ENGS}
        self.last_w = {}
        self.readers = {}
        self.dma_rr = {"sp": 0, "pool": 0}
        self.nops = 0

    def _deps(self, reads, writes, skip=None):
        deps = {}

        def add(tok):
            if tok is not None and deps.get(tok[0], 0) < tok[1]:
                deps[tok[0]] = tok[1]
        for k in reads:
            add(self.last_w.get(k))
        for k in writes:
            lw = self.last_w.get(k)
            if not (skip and lw is not None and lw[0] == skip):
                add(lw)
            for r in self.readers.get(k, ()):
                add(r)
        return deps

    def _commit(self, tok, reads, writes):
        for k in reads:
            self.readers.setdefault(k, []).append(tok)
        for k in writes:
            self.last_w[k] = tok
            self.readers[k] = []

    def _waits(self, eng, deps):
        waits = []
        for s, v in deps.items():
            if self.seen[eng][s] < v:
                self.seen[eng][s] = v
                waits.append((s, v))
        return waits

    def op(self, eng, fn, reads=(), writes=(), accum=False):
        self.last_pool_dma = False
        deps = self._deps(reads, writes, skip=(eng if accum else None))
        waits = self._waits(eng, deps)
        self.cnt[eng] += 1
        tok = (eng, self.cnt[eng])
        self.ops[eng].append((waits, fn, eng, 1))
        self._commit(tok, reads, writes)
        self.nops += 1
        return tok

    def dma(self, queue, fn, reads=(), writes=()):
        if queue == "pool":
            if not getattr(self, "last_pool_dma", False):
                self.barrier()
            self.last_pool_dma = True
        else:
            self.last_pool_dma = False
        i = self.dma_rr[queue]
        self.dma_rr[queue] = (i + 1) % NDMA_SEM
        semname = f"d{queue}{i}"
        deps = self._deps(reads, writes)
        if self.cnt[semname] > 0 and deps.get(semname, 0) < self.cnt[semname]:
            deps[semname] = self.cnt[semname]
        waits = self._waits(queue, deps)
        self.cnt[semname] += 16
        tok = (semname, self.cnt[semname])
        self.ops[queue].append((waits, fn, semname, 16))
        self._commit(tok, reads, writes)
        self.nops += 1
        return tok

    def barrier(self):
        snap = dict(self.cnt)
        for e in self.ENGS:
            waits = self._waits(e, {s: v for s, v in snap.items() if v > 0})
            if waits:
                self.ops[e].append((waits, None, None, 0))

    def emit(self, block):
        sem = self.sem
        ops = self.ops

        def run(engobj, lst):
            for waits, fn, semname, inc in lst:
                for s, v in waits:
                    engobj.wait_ge(sem[s], v)
                if fn is not None:
                    fn(engobj).then_inc(sem[semname], inc)

        @block.tensor
        def _(e):
            run(e, ops["pe"])

        @block.vector
        def _(e):
            run(e, ops["dve"])

        @block.scalar
        def _(e):
            run(e, ops["act"])

        @block.gpsimd
        def _(e):
            run(e, ops["pool"])

        @block.sync
        def _(e):
            run(e, ops["sp"])


def sem_names():
    names = ["pe", "dve", "act", "pool"]
    for q in ("sp", "pool"):
        for i in range(NDMA_SEM):
            names.append(f"d{q}{i}")
    return names


class Cfg:
    def __init__(self, SEQ=2048, NSEG=2, DFF=5632, DEPTH=4, NS=16):
        self.D = 2048
        self.SEQ = SEQ
        self.NSEG = NSEG
        self.T = SEQ // NSEG
        self.DFF = DFF
        self.DEPTH = DEPTH
        self.NS = NS
        self.NE = (DEPTH + 1) // 2
        self.NO = DEPTH // 2
        self.PE_ = 6144
        self.PO = 6408
        self.PD = 3328


CL = 64
EPS = 1e-6
EV_ROWS = ["a_conv_w0", "a_conv_w1", "a_conv_w2", "a_conv_w3", "a_conv_b", "a_gate_r_b", "a_gate_i_b", "a_lambda", "b_lb_gamma"]
OD_ROWS = ["d_w0", "d_a0", "d_k_k", "d_k_a", "d_r_k", "d_ln_w", "d_ln_b"]


def build(cfg):
    D, T, NS, DFF, DEPTH, NE, NO = cfg.D, cfg.T, cfg.NS, cfg.DFF, cfg.DEPTH, cfg.NE, cfg.NO
    SEQ, NSEG = cfg.SEQ, cfg.NSEG
    NCH = 16
    CMAX = T + NS
    NCK = T // CL
    NFF = DFF // 128
    NR = 9 * NE + 7 * NO + 2
    nc = bass.Bass("TRN2", target_bir_lowering=False)

    def din(name, shape):
        return nc.dram_tensor(name, list(shape), F32, kind="ExternalInput").ap()

    def dout(name, shape):
        return nc.dram_tensor(name, list(shape), F32, kind="ExternalOutput").ap()

    def dscr(name, shape):
        return nc.dram_tensor(name, list(shape), F32, kind="Internal").ap()

    xp = din("xp", [SEQ, D]); xs = din("xs", [NS, D]); cc = din("cc", [1 + NS, D])
    i_a_conv = din("i_a_conv", [NE, NS, 3, 1024]); i_a_h = din("i_a_h", [NE, NS, 1024])
    i_b_s = din("i_b_s", [NE, NS, 8, 128, 128])
    i_c_c = din("i_c_c", [max(NO, 1), NS, 4, 128, 256]); i_c_n = din("i_c_n", [max(NO, 1), NS, 4, 128])
    i_c_m = din("i_c_m", [max(NO, 1), NS, 4]); i_d_shift = din("i_d_shift", [max(NO, 1), NS, 3328])
    i_d_s = din("i_d_s", [max(NO, 1), NS, 16, 64, 64])
    w_mod = din("w_mod", [DEPTH, D, 9 * D]); b_mod = din("b_mod", [DEPTH, 9 * D])
    w_gate = din("ffn_w_gate", [DEPTH, 2, D, DFF]); w_up = din("ffn_w_up", [DEPTH, 2, D, DFF])
    w_down = din("ffn_w_down", [DEPTH, 2, DFF, D])
    w_in_even = din("w_in_even", [NE, D, 6144]); w_out_even = din("w_out_even", [NE, D, D])
    w_in_odd = din("w_in_odd", [max(NO, 1), D, 6408]); w_out_odd = din("w_out_odd", [max(NO, 1), D, D])
    prm = din("prm", [NR, 1024])
    a_gate_r_w = din("a_gate_r_w", [NE, 8, 128, 128]); a_gate_i_w = din("a_gate_i_w", [NE, 8, 128, 128])
    b_norm_w = din("b_norm_w", [NE, 128]); c_norm_w = din("c_norm_w", [max(NO, 1), 256])
    c_gb = din("c_gb", [1, 8 * max(NO, 1)])
    d_mu = din("d_mu", [max(NO, 1), 3328])
    d_w2 = din("d_w2", [max(NO, 1), 64, 1024]); d_a2 = din("d_a2", [max(NO, 1), 64, 1024])
    d_g2 = din("d_g2", [max(NO, 1), 128, 1024])
    c_ident = din("c_ident", [128, 128]); c_mmul = din("c_mmul", [64, 64]); c_madd = din("c_madd", [64, 64])
    c_cmask = din("c_cmask", [128, T]); c_blk = din("c_blk", [128, 128]); c_sel2 = din("c_sel2", [128, 64]); c_sel3 = din("c_sel3", [128, 64 * 128])

    y_p = dout("y_p", [SEQ, D]); y_s = dout("y_s", [NS, D])
    pa_conv = dout("pa_conv", [NE, 3, 1024]); pa_h = dout("pa_h", [NE, 1024]); pb_s = dout("pb_s", [NE, 8, 128, 128])
    pc_c = dout("pc_c", [max(NO, 1), 4, 128, 256]); pc_n = dout("pc_n", [max(NO, 1), 4, 128]); pc_m = dout("pc_m", [max(NO, 1), 4])
    pd_shift = dout("pd_shift", [max(NO, 1), 3328]); pd_s = dout("pd_s", [max(NO, 1), 16, 64, 64])
    sa_conv = dout("sa_conv", [NE, NS, 3, 1024]); sa_h = dout("sa_h", [NE, NS, 1024]); sb_s = dout("sb_s", [NE, NS, 8, 128, 128])
    sc_c = dout("sc_c", [max(NO, 1), NS, 4, 128, 256]); sc_n = dout("sc_n", [max(NO, 1), NS, 4, 128]); sc_m = dout("sc_m", [max(NO, 1), NS, 4])
    sd_shift = dout("sd_shift", [max(NO, 1), NS, 3328]); sd_s = dout("sd_s", [max(NO, 1), NS, 16, 64, 64])

    modscr = dscr("modscr", [DEPTH, 128, 144 * 17])
    xscr = dscr("xscr", [128, NCH * CMAX])
    rwscr = dscr("rwscr", [8, 8, 128, CMAX])

    with ExitStack() as es:
        sems = {n: es.enter_context(nc.semaphore(n)) for n in sem_names()}
        S = Sched(nc, sems)
        scopes = [es]

        uid = [0]

        def sb(name, shape, dt=F32):
            uid[0] += 1
            return scopes[-1].enter_context(nc.sbuf_tensor(f"{name}_u{uid[0]}", list(shape), dt))

        class scope:
            def __enter__(self_):
                self_.st = ExitStack()
                self_.st.__enter__()
                scopes.append(self_.st)

            def __exit__(self_, *a):
                S.barrier()
                scopes.pop()
                self_.st.__exit__(*a)

        PS = [es.enter_context(nc.psum_tensor(f"ps{i}", [128, 512], F32)) for i in range(8)]
        psrr = [0]

        def ps():
            i = psrr[0]
            psrr[0] = (i + 1) % 8
            return PS[i], f"ps{i}"

        def mm(out, lhsT, rhs, start, stop, r, w):
            S.op("pe", lambda e: e.matmul(out, lhsT=lhsT, rhs=rhs, start=start, stop=stop), reads=r, writes=w, accum=not start)

        def act(out, in_, func, r, w, bias=None, scale=None, accum_out=None):
            kw = {}
            if bias is not None:
                kw["bias"] = bias
            if scale is not None:
                kw["scale"] = scale
            if accum_out is not None:
                kw["accum_out"] = accum_out
            S.op("act", lambda e: e.activation(out=out, in_=in_, func=func, **kw), reads=r, writes=w)

        def tt(out, in0, in1, op, r, w, eng="dve"):
            S.op(eng, lambda e: e.tensor_tensor(out=out, in0=in0, in1=in1, op=op), reads=r, writes=w)

        def ts(out, in0, s1, s2, op0, op1, r, w, eng="dve"):
            if op1 is None:
                S.op(eng, lambda e: e.tensor_scalar(out=out, in0=in0, scalar1=s1, scalar2=None, op0=op0), reads=r, writes=w)
            else:
                S.op(eng, lambda e: e.tensor_scalar(out=out, in0=in0, scalar1=s1, scalar2=s2, op0=op0, op1=op1), reads=r, writes=w)

        def stt(out, in0, scalar, in1, op0, op1, r, w, accum_out=None):
            if accum_out is None:
                S.op("dve", lambda e: e.scalar_tensor_tensor(out=out, in0=in0, scalar=scalar, in1=in1, op0=op0, op1=op1), reads=r, writes=w)
            else:
                S.op("dve", lambda e: e.scalar_tensor_tensor(out=out, in0=in0, scalar=scalar, in1=in1, op0=op0, op1=op1, accum_out=accum_out), reads=r, writes=w)

        def cp(out, in_, r, w, eng="act"):
            if eng == "act":
                S.op("act", lambda e: e.activation(out=out, in_=in_, func=AF.Copy), reads=r, writes=w)
            else:
                S.op(eng, lambda e: e.tensor_copy(out=out, in_=in_), reads=r, writes=w)

        def mset(ap, val, w, eng="dve"):
            S.op(eng, lambda e: e.memset(ap, val), writes=w)

        def ld(out, in_, r, w, q="sp"):
            S.dma(q, lambda e: e.dma_start(out=out, in_=in_, allow_slow_non_contiguous=True), reads=r, writes=w)

        def tp(out, in_, npart, r, w):
            S.op("pe", lambda e: e.transpose(out, in_, ident[:npart, :npart]), reads=list(r) + ["ident"], writes=w)

        ident = sb("ident", [128, 128]); mmul = sb("mmul", [64, 64]); madd = sb("madd", [64, 64])
        cmask = sb("cmask", [128, T]); blk = sb("blk", [128, 128]); sel2 = sb("sel2", [128, 64])
        onesf = sb("onesf", [128, 128]); ones128 = sb("ones128", [128, 128]); onesD = sb("onesD", [128, 128], BF16)
        epsc = sb("epsc", [128, 1]); gnepsc = sb("gnepsc", [128, 1]); onec = sb("onec", [128, 1])
        prmT = sb("prmT", [128, 8, NR])
        omlT = sb("omlT", [128, 8, NE]); nspT = sb("nspT", [128, 8, NE])
        bnwT = sb("bnwT", [128, NE]); cnwT = sb("cnwT", [128, 2, max(NO, 1)]); muT = sb("muT", [128, 26, max(NO, 1)])
        cgb = sb("cgb", [1, 8 * max(NO, 1)])
        convst = sb("convst", [128, NE, 8, 3]); hst = sb("hst", [128, NE, 8]); mst = sb("mst", [1, 4 * max(NO, 1)])
        shst = sb("shst", [128, max(NO, 1), 26])
        modT = sb("modT", [128, 144, 17])
        hT = sb("hT", [128, NCH, CMAX], BF16)

        ld(ident[:], c_ident, [], ["ident"]); ld(mmul[:], c_mmul, [], ["mmul"]); ld(madd[:], c_madd, [], ["madd"])
        ld(cmask[:], c_cmask, [], ["cmask"]); ld(blk[:], c_blk, [], ["blk"]); ld(sel2[:], c_sel2, [], ["sel2"])
        mset(onesf[:], 1.0, ["onesf"]); mset(ones128[:], 1.0 / 128.0, ["ones128"]); mset(onesD[:], 1.0 / D, ["onesD"])
        mset(epsc[:], EPS, ["epsc"]); mset(gnepsc[:], 64e-5, ["gnepsc"]); mset(onec[:], 1.0, ["onec"])
        mset(convst[:], 0.0, ["convst"]); mset(hst[:], 0.0, ["hst"]); mset(mst[:], 0.0, ["mst"]); mset(shst[:], 0.0, ["shst"])
        ld(cgb[:], c_gb, [], ["cgb"])

        with scope():
            prow = sb("prow", [NR, 1024])
            ld(prow[:], prm, [], ["prow"])
            for c in range(8):
                p_, k_ = ps()
                tp(p_[:, :NR], prow[:NR, c * 128:(c + 1) * 128], NR, ["prow"], [k_])
                cp(prmT[:, c, :], p_[:, :NR], [k_], ["prmT"])
            if NO:
                murow = sb("murow", [NO, 3328])
                ld(murow[:], d_mu, [], ["murow"])
                for c in range(26):
                    p_, k_ = ps()
                    tp(p_[:, :NO], murow[:NO, c * 128:(c + 1) * 128], NO, ["murow"], [k_])
                    cp(muT[:, c, :], p_[:, :NO], [k_], ["muT"])
                cnrow = sb("cnrow", [NO, 256])
                ld(cnrow[:], c_norm_w, [], ["cnrow"])
                for c in range(2):
                    p_, k_ = ps()
                    tp(p_[:, :NO], cnrow[:NO, c * 128:(c + 1) * 128], NO, ["cnrow"], [k_])
                    cp(cnwT[:, c, :], p_[:, :NO], [k_], ["cnwT"])
            bnrow = sb("bnrow", [NE, 128])
            ld(bnrow[:], b_norm_w, [], ["bnrow"])
            p_, k_ = ps()
            tp(p_[:, :NE], bnrow[:NE, :], NE, ["bnrow"], [k_])
            cp(bnwT[:, :], p_[:, :NE], [k_], ["bnwT"])
            tmpa = sb("tmpa", [128, 8]); tmpz = sb("tmpz", [128, 8]); tmpe = sb("tmpe", [128, 8, NE])
            for j in range(NE):
                act(tmpa[:], prmT[:, :, 9 * j + 7], AF.Exp, ["prmT"], ["tmpa"], scale=-1.0)
                act(tmpa[:], tmpa[:], AF.Ln, ["tmpa", "onec"], ["tmpa"], bias=onec[:, 0:1], scale=1.0)
                ts(nspT[:, :, j], tmpa[:], -8.0, None, ALU.mult, None, ["tmpa"], ["nspT"])
                act(tmpe[:, :, j], prmT[:, :, 9 * j + 8], AF.Exp, ["prmT"], ["tmpe"])
            cp(tmpz[:], tmpe[:, :, 0], ["tmpe"], ["tmpz"], eng="dve")
            for j in range(1, NE):
                tt(tmpz[:], tmpz[:], tmpe[:, :, j], ALU.add, ["tmpz", "tmpe"], ["tmpz"])
            S.op("dve", lambda e: e.reciprocal(out=tmpz[:], in_=tmpz[:]), reads=["tmpz"], writes=["tmpz"])
            mset(omlT[:, :, 0], 1.0, ["omlT"])
            for j in range(1, NE):
                tt(tmpa[:], tmpe[:, :, j], tmpz[:], ALU.mult, ["tmpe", "tmpz"], ["tmpa"])
                tt(omlT[:, :, j], omlT[:, :, j - 1], tmpa[:], ALU.subtract, ["omlT", "tmpa"], ["omlT"])

        def loadw(pool, wsrc, col0, ncol, tag):
            i = pool["rr"]
            pool["rr"] = (i + 1) % len(pool["t"])
            t_, k_ = pool["t"][i], f"{pool['name']}{i}"
            src = wsrc[:, col0:col0 + ncol].rearrange("(c p) n -> p c n", p=128)
            S.dma("pool", lambda e: e.dma_start(out=t_[:, :, :ncol], in_=src), reads=[], writes=[k_])
            return t_, k_

        def mkpool(name, n, width):
            return {"name": name, "rr": 0, "t": [sb(f"{name}{i}", [128, NCH, width], BF16) for i in range(n)]}

        class WS:
            def __init__(self, pool, specs, n):
                self.pool, self.specs, self.n, self.i, self.q = pool, specs, n, 0, []
                self._issue()

            def _issue(self):
                if self.i < len(self.specs):
                    batch = self.specs[self.i:self.i + self.n]
                    self.i += self.n
                    self.q.append([loadw(self.pool, w_, c_, n_, "ws") for (w_, c_, n_) in batch])

            def next(self):
                tiles = self.q.pop(0)
                self._issue()
                return tiles

        with scope():
            crow = sb("crow", [1 + NS, D]); csT = sb("csT", [128, NCH, 17], BF16)
            brow = sb("brow", [72, 2, 128]); bmT = sb("bmT", [128, 144])
            mo = sb("mo", [128, 144, 17])
            wp = mkpool("wm", 3, 512)
            ld(crow[:], cc, [], ["crow"])
            act(crow[:], crow[:], AF.Silu, ["crow"], ["crow"])
            for c in range(NCH):
                p_, k_ = ps()
                tp(p_[:, :1 + NS], crow[:1 + NS, c * 128:(c + 1) * 128], 1 + NS, ["crow"], [k_])
                cp(csT[:, c, :], p_[:, :1 + NS], [k_], ["csT"])
            for l in range(DEPTH):
                ld(brow[:], b_mod[l].rearrange("(h r p) -> r h p", h=2, p=128), [], ["brow"])
                for h in range(2):
                    p_, k_ = ps()
                    tp(p_[:, :72], brow[:72, h, :], 72, ["brow"], [k_])
                    cp(bmT[:, h * 72:(h + 1) * 72], p_[:, :72], [k_], ["bmT"])
                for g in range(36):
                    wt, wk = loadw(wp, w_mod[l], g * 512, 512, "wm")
                    p_, k_ = ps()
                    for oc in range(4):
                        for c in range(NCH):
                            mm(p_[:, oc * 17:(oc + 1) * 17], wt[:, c, oc * 128:(oc + 1) * 128], csT[:, c, :], c == 0, c == NCH - 1, [wk, "csT"], [k_])
                    tt(mo[:, g * 4:(g + 1) * 4, :], p_[:, :68].rearrange("p (a b) -> p a b", b=17),
                       bmT[:, g * 4:(g + 1) * 4].unsqueeze(2).to_broadcast([128, 4, 17]), ALU.add, [k_, "bmT"], ["mo"])
                ld(modscr[l], mo[:].rearrange("p a b -> p (a b)"), ["mo"], [f"modscr{l}"])

        def col_tiles(C, has_s):
            tl = []
            c0 = 0
            while c0 < T:
                n = min(512, T - c0)
                tl.append((c0, n, False))
                c0 += n
            if has_s:
                tl.append((T, NS, True))
            return tl

        def modv(k, c):
            return modT[:, k * 16 + c, :]

        for seg in range(NSEG):
            has_s = (seg == NSEG - 1)
            last = has_s
            C = T + (NS if has_s else 0)
            ctl = col_tiles(C, has_s)
            tok0 = seg * T

            def xk(c, j):
                return f"xT:{c}:{j}"

            def hk(c, j):
                return f"hT:{c}:{j}"

            allx = [xk(c, j) for c in range(NCH) for j in range(len(ctl))]
            allh = [hk(c, j) for c in range(NCH) for j in range(len(ctl))]

            XT = [None]
            if True:
                def load_x():
                  with scope():
                      xrow = [sb(f"xrow{i}", [128, D]) for i in range(2)]
                      blocks = [(tok0 + b * 128, 128, b * 128, xp) for b in range(T // 128)]
                      if has_s:
                          blocks.append((0, NS, T, xs))
                      for bi, (r0, n, c0, src) in enumerate(blocks):
                          xr = xrow[bi % 2]
                          ld(xr[:n, :], src[r0:r0 + n, :], [], [f"xrow{bi % 2}"])
                          for c in range(NCH):
                              p_, k_ = ps()
                              tp(p_[:, :n], xr[:n, c * 128:(c + 1) * 128], n, [f"xrow{bi % 2}"], [k_])
                              cp(XT[0][:, c, c0:c0 + n], p_[:, :n], [k_], allx, eng=("act" if c % 2 else "dve"))

                def adaln(ksh, ksc, scale_ap=None):
                    with scope():
                        sq = [sb(f"sq{i}", [128, 512], BF16) for i in range(2)]
                        tmp = [sb(f"tmp{i}", [128, 512]) for i in range(2)]
                        rstd = sb("rstd", [128, 512])
                        for j, (cs, cn, is_s) in enumerate(ctl):
                            p_, k_ = ps()
                            for c in range(NCH):
                                act(sq[c % 2][:, :cn], XT[0][:, c, cs:cs + cn], AF.Square, [xk(c, j)], [f"sq{c % 2}"])
                                mm(p_[:, :cn], onesD[:], sq[c % 2][:, :cn], c == 0, c == NCH - 1, [f"sq{c % 2}", "onesD"], [k_])
                            act(rstd[:, :cn], p_[:, :cn], AF.Ln, [k_, "epsc"], ["rstd"], bias=epsc[:, 0:1], scale=1.0)
                            act(rstd[:, :cn], rstd[:, :cn], AF.Exp, ["rstd"], ["rstd"], scale=-0.5)
                            for c in range(NCH):
                                t_ = tmp[c % 2]; tk = f"tmp{c % 2}"
                                tt(t_[:, :cn], XT[0][:, c, cs:cs + cn], rstd[:, :cn], ALU.mult, [xk(c, j), "rstd"], [tk])
                                if not is_s:
                                    ts(hT[:, c, cs:cs + cn], t_[:, :cn], modv(ksc, c)[:, 0:1], modv(ksh, c)[:, 0:1], ALU.mult, ALU.add,
                                       [tk, "modT"], [hk(c, j)], eng="pool")
                                else:
                                    tt(t_[:, :cn], t_[:, :cn], modv(ksc, c)[:, 1:1 + NS], ALU.mult, [tk, "modT"], [tk])
                                    tt(hT[:, c, cs:cs + cn], t_[:, :cn], modv(ksh, c)[:, 1:1 + NS], ALU.add, [tk, "modT"], [hk(c, j)])

                stmp = sb("stmp", [128, NS])

                def resid_add(dc, j, cs, cn, is_s, p_, k_, kg):
                    if not is_s:
                        stt(XT[0][:, dc, cs:cs + cn], p_[:, :cn], modv(kg, dc)[:, 0:1], XT[0][:, dc, cs:cs + cn], ALU.mult, ALU.add,
                            [k_, "modT", xk(dc, j)], [xk(dc, j)])
                    else:
                        tt(stmp[:, :cn], p_[:, :cn], modv(kg, dc)[:, 1:1 + NS], ALU.mult, [k_, "modT"], ["stmp"])
                        tt(XT[0][:, dc, cs:cs + cn], XT[0][:, dc, cs:cs + cn], stmp[:, :cn], ALU.add, ["stmp", xk(dc, j)], [xk(dc, j)])

                def ffn(l, s, kg):
                    G = 2
                    with scope():
                        wgp = mkpool("wg", 2, G * 128); wup = mkpool("wu", 2, G * 128)
                        wdt = [sb(f"wd{i}", [128, G, D], BF16) for i in range(2)]
                        actb = [sb(f"actb{i}", [128, G, CMAX], BF16) for i in range(2)]
                        sg = [sb(f"sg{i}", [128, 512]) for i in range(2)]
                        ngrp = (NFF + G - 1) // G
                        def ffn_loads(gi):
                            f0 = gi * G
                            g_n = min(G, NFF - f0)
                            wg, wgk = loadw(wgp, w_gate[l, s], f0 * 128, g_n * 128, "wg")
                            wu, wuk = loadw(wup, w_up[l, s], f0 * 128, g_n * 128, "wu")
                            wd, wdk = wdt[gi % 2], f"wd{gi % 2}"
                            S.dma("pool", lambda e, wd=wd, f0=f0, g_n=g_n: e.dma_start(
                                out=wd[:, :g_n, :], in_=w_down[l, s, f0 * 128:(f0 + g_n) * 128, :].rearrange("(g p) d -> p g d", p=128)),
                                reads=[], writes=[wdk])
                            return (g_n, wg, wgk, wu, wuk, wd, wdk)
                        pend = ffn_loads(0)
                        for gi in range(ngrp):
                            g_n, wg, wgk, wu, wuk, wd, wdk = pend
                            if gi + 1 < ngrp:
                                pend = ffn_loads(gi + 1)
                            ab, abk = actb[gi % 2], f"actb{gi % 2}"
                            for g in range(g_n):
                                for j, (cs, cn, is_s) in enumerate(ctl):
                                    pg, pgk = ps(); pu, puk = ps()
                                    for c in range(NCH):
                                        mm(pg[:, :cn], wg[:, c, g * 128:(g + 1) * 128], hT[:, c, cs:cs + cn], c == 0, c == NCH - 1, [wgk, hk(c, j)], [pgk])
                                    for c in range(NCH):
                                        mm(pu[:, :cn], wu[:, c, g * 128:(g + 1) * 128], hT[:, c, cs:cs + cn], c == 0, c == NCH - 1, [wuk, hk(c, j)], [puk])
                                    sgi = (g * len(ctl) + j) % 2
                                    act(sg[sgi][:, :cn], pg[:, :cn], AF.Silu, [pgk], [f"sg{sgi}"])
                                    tt(ab[:, g, cs:cs + cn], sg[sgi][:, :cn], pu[:, :cn], ALU.mult, [f"sg{sgi}", puk], [f"{abk}:{g}:{j}"])
                            for dc in range(NCH):
                                for j, (cs, cn, is_s) in enumerate(ctl):
                                    pd, pdk = ps()
                                    for g in range(g_n):
                                        mm(pd[:, :cn], wd[:, g, dc * 128:(dc + 1) * 128], ab[:, g, cs:cs + cn], g == 0, g == g_n - 1,
                                           [wdk, f"{abk}:{g}:{j}"], [pdk])
                                    resid_add(dc, j, cs, cn, is_s, pd, pdk, kg)

                def out_proj(wsrc, kg, yT, yk):
                    with scope():
                        wop = mkpool("wo", 2, 512)
                        for dg in range(4):
                            wt, wk = loadw(wop, wsrc, dg * 512, 512, "wo")
                            for d4 in range(4):
                                dc = dg * 4 + d4
                                for j, (cs, cn, is_s) in enumerate(ctl):
                                    p_, k_ = ps()
                                    for m in range(NCH):
                                        mm(p_[:, :cn], wt[:, m, d4 * 128:(d4 + 1) * 128], yT[:, m, cs:cs + cn], m == 0, m == NCH - 1, [wk, yk(m)], [k_])
                                    resid_add(dc, j, cs, cn, is_s, p_, k_, kg)

                def proj(wt, wk, off, n, dst, dk, func=AF.Copy, bias=None, scale=None, extra_r=()):
                    for j, (cs, cn, is_s) in enumerate(ctl):
                        p_, k_ = ps()
                        for c in range(NCH):
                            mm(p_[:n, :cn], wt[:, c, off:off + n], hT[:, c, cs:cs + cn], c == 0, c == NCH - 1, [wk, hk(c, j)], [k_])
                        act(dst(cs, cn), p_[:n, :cn], func, [k_] + list(extra_r), [dk], bias=bias, scale=scale)

                seqs = [("p", 0, [(c * CL, CL, c) for c in range(NCK)])]
                if has_s:
                    seqs += [("s", s, [(T + s, 1, NCK + s)]) for s in range(NS)]
                NCKA = NCK + (NS if has_s else 0)

                def even_mixer(l, yT, yk):
                    j = l // 2
                    Wd = w_in_even[j]
                    R0 = 9 * j

                    def pr(row, n):
                        return prmT[:, n, R0 + row:R0 + row + 1]
                    with scope():
                        wpool = mkpool("wi", 6, 128)
                        xpad = sb("xpad", [128, 3 + T]); u = sb("u", [128, CMAX]); axs = sb("axs", [128, NS])
                        agt = sb("agt", [128, CMAX]); rg = sb("rg", [128, CMAX]); ig = sb("ig", [128, CMAX])
                        aa = sb("aa", [128, CMAX]); bb = sb("bb", [128, CMAX]); hh = sb("hh", [128, CMAX])
                        wr = [sb(f"wr{i}", [128, 128]) for i in range(2)]; wi_ = [sb(f"wig{i}", [128, 128]) for i in range(2)]
                        cvs = sb("cvs", [128, 8, 3, NS]); h0s = sb("h0s", [128, 8, NS])
                        ocv = sb("ocv", [3, 1024]); ohs = sb("ohs", [8, 128])
                        ocs = sb("ocs", [NS, 3, 1024]); ohss = sb("ohss", [NS, 1024])
                        if has_s:
                            crow_ = sb("crow_", [NS, 3, 1024]); hrow_ = sb("hrow_", [NS, 1024])
                            ld(crow_[:], i_a_conv[j], [], ["crow_"]); ld(hrow_[:], i_a_h[j], [], ["hrow_"])
                            for n in range(8):
                                for k in range(3):
                                    p_, k_ = ps()
                                    tp(p_[:, :NS], crow_[:NS, k, n * 128:(n + 1) * 128], NS, ["crow_"], [k_])
                                    cp(cvs[:, n, k, :], p_[:, :NS], [k_], ["cvs"])
                                p_, k_ = ps()
                                tp(p_[:, :NS], hrow_[:NS, n * 128:(n + 1) * 128], NS, ["hrow_"], [k_])
                                cp(h0s[:, n, :], p_[:, :NS], [k_], ["h0s"])
                        ws = WS(wpool, [sp_ for n in range(8) for sp_ in ((Wd, n * 128, 128), (Wd, 1024 + n * 128, 128))], 2)
                        for n in range(8):
                            (wax, waxk), (wag, wagk) = ws.next()
                            ld(wr[n % 2][:], a_gate_r_w[j, n], [], [f"wr{n % 2}"]); ld(wi_[n % 2][:], a_gate_i_w[j, n], [], [f"wig{n % 2}"])
                            proj(wax, waxk, 0, 128, lambda cs, cn: (xpad[:, 3 + cs:3 + cs + cn] if cs < T else axs[:, :cn]), "xpad")
                            proj(wag, wagk, 0, 128, lambda cs, cn: agt[:, cs:cs + cn], "agt")
                            cp(xpad[:, 0:3], convst[:, j, n, :], ["convst"], ["xpad"], eng="dve")
                            ts(u[:, 0:T], xpad[:, 3:3 + T], pr(3, n), pr(4, n), ALU.mult, ALU.add, ["xpad", "prmT"], ["u"])
                            for k in range(3):
                                stt(u[:, 0:T], xpad[:, k:k + T], pr(k, n), u[:, 0:T], ALU.mult, ALU.add, ["xpad", "prmT", "u"], ["u"])
                            cp(convst[:, j, n, :], xpad[:, T:T + 3], ["xpad"], ["convst"], eng="dve")
                            if has_s:
                                ts(u[:, T:T + NS], axs[:, :], pr(3, n), pr(4, n), ALU.mult, ALU.add, ["xpad", "prmT"], ["u"])
                                for k in range(3):
                                    stt(u[:, T:T + NS], cvs[:, n, k, :], pr(k, n), u[:, T:T + NS], ALU.mult, ALU.add, ["cvs", "prmT", "u"], ["u"])
                            for jj, (cs, cn, is_s) in enumerate(ctl):
                                p_, k_ = ps()
                                mm(p_[:, :cn], wr[n % 2][:], u[:, cs:cs + cn], True, True, [f"wr{n % 2}", "u"], [k_])
                                act(rg[:, cs:cs + cn], p_[:, :cn], AF.Sigmoid, [k_, "prmT"], ["rg"], bias=pr(5, n), scale=1.0)
                                p_, k_ = ps()
                                mm(p_[:, :cn], wi_[n % 2][:], u[:, cs:cs + cn], True, True, [f"wig{n % 2}", "u"], [k_])
                                act(ig[:, cs:cs + cn], p_[:, :cn], AF.Sigmoid, [k_, "prmT"], ["ig"], bias=pr(6, n), scale=1.0)
                            act(aa[:, :C], rg[:, :C], AF.Exp, ["rg", "nspT"], ["aa"], scale=nspT[:, n, j:j + 1])
                            tt(rg[:, :C], aa[:, :C], aa[:, :C], ALU.mult, ["aa"], ["rg"])
                            act(rg[:, :C], rg[:, :C], AF.Sqrt, ["rg", "onec"], ["rg"], bias=onec[:, 0:1], scale=-1.0)
                            tt(bb[:, :C], ig[:, :C], u[:, :C], ALU.mult, ["ig", "u"], ["bb"])
                            tt(bb[:, :C], bb[:, :C], rg[:, :C], ALU.mult, ["bb", "rg"], ["bb"])
                            S.op("dve", lambda e, n=n: e.tensor_tensor_scan(out=hh[:, 0:T], data0=aa[:, 0:T], data1=bb[:, 0:T],
                                                                         initial=hst[:, j, n:n + 1], op0=ALU.mult, op1=ALU.add),
                                 reads=["aa", "bb", "hst"], writes=["hh"])
                            cp(hst[:, j, n:n + 1], hh[:, T - 1:T], ["hh"], ["hst"], eng="dve")
                            if has_s:
                                tt(hh[:, T:T + NS], aa[:, T:T + NS], h0s[:, n, :], ALU.mult, ["aa", "h0s"], ["hh"])
                                tt(hh[:, T:T + NS], hh[:, T:T + NS], bb[:, T:T + NS], ALU.add, ["hh", "bb"], ["hh"])
                            tt(ig[:, :C], agt[:, :C], agt[:, :C], ALU.mult, ["agt"], ["ig"])
                            ts(ig[:, :C], ig[:, :C], 0.044715, 1.0, ALU.mult, ALU.add, ["ig"], ["ig"])
                            tt(ig[:, :C], ig[:, :C], agt[:, :C], ALU.mult, ["ig", "agt"], ["ig"])
                            act(ig[:, :C], ig[:, :C], AF.Sigmoid, ["ig"], ["ig"], scale=1.5957691216057308)
                            tt(ig[:, :C], ig[:, :C], agt[:, :C], ALU.mult, ["ig", "agt"], ["ig"])
                            tt(yT[:, n, :C], ig[:, :C], hh[:, :C], ALU.mult, ["ig", "hh"], [yk(n)])
                            if last:
                                p_, k_ = ps()
                                tp(p_[:3, :128], convst[:, j, n, :], 128, ["convst"], [k_])
                                cp(ocv[:3, n * 128:(n + 1) * 128], p_[:3, :128], [k_], ["ocv"])
                                srcs = [cvs[:, n, 1, :], cvs[:, n, 2, :], axs[:, :]]
                                for k in range(3):
                                    p_, k_ = ps()
                                    tp(p_[:NS, :128], srcs[k], 128, ["cvs", "xpad"], [k_])
                                    cp(ocs[:NS, k, n * 128:(n + 1) * 128], p_[:NS, :128], [k_], ["ocs"])
                                p_, k_ = ps()
                                tp(p_[:NS, :128], hh[:, T:T + NS], 128, ["hh"], [k_])
                                cp(ohss[:NS, n * 128:(n + 1) * 128], p_[:NS, :128], [k_], ["ohss"])
                        if last:
                            p_, k_ = ps()
                            tp(p_[:8, :128], hst[:, j, :], 128, ["hst"], [k_])
                            cp(ohs[:8, :], p_[:8, :128], [k_], ["ohs"])
                            ld(pa_conv[j], ocv[:], ["ocv"], [f"pa_conv{j}"])
                            ld(pa_h[j].rearrange("(a b) -> a b", b=128), ohs[:], ["ohs"], [f"pa_h{j}"])
                            ld(sa_conv[j], ocs[:], ["ocs"], [f"sa_conv{j}"])
                            ld(sa_h[j], ohss[:], ["ohss"], [f"sa_h{j}"])
                    with scope():
                        wpool = mkpool("wi", 8, 128)
                        ws = WS(wpool, [sp_ for hd in range(8) for sp_ in ((Wd, 2048 + hd * 128, 128), (Wd, 3072 + hd * 128, 128),
                                                                          (Wd, 4096 + hd * 128, 128), (Wd, 5120 + hd * 128, 128))], 4)
                        q = sb("q", [128, CMAX]); kk = sb("kk", [128, CMAX]); iv = sb("iv", [128, CMAX]); gs = sb("gs", [128, CMAX])
                        bc = sb("bc", [128, CMAX]); d1 = sb("d1", [128, CMAX]); e2 = sb("e2", [128, CMAX])
                        qin = sb("qin", [128, CMAX]); qt = sb("qt", [128, CMAX]); kt = sb("kt", [128, CMAX]); ko = sb("ko", [128, CMAX])
                        eg = sb("eg", [128, NCK + NS]); o = sb("o", [128, CMAX])
                        Sst = [sb(f"Sst{i}", [128, 128]) for i in range(2)]
                        attT = sb("attT", [64, 64]); ivtok = sb("ivtok", [64, 128]); kotok = sb("kotok", [64, 128])

                        def v3(t_):
                            return t_[:, 0:T].rearrange("p (c t) -> p c t", t=CL)
                        for hd in range(8):
                            (wq, wqk), (wf, wfk), (wv, wvk), (wg_, wgk_) = ws.next()
                            proj(wq, wqk, 0, 128, lambda cs, cn: q[:, cs:cs + cn], "q")
                            proj(wf, wfk, 0, 128, lambda cs, cn: kk[:, cs:cs + cn], "kk", func=AF.Sigmoid, scale=-1.0)
                            proj(wv, wvk, 0, 128, lambda cs, cn: iv[:, cs:cs + cn], "iv")
                            proj(wg_, wgk_, 0, 128, lambda cs, cn: gs[:, cs:cs + cn], "gs", func=AF.Silu)
                            ts(kk[:, :C], kk[:, :C], omlT[:, hd, j:j + 1], None, ALU.mult, None, ["kk", "omlT"], ["kk"])
                            act(d1[:, :C], kk[:, :C], AF.Ln, ["kk", "onec"], ["d1"], bias=onec[:, 0:1], scale=-1.0)
                            S.op("dve", lambda e: e.tensor_tensor_scan(out=bc[:, 0:T], data0=cmask[:, 0:T], data1=d1[:, 0:T], initial=0.0,
                                                                     op0=ALU.mult, op1=ALU.add), reads=["cmask", "d1"], writes=["bc"])
                            if has_s:
                                cp(bc[:, T:T + NS], d1[:, T:T + NS], ["d1"], ["bc"], eng="dve")
                            act(e2[:, :C], bc[:, :C], AF.Exp, ["bc"], ["e2"])
                            tt(qin[:, :C], q[:, :C], e2[:, :C], ALU.mult, ["q", "e2"], ["qin"])
                            act(eg[:, 0:NCK], v3(bc)[:, :, CL - 1], AF.Exp, ["bc"], ["eg"])
                            if has_s:
                                cp(eg[:, NCK:NCK + NS], e2[:, T:T + NS], ["e2"], ["eg"], eng="dve")
                            tt(v3(d1), v3(bc), v3(bc)[:, :, CL // 2 - 1:CL // 2].to_broadcast([128, NCK, CL]), ALU.subtract, ["bc"], ["d1"])
                            if has_s:
                                mset(d1[:, T:T + NS], 0.0, ["d1"])
                            act(e2[:, :C], d1[:, :C], AF.Exp, ["d1"], ["e2"])
                            tt(qt[:, :C], q[:, :C], e2[:, :C], ALU.mult, ["q", "e2"], ["qt"])
                            act(e2[:, :C], d1[:, :C], AF.Exp, ["d1"], ["e2"], scale=-1.0)
                            tt(kt[:, :C], kk[:, :C], e2[:, :C], ALU.mult, ["kk", "e2"], ["kt"])
                            tt(v3(d1), v3(bc)[:, :, CL - 1:CL].to_broadcast([128, NCK, CL]), v3(bc), ALU.subtract, ["bc"], ["d1"])
                            act(e2[:, :C], d1[:, :C], AF.Exp, ["d1"], ["e2"])
                            tt(ko[:, :C], kk[:, :C], e2[:, :C], ALU.mult, ["kk", "e2"], ["ko"])
                            for si, (kind, s_, chunks) in enumerate(seqs):
                                St, Sk = Sst[si % 2], f"Sst{si % 2}"
                                if kind == "p":
                                    if seg == 0:
                                        mset(St[:], 0.0, [Sk])
                                    else:
                                        ld(St[:], pb_s[j, hd], [f"pb_s{j}:{hd}"], [Sk])
                                else:
                                    ld(St[:], i_b_s[j, s_, hd], [], [Sk])
                                for (c0, cl, ci) in chunks:
                                    pa, pak = ps()
                                    mm(pa[:cl, :cl], kt[:, c0:c0 + cl], qt[:, c0:c0 + cl], True, True, ["kt", "qt"], [pak])
                                    tt(attT[:cl, :cl], pa[:cl, :cl], mmul[:cl, :cl], ALU.mult, [pak, "mmul"], ["attT"])
                                    pb, pbk = ps()
                                    tp(pb[:cl, :128], iv[:, c0:c0 + cl], 128, ["iv"], [pbk])
                                    cp(ivtok[:cl, :], pb[:cl, :128], [pbk], ["ivtok"])
                                    po, pok = ps()
                                    mm(po[:, :cl], St[:, :], qin[:, c0:c0 + cl], True, False, [Sk, "qin"], [pok])
                                    mm(po[:, :cl], ivtok[:cl, :], attT[:cl, :cl], False, True, ["ivtok", "attT"], [pok])
                                    cp(o[:, c0:c0 + cl], po[:, :cl], [pok], ["o"])
                                    pk_, pkk = ps()
                                    tp(pk_[:cl, :128], ko[:, c0:c0 + cl], 128, ["ko"], [pkk])
                                    cp(kotok[:cl, :], pk_[:cl, :128], [pkk], ["kotok"])
                                    pu, puk = ps()
                                    mm(pu[:, :128], kotok[:cl, :], ivtok[:cl, :], True, True, ["kotok", "ivtok"], [puk])
                                    stt(St[:, :], St[:, :], eg[:, ci:ci + 1], pu[:, :128], ALU.mult, ALU.add, [Sk, "eg", puk], [Sk])
                                if kind == "p":
                                    ld(pb_s[j, hd], St[:], [Sk], [f"pb_s{j}:{hd}"])
                                else:
                                    ld(sb_s[j, s_, hd], St[:], [Sk], [f"sb_s{j}:{s_}:{hd}"])
                            act(e2[:, :C], o[:, :C], AF.Square, ["o"], ["e2"])
                            for jj, (cs, cn, is_s) in enumerate(ctl):
                                p_, k_ = ps()
                                mm(p_[:, :cn], ones128[:], e2[:, cs:cs + cn], True, True, ["ones128", "e2"], [k_])
                                act(d1[:, cs:cs + cn], p_[:, :cn], AF.Ln, [k_, "epsc"], ["d1"], bias=epsc[:, 0:1], scale=1.0)
                            act(d1[:, :C], d1[:, :C], AF.Exp, ["d1"], ["d1"], scale=-0.5)
                            tt(o[:, :C], o[:, :C], d1[:, :C], ALU.mult, ["o", "d1"], ["o"])
                            stt(yT[:, 8 + hd, :C], o[:, :C], bnwT[:, j:j + 1], gs[:, :C], ALU.mult, ALU.mult, ["o", "bnwT", "gs"], [yk(8 + hd)])

                def odd_mixer(l, yT, yk):
                    j = l // 2
                    Wd = w_in_odd[j]
                    R0 = 9 * NE + 7 * j

                    def pr(row, n):
                        return prmT[:, n, R0 + row:R0 + row + 1]
                    with scope():
                        wpool = mkpool("wi", 12, 128)
                        wgi = sb("wgi", [128, NCH, 8], BF16)
                        S.dma("pool", lambda e: e.dma_start(out=wgi[:], in_=Wd[:, 3072:3080].rearrange("(c p) n -> p c n", p=128)), reads=[], writes=["wgi"])
                        q = sb("q", [128, CMAX]); k = sb("k", [128, CMAX]); v2 = sb("v2", [128, 2, CMAX]); so = sb("so", [128, 2, CMAX])
                        igr = sb("igr", [1, CMAX]); lfr = sb("lfr", [1, CMAX]); br = sb("br", [1, CMAX]); igb = sb("igb", [1, CMAX])
                        inr = sb("inr", [1, CMAX]); wkr = sb("wkr", [1, CMAX])
                        Mr = sb("Mr", [1, NCK + NS]); gr = sb("gr", [1, NCK + NS]); mr = sb("mr", [1, NCK + NS]); mpv = sb("mpv", [1, NCK + NS])
                        fsr = sb("fsr", [1, NCK + NS]); t1r = sb("t1r", [1, NCK + NS]); fsb = sb("fsb", [128, NCK + NS])
                        Cn = [sb(f"Cn{i}", [128, 257]) for i in range(2)]
                        dm = sb("dm", [64, 64]); Pm = sb("Pm", [64, 64]); sm = sb("sm", [64, 64]); smT = sb("smT", [64, 64])
                        vtok = sb("vtok", [64, 257]); ktok = sb("ktok", [64, 128]); sv = sb("sv", [64, 256]); num = sb("num", [64, 256])
                        junk = sb("junk", [64, 256])
                        cols = sb("cols", [64, 8])
                        icol = sb("icol", [64, 2])
                        mset(vtok[:, 256:257], 1.0, ["vtok"])

                        def r3(t_):
                            return t_[0:1, 0:T].rearrange("p (c t) -> p c t", t=CL)
                        ws = WS(wpool, [sp_ for hc in range(4) for sp_ in ((Wd, hc * 128, 128), (Wd, 512 + hc * 128, 128),
                                                                          (Wd, 1024 + hc * 256, 128), (Wd, 2048 + hc * 256, 128),
                                                                          (Wd, 1024 + hc * 256 + 128, 128), (Wd, 2048 + hc * 256 + 128, 128))], 6)
                        for hc in range(4):
                            (wq, wqk), (wk_, wkk), wv0, wo0, wv1, wo1 = ws.next()
                            wvo = [(wv0, wo0), (wv1, wo1)]
                            proj(wq, wqk, 0, 128, lambda cs, cn: q[:, cs:cs + cn], "q")
                            proj(wk_, wkk, 0, 128, lambda cs, cn: k[:, cs:cs + cn], "k", scale=128.0 ** -0.5)
                            for i in range(2):
                                (wv, wvk), (wo_, wok) = wvo[i]
                                proj(wv, wvk, 0, 128, lambda cs, cn, i=i: v2[:, i, cs:cs + cn], "v2")
                                proj(wo_, wok, 0, 128, lambda cs, cn, i=i: so[:, i, cs:cs + cn], "so", func=AF.Sigmoid)
                            proj(wgi, "wgi", hc, 1, lambda cs, cn: igr[0:1, cs:cs + cn], "igr", func=AF.Identity,
                                 bias=cgb[0:1, 8 * j + hc:8 * j + hc + 1], scale=1.0, extra_r=["cgb"])
                            proj(wgi, "wgi", 4 + hc, 1, lambda cs, cn: lfr[0:1, cs:cs + cn], "lfr", func=AF.Sigmoid,
                                 bias=cgb[0:1, 8 * j + 4 + hc:8 * j + 5 + hc], scale=1.0, extra_r=["cgb"])
                            act(lfr[0:1, :C], lfr[0:1, :C], AF.Ln, ["lfr"], ["lfr"])
                            S.op("dve", lambda e: e.tensor_tensor_scan(out=br[0:1, 0:T], data0=cmask[0:1, 0:T], data1=lfr[0:1, 0:T], initial=0.0,
                                                                     op0=ALU.mult, op1=ALU.add), reads=["cmask", "lfr"], writes=["br"])
                            if has_s:
                                cp(br[0:1, T:T + NS], lfr[0:1, T:T + NS], ["lfr"], ["br"], eng="dve")
                            tt(igb[0:1, :C], igr[0:1, :C], br[0:1, :C], ALU.subtract, ["igr", "br"], ["igb"])
                            S.op("dve", lambda e: e.tensor_reduce(out=Mr[0:1, 0:NCK], in_=r3(igb), axis=AX.X, op=ALU.max), reads=["igb"], writes=["Mr"])
                            cp(gr[0:1, 0:NCK], r3(br)[:, :, CL - 1], ["br"], ["gr"], eng="dve")
                            S.op("dve", lambda e, hc=hc: e.tensor_tensor_scan(out=mr[0:1, 0:NCK], data0=Mr[0:1, 0:NCK], data1=gr[0:1, 0:NCK],
                                                                            initial=mst[0:1, 4 * j + hc:4 * j + hc + 1], op0=ALU.max, op1=ALU.add),
                                 reads=["Mr", "gr", "mst"], writes=["mr"])
                            cp(mpv[0:1, 0:1], mst[0:1, 4 * j + hc:4 * j + hc + 1], ["mst"], ["mpv"], eng="dve")
                            if NCK > 1:
                                cp(mpv[0:1, 1:NCK], mr[0:1, 0:NCK - 1], ["mr"], ["mpv"], eng="dve")
                            cp(mst[0:1, 4 * j + hc:4 * j + hc + 1], mr[0:1, NCK - 1:NCK], ["mr", "mpv"], ["mst"], eng="dve")
                            if has_s:
                                cp(Mr[0:1, NCK:NCK + NS], igb[0:1, T:T + NS], ["igb"], ["Mr"], eng="dve")
                                cp(gr[0:1, NCK:NCK + NS], br[0:1, T:T + NS], ["br"], ["gr"], eng="dve")
                                ld(mpv[0:1, NCK:NCK + NS], i_c_m[j, :, hc:hc + 1].rearrange("s o -> o s"), [], ["mpv"])
                                tt(mr[0:1, NCK:NCK + NS], Mr[0:1, NCK:NCK + NS], mpv[0:1, NCK:NCK + NS], ALU.max, ["Mr", "mpv"], ["mr"])
                                tt(mr[0:1, NCK:NCK + NS], mr[0:1, NCK:NCK + NS], gr[0:1, NCK:NCK + NS], ALU.add, ["mr", "gr"], ["mr"])
                                ld(sc_m[j, :, hc:hc + 1].rearrange("s o -> o s"), mr[0:1, NCK:NCK + NS], ["mr"], [f"sc_m{j}:{hc}"])
                            if last:
                                ld(pc_m[j:j + 1, hc:hc + 1], mr[0:1, NCK - 1:NCK], ["mr"], [f"pc_m{j}:{hc}"])
                            tt(r3(inr), r3(br), mpv[0:1, 0:NCK].unsqueeze(2).to_broadcast([1, NCK, CL]), ALU.add, ["br", "mpv"], ["inr"])
                            tt(t1r[0:1, :NCKA], gr[0:1, :NCKA], mr[0:1, :NCKA], ALU.subtract, ["gr", "mr"], ["t1r"])
                            tt(r3(wkr), r3(igb), t1r[0:1, 0:NCK].unsqueeze(2).to_broadcast([1, NCK, CL]), ALU.add, ["igb", "t1r"], ["wkr"])
                            if has_s:
                                tt(inr[0:1, T:T + NS], br[0:1, T:T + NS], mpv[0:1, NCK:NCK + NS], ALU.add, ["br", "mpv"], ["inr"])
                                tt(wkr[0:1, T:T + NS], igb[0:1, T:T + NS], t1r[0:1, NCK:NCK + NS], ALU.add, ["igb", "t1r"], ["wkr"])
                            act(wkr[0:1, :C], wkr[0:1, :C], AF.Exp, ["wkr"], ["wkr"])
                            tt(fsr[0:1, :NCKA], t1r[0:1, :NCKA], mpv[0:1, :NCKA], ALU.add, ["t1r", "mpv"], ["fsr"])
                            act(fsr[0:1, :NCKA], fsr[0:1, :NCKA], AF.Exp, ["fsr"], ["fsr"])
                            p_, k_ = ps()
                            mm(p_[:, :NCKA], onesf[0:1, :], fsr[0:1, :NCKA], True, True, ["onesf", "fsr"], [k_])
                            cp(fsb[:, :NCKA], p_[:, :NCKA], [k_], ["fsb"])
                            for si, (kind, s_, chunks) in enumerate(seqs):
                                Ct, Ck = Cn[si % 2], f"Cn{si % 2}"
                                if kind == "p":
                                    if seg == 0:
                                        mset(Ct[:], 0.0, [Ck])
                                    else:
                                        ld(Ct[:, 0:256], pc_c[j, hc], [f"pc_c{j}:{hc}"], [Ck])
                                        ld(Ct[:, 256:257], pc_n[j, hc].rearrange("(k o) -> k o", o=1), [f"pc_n{j}:{hc}"], [Ck])
                                else:
                                    ld(Ct[:, 0:256], i_c_c[j, s_, hc], [], [Ck])
                                    ld(Ct[:, 256:257], i_c_n[j, s_, hc].rearrange("(k o) -> k o", o=1), [], [Ck])
                                for (c0, cl, ci) in chunks:
                                    pdm, pdmk = ps()
                                    mm(pdm[:cl, :cl], onesf[0:1, :cl], igb[0:1, c0:c0 + cl], True, False, ["onesf", "igb"], [pdmk])
                                    mm(pdm[:cl, :cl], br[0:1, c0:c0 + cl], onesf[0:1, :cl], False, True, ["br", "onesf"], [pdmk])
                                    tt(dm[:cl, :cl], pdm[:cl, :cl], madd[:cl, :cl], ALU.add, [pdmk, "madd"], ["dm"])
                                    S.op("dve", lambda e, cl=cl: e.tensor_reduce(out=cols[:cl, 0:1], in_=dm[:cl, :cl], axis=AX.X, op=ALU.max), reads=["dm"], writes=["c0"])
                                    pic, pick = ps()
                                    mm(pic[:cl, 0:1], inr[0:1, c0:c0 + cl], onesf[0:1, 0:1], True, True, ["inr", "onesf"], [pick])
                                    mm(pic[:cl, 1:2], wkr[0:1, c0:c0 + cl], onesf[0:1, 0:1], True, True, ["wkr", "onesf"], [pick])
                                    cp(icol[:cl, :], pic[:cl, 0:2], [pick], ["icol"])
                                    tt(cols[:cl, 1:2], cols[:cl, 0:1], icol[:cl, 0:1], ALU.max, ["c0", "icol"], ["c1"])
                                    ts(cols[:cl, 2:3], cols[:cl, 1:2], -1.0, None, ALU.mult, None, ["c1"], ["c2"])
                                    act(cols[:cl, 3:4], icol[:cl, 0:1], AF.Exp, ["icol", "c2"], ["c3"], bias=cols[:cl, 2:3], scale=1.0)
                                    act(cols[:cl, 4:5], cols[:cl, 2:3], AF.Exp, ["c2"], ["c4"])
                                    act(Pm[:cl, :cl], dm[:cl, :cl], AF.Exp, ["dm", "c2"], ["Pm"], bias=cols[:cl, 2:3], scale=1.0)
                                    pqk, pqkk = ps()
                                    mm(pqk[:cl, :cl], q[:, c0:c0 + cl], k[:, c0:c0 + cl], True, True, ["q", "k"], [pqkk])
                                    tt(sm[:cl, :cl], pqk[:cl, :cl], Pm[:cl, :cl], ALU.mult, [pqkk, "Pm"], ["sm"])
                                    S.op("dve", lambda e, cl=cl: e.tensor_reduce(out=cols[:cl, 5:6], in_=sm[:cl, :cl], axis=AX.X, op=ALU.add), reads=["sm"], writes=["c5"])
                                    pst, pstk = ps()
                                    tp(pst[:cl, :cl], sm[:cl, :cl], cl, ["sm"], [pstk])
                                    cp(smT[:cl, :cl], pst[:cl, :cl], [pstk], ["smT"])
                                    pv, pvk = ps()
                                    for i in range(2):
                                        tp(pv[:cl, i * 128:(i + 1) * 128], v2[:, i, c0:c0 + cl], 128, ["v2"], [pvk])
                                    cp(vtok[:cl, 0:256], pv[:cl, 0:256], [pvk], ["vtok"])
                                    pqc, pqck = ps()
                                    mm(pqc[:cl, :257], q[:, c0:c0 + cl], Ct[:, :257], True, True, ["q", Ck], [pqck])
                                    psv, psvk = ps()
                                    mm(psv[:cl, :256], smT[:cl, :cl], vtok[:cl, :256], True, True, ["smT", "vtok"], [psvk])
                                    cp(sv[:cl, :], psv[:cl, :256], [psvk], ["sv"])
                                    stt(num[:cl, :], pqc[:cl, :256], cols[:cl, 3:4], sv[:cl, :], ALU.mult, ALU.add, [pqck, "c3", "sv"], ["num"])
                                    stt(cols[:cl, 6:7], pqc[:cl, 256:257], cols[:cl, 3:4], cols[:cl, 5:6], ALU.mult, ALU.add, [pqck, "c3", "c5"], ["c6"])
                                    act(cols[:cl, 6:7], cols[:cl, 6:7], AF.Abs, ["c6"], ["c6"])
                                    tt(cols[:cl, 6:7], cols[:cl, 6:7], cols[:cl, 4:5], ALU.max, ["c6", "c4"], ["c6"])
                                    S.op("dve", lambda e, cl=cl: e.reciprocal(out=cols[:cl, 6:7], in_=cols[:cl, 6:7]), reads=["c6"], writes=["c6"])
                                    ts(num[:cl, :], num[:cl, :], cols[:cl, 6:7], None, ALU.mult, None, ["num", "c6"], ["num"])
                                    tt(junk[:cl, :], num[:cl, :], num[:cl, :], ALU.mult, ["num"], ["junk"])
                                    S.op("dve", lambda e, cl=cl: e.tensor_reduce(out=cols[:cl, 7:8], in_=junk[:cl, :], axis=AX.X, op=ALU.add), reads=["junk"], writes=["c7"])
                                    act(cols[:cl, 7:8], cols[:cl, 7:8], AF.Ln, ["c7", "epsc"], ["c7"], bias=epsc[:cl, 0:1], scale=1.0 / 256.0)
                                    act(cols[:cl, 7:8], cols[:cl, 7:8], AF.Exp, ["c7"], ["c7"], scale=-0.5)
                                    ts(num[:cl, :], num[:cl, :], cols[:cl, 7:8], None, ALU.mult, None, ["num", "c7"], ["num"])
                                    for i in range(2):
                                        ph, phk = ps()
                                        tp(ph[:, :cl], num[:cl, i * 128:(i + 1) * 128], cl, ["num"], [phk])
                                        stt(yT[:, hc * 2 + i, c0:c0 + cl], ph[:, :cl], cnwT[:, i, j:j + 1], so[:, i, c0:c0 + cl], ALU.mult, ALU.mult,
                                            [phk, "cnwT", "so"], [yk(hc * 2 + i)])
                                    pkt, pktk = ps()
                                    tp(pkt[:cl, :128], k[:, c0:c0 + cl], 128, ["k"], [pktk])
                                    ts(ktok[:cl, :], pkt[:cl, :128], icol[:cl, 1:2], None, ALU.mult, None, [pktk, "icol"], ["ktok"])
                                    pcu, pcuk = ps()
                                    mm(pcu[:, :257], ktok[:cl, :], vtok[:cl, :257], True, True, ["ktok", "vtok"], [pcuk])
                                    stt(Ct[:, :257], Ct[:, :257], fsb[:, ci:ci + 1], pcu[:, :257], ALU.mult, ALU.add, [Ck, "fsb", pcuk], [Ck])
                                if kind == "p":
                                    ld(pc_c[j, hc], Ct[:, 0:256], [Ck], [f"pc_c{j}:{hc}"])
                                    ld(pc_n[j, hc].rearrange("(k o) -> k o", o=1), Ct[:, 256:257], [Ck], [f"pc_n{j}:{hc}"])
                                else:
                                    ld(sc_c[j, s_, hc], Ct[:, 0:256], [Ck], [f"sc_c{j}:{s_}:{hc}"])
                                    ld(sc_n[j, s_, hc].rearrange("(k o) -> k o", o=1), Ct[:, 256:257], [Ck], [f"sc_n{j}:{s_}:{hc}"])

                    with scope():
                        wpool = mkpool("wi", 6, 128)
                        zp = sb("zp", [128, 1 + T]); zss = sb("zss", [128, NS]); tz = sb("tz", [128, CMAX])
                        prevT = sb("prevT", [128, 26, NS]); zdl = sb("zdl", [128, 26, NS])
                        twl = sb("twl", [128, CMAX]); sgl = sb("sgl", [128, CMAX])
                        w2t = sb("w2t", [128, 1024]); g2t = sb("g2t", [128, 1024])
                        rr = sb("rr", [128, CMAX]); kx = sb("kx", [128, CMAX]); vv = sb("vv", [128, CMAX]); aa = sb("aa", [128, CMAX])
                        t1 = sb("t1", [128, CMAX]); t2 = sb("t2", [128, CMAX]); t3 = sb("t3", [128, CMAX])
                        ld(w2t[0:64, :], d_w2[j], [], ["w2t"]); ld(w2t[64:128, :], d_a2[j], [], ["w2t"]); ld(g2t[:], d_g2[j], [], ["g2t"])
                        if has_s:
                            prow_ = sb("prow_", [NS, 3328]); orow_ = sb("orow_", [NS, 3328])
                            ld(prow_[:], i_d_shift[j], [], ["prow_"])
                            for fc in range(26):
                                p_, k_ = ps()
                                tp(p_[:, :NS], prow_[:NS, fc * 128:(fc + 1) * 128], NS, ["prow_"], [k_])
                                cp(prevT[:, fc, :], p_[:, :NS], [k_], ["prevT"])

                        def shifted(fc, dst, dk):
                            ((wt, wk),) = wsr.next()
                            proj(wt, wk, 0, 128, lambda cs, cn: (zp[:, 1 + cs:1 + cs + cn] if cs < T else zss[:, :cn]), "zp")
                            cp(zp[:, 0:1], shst[:, j, fc:fc + 1], ["shst"], ["zp"], eng="dve")
                            tt(tz[:, 0:T], zp[:, 0:T], zp[:, 1:1 + T], ALU.subtract, ["zp"], ["tz"])
                            stt(dst[:, 0:T], tz[:, 0:T], muT[:, fc, j:j + 1], zp[:, 1:1 + T], ALU.mult, ALU.add, ["tz", "muT", "zp"], [dk])
                            cp(shst[:, j, fc:fc + 1], zp[:, T:T + 1], ["zp"], ["shst"], eng="dve")
                            if has_s:
                                tt(tz[:, T:T + NS], prevT[:, fc, :], zss[:, :], ALU.subtract, ["prevT", "zp"], ["tz"])
                                stt(dst[:, T:T + NS], tz[:, T:T + NS], muT[:, fc, j:j + 1], zss[:, :], ALU.mult, ALU.add, ["tz", "muT", "zp"], [dk])
                                cp(zdl[:, fc, :], zss[:, :], ["zp"], ["zdl"], eng="dve")
                        order = [24, 25] + [x_ for fc in range(8) for x_ in (fc, 8 + fc, 16 + fc)]
                        wsr = WS(wpool, [(Wd, 3080 + fc_ * 128, 128) for fc_ in order], 1)
                        shifted(24, twl, "twl")
                        act(twl[0:64, :C], twl[0:64, :C], AF.Tanh, ["twl"], ["twl"])
                        shifted(25, sgl, "sgl")
                        act(sgl[:, :C], sgl[:, :C], AF.Sigmoid, ["sgl"], ["sgl"])
                        for fc in range(8):
                            shifted(fc, rr, "rr"); shifted(8 + fc, kx, "kx"); shifted(16 + fc, vv, "vv")
                            for jj, (cs, cn, is_s) in enumerate(ctl):
                                p_, k_ = ps()
                                mm(p_[:, :cn], w2t[0:64, fc * 128:(fc + 1) * 128], twl[0:64, cs:cs + cn], True, True, ["w2t", "twl"], [k_])
                                act(t1[:, cs:cs + cn], p_[:, :cn], AF.Sigmoid, [k_, "prmT"], ["t1"], bias=pr(0, fc), scale=1.0)
                                p_, k_ = ps()
                                mm(p_[:, :cn], w2t[64:128, fc * 128:(fc + 1) * 128], twl[64:128, cs:cs + cn], True, True, ["w2t", "twl"], [k_])
                                act(aa[:, cs:cs + cn], p_[:, :cn], AF.Sigmoid, [k_, "prmT"], ["aa"], bias=pr(1, fc), scale=1.0)
                                p_, k_ = ps()
                                mm(p_[:, :cn], g2t[:, fc * 128:(fc + 1) * 128], sgl[:, cs:cs + cn], True, True, ["g2t", "sgl"], [k_])
                                cp(t3[:, cs:cs + cn], p_[:, :cn], [k_], ["t3"])
                            ld(rwscr[7, fc, :, :C], t3[:, :C], ["t3"], [f"rw7:{fc}"])
                            act(t1[:, :C], t1[:, :C], AF.Exp, ["t1"], ["t1"], scale=-0.6065306597126334)
                            ld(rwscr[1, fc, :, :C], t1[:, :C], ["t1"], [f"rw1:{fc}"])
                            ts(t2[:, :C], kx[:, :C], pr(2, fc), None, ALU.mult, None, ["kx", "prmT"], ["t2"])
                            tt(t3[:, :C], t2[:, :C], t2[:, :C], ALU.mult, ["t2"], ["t3"])
                            for jj, (cs, cn, is_s) in enumerate(ctl):
                                p_, k_ = ps()
                                mm(p_[:, :cn], blk[:], t3[:, cs:cs + cn], True, True, ["blk", "t3"], [k_])
                                act(t1[:, cs:cs + cn], p_[:, :cn], AF.Sqrt, [k_], ["t1"])
                            ts(t1[:, :C], t1[:, :C], 1e-12, None, ALU.max, None, ["t1"], ["t1"])
                            S.op("dve", lambda e, t1=t1, C=C: e.reciprocal(out=t1[:, :C], in_=t1[:, :C]), reads=["t1"], writes=["t1"])
                            tt(t2[:, :C], t2[:, :C], t1[:, :C], ALU.mult, ["t2", "t1"], ["t2"])
                            ld(rwscr[0, fc, :, :C], t2[:, :C], ["t2"], [f"rw0:{fc}"])
                            tt(t3[:, :C], t2[:, :C], aa[:, :C], ALU.mult, ["t2", "aa"], ["t3"])
                            ld(rwscr[2, fc, :, :C], t3[:, :C], ["t3"], [f"rw2:{fc}"])
                            ts(t1[:, :C], aa[:, :C], -1.0, pr(3, fc), ALU.add, ALU.mult, ["aa", "prmT"], ["t1"])
                            ts(t1[:, :C], t1[:, :C], 1.0, None, ALU.add, None, ["t1"], ["t1"])
                            tt(t1[:, :C], t1[:, :C], kx[:, :C], ALU.mult, ["t1", "kx"], ["t1"])
                            ld(rwscr[3, fc, :, :C], t1[:, :C], ["t1"], [f"rw3:{fc}"])
                            ld(rwscr[4, fc, :, :C], rr[:, :C], ["rr"], [f"rw4:{fc}"])
                            ld(rwscr[5, fc, :, :C], vv[:, :C], ["vv"], [f"rw5:{fc}"])
                            tt(t2[:, :C], t1[:, :C], rr[:, :C], ALU.mult, ["t1", "rr"], ["t2"])
                            ts(t2[:, :C], t2[:, :C], pr(4, fc), None, ALU.mult, None, ["t2", "prmT"], ["t2"])
                            for jj, (cs, cn, is_s) in enumerate(ctl):
                                p_, k_ = ps()
                                mm(p_[:, :cn], blk[:], t2[:, cs:cs + cn], True, True, ["blk", "t2"], [k_])
                                tt(t3[:, cs:cs + cn], p_[:, :cn], vv[:, cs:cs + cn], ALU.mult, [k_, "vv"], ["t3"])
                            ld(rwscr[6, fc, :, :C], t3[:, :C], ["t3"], [f"rw6:{fc}"])
                        if has_s:
                            for fc in range(26):
                                p_, k_ = ps()
                                tp(p_[:NS, :128], zdl[:, fc, :], 128, ["zdl"], [k_])
                                cp(orow_[:NS, fc * 128:(fc + 1) * 128], p_[:NS, :128], [k_], ["orow_"])
                            ld(sd_shift[j], orow_[:], ["orow_"], [f"sd_shift{j}"])
                        if last:
                            oshp = sb("oshp", [26, 128])
                            p_, k_ = ps()
                            tp(p_[:26, :128], shst[:, j, :], 128, ["shst"], [k_])
                            cp(oshp[:26, :], p_[:26, :128], [k_], ["oshp"])
                            ld(pd_shift[j].rearrange("(a b) -> a b", b=128), oshp[:], ["oshp"], [f"pd_shift{j}"])
                    with scope():
                      yv = sb("yv", [128, 8, CMAX])
                      rwkeys = [f"rw{a}:{fc}" for a in range(8) for fc in range(8)]
                      with scope():
                        sel = sb("sel", [128, 64, 128])
                        ld(sel[:].rearrange("p a b -> p (a b)"), c_sel3, [], ["sel"])
                        fmb = [sb(f"fmb{i}", [128, 8, 2, 64]) for i in range(2)]
                        vblk = [sb(f"vblk{i}", [128, 8, 64]) for i in range(2)]
                        tokm = [sb(f"tokm_{a}", [128, 512]) for a in range(5)]
                        Sd = [sb(f"Sd{i}", [128, 512]) for i in range(2)]
                        w1 = sb("w1", [128, 512]); w2_ = sb("w2_", [128, 512]); sa = sb("sa", [128, 8]); bks = sb("bks", [128, 512])
                        for i in range(2):
                            mset(fmb[i][:], 0.0, [f"fmb{i}"])

                        def sdram(base):
                            return [base.rearrange("(j h) v k -> h v j k", h=2)[hh] for hh in range(2)]
                        blocks = [("p", b * 64, 64) for b in range(T // 64)]
                        if has_s:
                            blocks.append(("s", T, NS))
                        Sp, Spk = Sd[0], "Sd0"
                        if seg == 0:
                            mset(Sp[:], 0.0, [Spk])
                        else:
                            for hh, v_ in enumerate(sdram(pd_s[j])):
                                ld(Sp[hh * 64:(hh + 1) * 64, :].rearrange("p (j k) -> p j k", k=64), v_, [f"pd_s{j}"], [Spk])
                        for bi, (kind, c0, nt) in enumerate(blocks):
                            vb, vbk = vblk[bi % 2], f"vblk{bi % 2}"
                            for a in range(5):
                                fb, fbk = fmb[a % 2], f"fmb{a % 2}"
                                for dd in range(2):
                                    ld(fb[:, :, dd, :nt], rwscr[a, :, :, c0:c0 + nt].rearrange("c p t -> p c t"), rwkeys, [fbk])
                                for g4 in range(2):
                                    p_, k_ = ps()
                                    for i in range(4):
                                        fc = g4 * 4 + i
                                        tp(p_[:, i * 128:(i + 1) * 128], fb[:, fc, :, :].rearrange("p a b -> p (a b)"), 128, [fbk], [k_])
                                    pv4 = p_[:, :].rearrange("p (i h k) -> p i h k", h=2, k=64)
                                    cp(tokm[a][0:64, g4 * 256:(g4 + 1) * 256].rearrange("p (i k) -> p i k", k=64), pv4[0:64, :, 0, :], [k_], [f"tokm_{a}"])
                                    cp(tokm[a][64:128, g4 * 256:(g4 + 1) * 256].rearrange("p (i k) -> p i k", k=64), pv4[64:128, :, 1, :], [k_], [f"tokm_{a}"], eng="dve")
                            ld(vb[:, :, :nt], rwscr[5, :, :, c0:c0 + nt].rearrange("c p t -> p c t"), rwkeys, [vbk])
                            for ti in range(nt):
                                if kind == "p":
                                    St, Sk = Sp, Spk
                                else:
                                    St, Sk = Sd[1], "Sd1"
                                    for hh, v_ in enumerate(sdram(i_d_s[j, ti])):
                                        ld(St[hh * 64:(hh + 1) * 64, :].rearrange("p (j k) -> p j k", k=64), v_, [], [Sk])
                                col = c0 + ti
                                bcs = []
                                for a in range(5):
                                    p_, k_ = ps()
                                    mm(p_[:, :], sel[:, ti, :], tokm[a][:, :], True, True, [f"tokm_{a}", "sel"], [k_])
                                    bcs.append((p_, k_))
                                (bkk, kkk), (bd, bdk), (bka, bkak), (bk, bkk_), (br_, brk) = bcs
                                w13 = w1[:, :].rearrange("p (j k) -> p j k", k=64)
                                w23 = w2_[:, :].rearrange("p (j k) -> p j k", k=64)
                                tt(w1[:, :], St[:, :], bkk[:, :], ALU.mult, [Sk, kkk], ["w1"])
                                S.op("dve", lambda e: e.tensor_reduce(out=sa[:, :], in_=w13, axis=AX.X, op=ALU.add, negate=True), reads=["w1"], writes=["sa"])
                                tt(St[:, :], St[:, :], bd[:, :], ALU.mult, [Sk, bdk], [Sk])
                                tt(w13, bka[:, :].rearrange("p (j k) -> p j k", k=64), sa[:, :].unsqueeze(2).to_broadcast([128, 8, 64]), ALU.mult, [bkak, "sa"], ["w1"])
                                tt(St[:, :], St[:, :], w1[:, :], ALU.add, [Sk, "w1"], [Sk])
                                cp(bks[:, :], bk[:, :], [bkk_], ["bks"])
                                tt(w23, bks[:, :].rearrange("p (j k) -> p j k", k=64), vb[:, :, ti:ti + 1].to_broadcast([128, 8, 64]), ALU.mult, ["bks", vbk], ["w2_"], eng="pool")
                                tt(St[:, :], St[:, :], w2_[:, :], ALU.add, [Sk, "w2_"], [Sk])
                                tt(w1[:, :], St[:, :], br_[:, :], ALU.mult, [Sk, brk], ["w1"])
                                S.op("dve", lambda e, col=col: e.tensor_reduce(out=yv[:, :, col], in_=w13, axis=AX.X, op=ALU.add), reads=["w1"], writes=["yv"])
                                if kind == "s":
                                    for hh, v_ in enumerate(sdram(sd_s[j, ti])):
                                        ld(v_, St[hh * 64:(hh + 1) * 64, :].rearrange("p (j k) -> p j k", k=64), [Sk], [f"sd_s{j}:{ti}"])
                        for hh, v_ in enumerate(sdram(pd_s[j])):
                            ld(v_, Sp[hh * 64:(hh + 1) * 64, :].rearrange("p (j k) -> p j k", k=64), [Spk], [f"pd_s{j}"])
                      with scope():
                        t1 = sb("t1", [128, CMAX]); t2 = sb("t2", [128, CMAX]); t3 = sb("t3", [128, CMAX]); t4 = sb("t4", [128, CMAX])
                        for fc in range(8):
                            ld(t3[:, :C], rwscr[6, fc, :, :C], rwkeys, ["t3"]); ld(t4[:, :C], rwscr[7, fc, :, :C], rwkeys, ["t4"])
                            for jj, (cs, cn, is_s) in enumerate(ctl):
                                p_, k_ = ps()
                                mm(p_[:, :cn], blk[:], yv[:, fc, cs:cs + cn], True, True, ["blk", "yv"], [k_])
                                ts(t1[:, cs:cs + cn], p_[:, :cn], -1.0 / 64.0, None, ALU.mult, None, [k_], ["t1"])
                            tt(t1[:, :C], t1[:, :C], yv[:, fc, :C], ALU.add, ["t1", "yv"], ["t1"])
                            tt(t2[:, :C], t1[:, :C], t1[:, :C], ALU.mult, ["t1"], ["t2"])
                            for jj, (cs, cn, is_s) in enumerate(ctl):
                                p_, k_ = ps()
                                mm(p_[:, :cn], blk[:], t2[:, cs:cs + cn], True, True, ["blk", "t2"], [k_])
                                act(t2[:, cs:cs + cn], p_[:, :cn], AF.Ln, [k_, "gnepsc", "t2"], ["t2"], bias=gnepsc[:, 0:1], scale=1.0 / 64.0)
                            act(t2[:, :C], t2[:, :C], AF.Exp, ["t2"], ["t2"], scale=-0.5)
                            tt(t1[:, :C], t1[:, :C], t2[:, :C], ALU.mult, ["t1", "t2"], ["t1"])
                            ts(t1[:, :C], t1[:, :C], pr(5, fc), pr(6, fc), ALU.mult, ALU.add, ["t1", "prmT"], ["t1"])
                            tt(t1[:, :C], t1[:, :C], t3[:, :C], ALU.add, ["t1", "t3"], ["t1"])
                            tt(yT[:, 8 + fc, :C], t1[:, :C], t4[:, :C], ALU.mult, ["t1", "t4"], [yk(8 + fc)])

                def final_norm():
                  with scope():
                      fnT = sb("fnT", [128, NCH])
                      frow = sb("frow", [2, 1024])
                      ld(frow[:], prm[NR - 2:NR, :], [], ["frow"])
                      for c in range(8):
                          p_, k_ = ps()
                          tp(p_[:, :2], frow[:2, c * 128:(c + 1) * 128], 2, ["frow"], [k_])
                          cp(fnT[:, c:c + 1], p_[:, 0:1], [k_], ["fnT"]); cp(fnT[:, 8 + c:9 + c], p_[:, 1:2], [k_], ["fnT"])
                      sq = [sb(f"sq{i}", [128, 512], BF16) for i in range(2)]
                      rstd = sb("rstd", [128, 512])
                      yf = sb("yf", [128, NCH, 128]); orow = [sb(f"orow{i}", [128, D]) for i in range(2)]
                      bidx = 0
                      for jx, (cs, cn, is_s) in enumerate(ctl):
                          p_, k_ = ps()
                          for c in range(NCH):
                              act(sq[c % 2][:, :cn], XT[0][:, c, cs:cs + cn], AF.Square, [xk(c, jx)], [f"sq{c % 2}"])
                              mm(p_[:, :cn], onesD[:], sq[c % 2][:, :cn], c == 0, c == NCH - 1, [f"sq{c % 2}", "onesD"], [k_])
                          act(rstd[:, :cn], p_[:, :cn], AF.Ln, [k_, "epsc"], ["rstd"], bias=epsc[:, 0:1], scale=1.0)
                          act(rstd[:, :cn], rstd[:, :cn], AF.Exp, ["rstd"], ["rstd"], scale=-0.5)
                          for b0 in range(0, cn, 128):
                              bn = min(128, cn - b0)
                              for c in range(NCH):
                                  stt(yf[:, c, :bn], XT[0][:, c, cs + b0:cs + b0 + bn], fnT[:, c:c + 1], rstd[:, b0:b0 + bn], ALU.mult, ALU.mult,
                                      [xk(c, jx), "fnT", "rstd"], ["yf"])
                              orw, ork = orow[bidx % 2], f"orow{bidx % 2}"
                              bidx += 1
                              for g4 in range(4):
                                  p2, k2 = ps()
                                  for i in range(4):
                                      c = g4 * 4 + i
                                      tp(p2[:bn, i * 128:(i + 1) * 128], yf[:, c, :bn], 128, ["yf"], [k2])
                                  cp(orw[:bn, g4 * 512:(g4 + 1) * 512], p2[:bn, :512], [k2], [ork])
                              if is_s:
                                  ld(y_s[0:bn, :], orw[:bn, :], [ork], ["y_s"])
                              else:
                                  r0 = tok0 + cs + b0
                                  ld(y_p[r0:r0 + bn, :], orw[:bn, :], [ork], [f"y_p{r0}"])


                def load_mod(l):
                    ld(modT[:].rearrange("p a b -> p (a b)"), modscr[l], [f"modscr{l}"], ["modT"])
                    for k_ in (1, 4, 5, 7):
                        ts(modT[:, k_ * 16:(k_ + 1) * 16, :], modT[:, k_ * 16:(k_ + 1) * 16, :], 1.0, None, ALU.add, None, ["modT"], ["modT"])
                    for k_ in (2, 8):
                        ts(modT[:, k_ * 16:(k_ + 1) * 16, :], modT[:, k_ * 16:(k_ + 1) * 16, :], 1.0, 0.5, ALU.add, ALU.mult, ["modT"], ["modT"])

                def spill():
                    ld(xscr[:, :], XT[0][:].rearrange("p a b -> p (a b)"), allx, ["xscr"])

                def reload():
                    ld(XT[0][:].rearrange("p a b -> p (a b)"), xscr[:, :], ["xscr"], allx)

                for l in range(DEPTH):
                    with scope():
                        XT[0] = sb("xT", [128, NCH, CMAX])
                        if l == 0:
                            load_x()
                        else:
                            reload()
                        load_mod(l)
                        adaln(0, 1)
                        ffn(l, 0, 2)
                        adaln(3, 4)
                        spill()
                    with scope():
                        yT = sb("yT", [128, NCH, CMAX], BF16)

                        def yk(m):
                            return f"yT:{m}"
                        if l % 2 == 0:
                            even_mixer(l, yT, yk)
                        else:
                            odd_mixer(l, yT, yk)
                        with scope():
                            XT[0] = sb("xT", [128, NCH, CMAX])
                            reload()
                            out_proj((w_out_even if l % 2 == 0 else w_out_odd)[l // 2], 5, yT, yk)
                            spill()
                    with scope():
                        XT[0] = sb("xT", [128, NCH, CMAX])
                        reload()
                        adaln(6, 7)
                        ffn(l, 1, 8)
                        if l == DEPTH - 1:
                            final_norm()
                        else:
                            spill()

        snap = {s: v for s, v in S.cnt.items() if v > 0}
        S.ops["sp"].append((S._waits("sp", snap), None, None, 0))
        block = es.enter_context(nc.Block())
        S.emit(block)
    return nc, S.nops


def make_inputs(cfg, inp, core):
    NS, NE, NO = cfg.NS, cfg.NE, cfg.NO
    f = lambda a: np.ascontiguousarray(a, dtype=np.float32)
    b = core % inp["x_prompt"].shape[0]
    s0, s1 = core * NS, (core + 1) * NS
    m = {}
    m["xp"] = f(inp["x_prompt"][b])
    m["xs"] = f(inp["x_sample"][s0:s1, 0])
    m["cc"] = f(np.concatenate([inp["c_prompt"][b:b + 1], inp["c_sample"][s0:s1]], axis=0))
    m["i_a_conv"] = f(inp["state_a_conv"][:, s0:s1]); m["i_a_h"] = f(inp["state_a_h"][:, s0:s1])
    m["i_b_s"] = f(inp["state_b_s"][:, s0:s1])
    m["i_c_c"] = f(inp["state_c_c"][:, s0:s1]); m["i_c_n"] = f(inp["state_c_n"][:, s0:s1]); m["i_c_m"] = f(inp["state_c_m"][:, s0:s1])
    m["i_d_shift"] = f(inp["state_d_shift"][:, s0:s1]); m["i_d_s"] = f(inp["state_d_s"][:, s0:s1])
    return m


def shared_inputs(cfg, inp):
    NE, NO, T = cfg.NE, cfg.NO, cfg.T
    f = lambda a: np.ascontiguousarray(a, dtype=np.float32)
    m = {}
    for k in ("w_mod", "b_mod", "ffn_w_gate", "ffn_w_up", "ffn_w_down", "w_in_even", "w_out_even", "w_in_odd", "w_out_odd",
              "a_gate_r_w", "a_gate_i_w", "b_norm_w", "c_norm_w", "d_mu", "d_w2", "d_a2", "d_g2"):
        m[k] = f(inp[k])
    rows = []
    for j in range(NE):
        rows += [inp["a_conv_w"][j, 0], inp["a_conv_w"][j, 1], inp["a_conv_w"][j, 2], inp["a_conv_w"][j, 3], inp["a_conv_b"][j],
                 inp["a_gate_r_b"][j], inp["a_gate_i_b"][j], inp["a_lambda"][j], inp["b_lb_gamma"][j]]
    for j in range(NO):
        rows += [inp["d_w0"][j], inp["d_a0"][j], inp["d_k_k"][j], inp["d_k_a"][j], np.reshape(inp["d_r_k"][j], (1024,)),
                 inp["d_ln_w"][j], inp["d_ln_b"][j]]
    fn = np.reshape(inp["final_norm_w"], (2, 1024))
    rows += [fn[0], fn[1]]
    m["prm"] = f(np.stack(rows, axis=0))
    m["c_gb"] = f(np.concatenate([np.concatenate([inp["c_igate_b"][j], inp["c_fgate_b"][j]]) for j in range(NO)])[None, :])
    m["c_ident"] = np.eye(128, dtype=np.float32)
    m["c_mmul"] = np.triu(np.ones((64, 64), dtype=np.float32))
    m["c_madd"] = np.where(np.tril(np.ones((64, 64), dtype=bool)), 0.0, -1e30).astype(np.float32)
    cm = np.ones((128, T), dtype=np.float32); cm[:, ::CL] = 0.0
    m["c_cmask"] = cm
    blk = np.zeros((128, 128), dtype=np.float32); blk[:64, :64] = 1.0; blk[64:, 64:] = 1.0
    m["c_blk"] = blk
    m["c_sel2"] = np.concatenate([np.eye(64, dtype=np.float32)] * 2, axis=0)
    s3 = np.zeros((128, 64, 128), dtype=np.float32)
    for r in range(128):
        s3[r, r % 64, (r // 64) * 64:(r // 64) * 64 + 64] = 1.0
    m["c_sel3"] = s3.reshape(128, 64 * 128)
    return m


def run(cfg, inp, ncores=8):
    nc, nops = build(cfg)
    sh = shared_inputs(cfg, inp)
    in_maps = []
    for c in range(ncores):
        m = dict(sh)
        m.update(make_inputs(cfg, inp, c))
        in_maps.append(m)
    res = run_bass_kernel_spmd(nc, in_maps, core_ids=list(range(ncores)))
    R = res.results
    B = inp["x_prompt"].shape[0]
    cat_p = lambda k, ax=0: np.stack([R[b][k] for b in range(B)], axis=ax)
    cat_s = lambda k, ax: np.concatenate([R[c][k] for c in range(ncores)], axis=ax)
    outs = (
        cat_p("y_p"), cat_s("y_s", 0)[:, None, :],
        cat_p("pa_conv", 1), cat_p("pa_h", 1), cat_p("pb_s", 1), cat_p("pc_c", 1), cat_p("pc_n", 1), cat_p("pc_m", 1),
        cat_p("pd_shift", 1), cat_p("pd_s", 1),
        cat_s("sa_conv", 1), cat_s("sa_h", 1), cat_s("sb_s", 1), cat_s("sc_c", 1), cat_s("sc_n", 1), cat_s("sc_m", 1),
        cat_s("sd_shift", 1), cat_s("sd_s", 1),
    )
    return tuple(np.ascontiguousarray(o, dtype=np.float32) for o in outs)


def kernel(**inputs):
    inp = {k: np.asarray(v) for k, v in inputs.items()}
    cfg = Cfg()
    return run(cfg, inp)
```
